# Optimizing a Trainium2 kernel written in Bass

```python
import jax, jax.numpy as jnp
from jax import lax
import numpy as np

D_MODEL = 1024
BATCH = 8
SEQ = 2048
DEPTH = 2

N_A_LAYERS = DEPTH // 2
N_B_LAYERS = DEPTH - N_A_LAYERS
GLA_HEADS = 4
GLA_KEY_DIM = D_MODEL // 2
GLA_VAL_DIM = D_MODEL
GLA_HEAD_K = GLA_KEY_DIM // GLA_HEADS
GLA_HEAD_V = GLA_VAL_DIM // GLA_HEADS
GLA_GATE_RANK = 16
GLA_GATE_NORMALIZER = 16.0
GLA_CHUNK = 64
GLA_SPLITS = [GLA_KEY_DIM, 2 * GLA_KEY_DIM, 2 * GLA_KEY_DIM + GLA_VAL_DIM, 2 * GLA_KEY_DIM + 2 * GLA_VAL_DIM]
GLA_IN_DIM = 2 * GLA_KEY_DIM + 2 * GLA_VAL_DIM + GLA_GATE_RANK
SB_HEADS = 16
SB_HEAD_DIM = D_MODEL // SB_HEADS
SB_WIDTH = SB_HEADS * SB_HEAD_DIM
SB_QBLOCK = 128
RMS_EPS = 1e-6

kernel_name = 'yoco_gla_stickbreaking_adaln'


def rms_norm(x, gain):
    x32 = x.astype(jnp.float32)
    y = x32 * lax.rsqrt(jnp.mean(x32 * x32, axis=-1, keepdims=True) + RMS_EPS)
    return (y * gain.astype(jnp.float32)).astype(x.dtype)


def ada_modulate(c, w, b, n):
    cond = jax.nn.silu(c) @ w + b
    return jnp.split(cond[:, None, :], n, axis=-1)


def gla_mixer(h, w_in, w_gk2, b_gk, o_gain, w_out):
    bsz, seq, _ = h.shape
    n_chunks = seq // GLA_CHUNK
    f32 = jnp.float32
    proj = h @ w_in
    q, k, v, g, gk_lr = jnp.split(proj, GLA_SPLITS, axis=-1)
    gk = jax.nn.log_sigmoid((gk_lr @ w_gk2 + b_gk).astype(f32)) / GLA_GATE_NORMALIZER

    def chunked(t, dh):
        return t.astype(f32).reshape(bsz, n_chunks, GLA_CHUNK, GLA_HEADS, dh).transpose(0, 3, 1, 2, 4)

    q = chunked(q, GLA_HEAD_K) * (GLA_HEAD_K ** -0.5)
    k = chunked(k, GLA_HEAD_K)
    v = chunked(v, GLA_HEAD_V)
    b = jnp.cumsum(chunked(gk, GLA_HEAD_K), axis=3)
    b_last = b[:, :, :, -1:, :]
    q_e = q * jnp.exp(b)
    k_e = k * jnp.exp(-b)
    k_to_end = k * jnp.exp(b_last - b)
    causal = jnp.tril(jnp.ones((GLA_CHUNK, GLA_CHUNK), dtype=bool))
    scores = jnp.where(causal, jnp.einsum('bhnid,bhnjd->bhnij', q_e, k_e), 0.0)
    o_intra = jnp.einsum('bhnij,bhnjv->bhniv', scores, v)
    state_inc = jnp.einsum('bhncd,bhncv->bhndv', k_to_end, v)
    decay = jnp.exp(b_last[:, :, :, 0, :])

    def step(state, inp):
        dec, inc = inp
        return dec[..., None] * state + inc, state

    s0 = jnp.zeros((bsz, GLA_HEADS, GLA_HEAD_K, GLA_HEAD_V), f32)
    _, s_prev = lax.scan(step, s0, (jnp.moveaxis(decay, 2, 0), jnp.moveaxis(state_inc, 2, 0)))
    s_prev = jnp.moveaxis(s_prev, 0, 2)
    o = o_intra + jnp.einsum('bhncd,bhndv->bhncv', q_e, s_prev)
    o = o.transpose(0, 2, 3, 1, 4).reshape(bsz, seq, GLA_HEADS, GLA_HEAD_V)
    o = rms_norm(o, o_gain).reshape(bsz, seq, GLA_VAL_DIM) * jax.nn.silu(g.astype(f32))
    return o.astype(h.dtype) @ w_out


def stick_breaking_mixer(h, k, v, w_in, w_out):
    bsz, seq, _ = h.shape
    q, g = jnp.split(h @ w_in, 2, axis=-1)
    q = q.reshape(bsz, seq, SB_HEADS, SB_HEAD_DIM).transpose(0, 2, 1, 3)
    scale = SB_HEAD_DIM ** -0.5
    blocks = []
    for start in range(0, seq, SB_QBLOCK):
        end = start + SB_QBLOCK
        z = jnp.einsum('bhtd,bhsd->bhts', q[:, :, start:end], k[:, :, :end]).astype(jnp.float32) * scale
        t_idx = start + jnp.arange(SB_QBLOCK)[:, None]
        s_idx = jnp.arange(end)[None, :]
        strictly_before = s_idx < t_idx
        log_beta = jax.nn.log_sigmoid(z)
        log_one_minus = jnp.where(strictly_before, log_beta - z, 0.0)
        log_survive = lax.cumsum(log_one_minus, axis=3, reverse=True) - log_one_minus
        weights = jnp.where(strictly_before, jnp.exp(log_beta + log_survive), 0.0)
        blocks.append(jnp.einsum('bhts,bhsd->bhtd', weights.astype(v.dtype), v[:, :, :end]))
    o = jnp.concatenate(blocks, axis=2).transpose(0, 2, 1, 3).reshape(bsz, seq, SB_WIDTH)
    o = o * jax.nn.silu(g)
    return o @ w_out


def setup_inputs(seed: int = 0) -> dict:
    key = jax.random.key(seed)
    ks = jax.random.split(key, 20)
    f32 = jnp.float32

    def dense(k, shape, fan_in, mult=1.0):
        return jax.random.normal(k, shape, f32) * (mult * fan_in ** -0.5)

    def gain(k, shape):
        return 1.0 + 0.02 * jax.random.normal(k, shape, f32)

    def bias(k, shape):
        return 0.02 * jax.random.normal(k, shape, f32)

    D = D_MODEL
    return {
        'x': jax.random.normal(ks[0], (BATCH, SEQ, D), f32),
        'c': jax.random.normal(ks[1], (BATCH, D), f32),
        'norm_gain': gain(ks[2], (DEPTH, D)),
        'w_ada': dense(ks[3], (DEPTH, D, 3 * D), D, 0.2),
        'b_ada': bias(ks[4], (DEPTH, 3 * D)),
        'gla_w_in': dense(ks[5], (N_A_LAYERS, D, GLA_IN_DIM), D),
        'gla_w_gk2': dense(ks[6], (N_A_LAYERS, GLA_GATE_RANK, GLA_KEY_DIM), GLA_GATE_RANK),
        'gla_b_gk': bias(ks[7], (N_A_LAYERS, GLA_KEY_DIM)),
        'gla_o_gain': gain(ks[8], (N_A_LAYERS, GLA_HEAD_V)),
        'gla_w_out': dense(ks[9], (N_A_LAYERS, GLA_VAL_DIM, D), GLA_VAL_DIM),
        'kv_gain': gain(ks[10], (D,)),
        'kv_w_ada': dense(ks[11], (D, 2 * D), D, 0.2),
        'kv_b_ada': bias(ks[12], (2 * D,)),
        'w_kv': dense(ks[13], (D, 2 * SB_WIDTH), D),
        'sb_w_in': dense(ks[14], (N_B_LAYERS, D, 2 * SB_WIDTH), D),
        'sb_w_out': dense(ks[15], (N_B_LAYERS, SB_WIDTH, D), SB_WIDTH),
        'final_gain': gain(ks[16], (D,)),
    }


def reference(x, c, norm_gain, w_ada, b_ada, gla_w_in, gla_w_gk2, gla_b_gk, gla_o_gain, gla_w_out,
              kv_gain, kv_w_ada, kv_b_ada, w_kv, sb_w_in, sb_w_out, final_gain):
    bsz, seq, _ = x.shape
    for i in range(N_A_LAYERS):
        shift, scale, gate = ada_modulate(c, w_ada[i], b_ada[i], 3)
        h = rms_norm(x, norm_gain[i]) * (1.0 + scale) + shift
        x = x + gate * gla_mixer(h, gla_w_in[i], gla_w_gk2[i], gla_b_gk[i], gla_o_gain[i], gla_w_out[i])
    kv_shift, kv_scale = ada_modulate(c, kv_w_ada, kv_b_ada, 2)
    hk = rms_norm(x, kv_gain) * (1.0 + kv_scale) + kv_shift
    k_sh, v_sh = jnp.split(hk @ w_kv, 2, axis=-1)
    k_sh = k_sh.reshape(bsz, seq, SB_HEADS, SB_HEAD_DIM).transpose(0, 2, 1, 3)
    v_sh = v_sh.reshape(bsz, seq, SB_HEADS, SB_HEAD_DIM).transpose(0, 2, 1, 3)
    for j in range(N_B_LAYERS):
        l = N_A_LAYERS + j
        shift, scale, gate = ada_modulate(c, w_ada[l], b_ada[l], 3)
        h = rms_norm(x, norm_gain[l]) * (1.0 + scale) + shift
        x = x + gate * stick_breaking_mixer(h, k_sh, v_sh, sb_w_in[j], sb_w_out[j])
    return rms_norm(x, final_gain)
```

```python
import numpy as np
from contextlib import ExitStack
import concourse.bass as bass
import concourse.mybir as mybir
from concourse.bass_utils import run_bass_kernel_spmd

F32 = mybir.dt.float32
BF16 = mybir.dt.bfloat16
AF = mybir.ActivationFunctionType
ALU = mybir.AluOpType

T = 2048
D = 1024
NB = 16
KC = 8
EPS = 1e-6
GIN = 3088


ENG = ("pe", "act", "dve", "pool", "sp")


class Res:
    __slots__ = ("name", "w", "r")

    def __init__(self, name):
        self.name = name
        self.w = None
        self.r = []


class Op:
    __slots__ = ("eng", "emit", "deps", "sig", "count", "dma", "sem", "semval", "final", "phase")

    def __init__(self, eng, emit, dma):
        self.eng = eng
        self.emit = emit
        self.dma = dma
        self.deps = []
        self.sig = False
        self.count = 0
        self.sem = None
        self.final = False
        self.phase = 0


class Rec:
    def __init__(self, nc, es):
        self.nc = nc
        self.es = es
        self.q = {e: [] for e in ENG}
        self.n = 0
        self.phase = 0
        self.sems = {e: es.enter_context(nc.semaphore("s_" + e)) for e in ENG}
        self.bsem = es.enter_context(nc.semaphore("s_bar"))
        self.cnt = {e: 0 for e in ENG}
        self.finals = []
        self.dsems = []
        self.res = []

    def R(self, name):
        r = Res(name)
        self.res.append(r)
        return r

    def add(self, eng, emit, reads=(), writes=(), dma=False, final=False):
        op = Op(eng, emit, dma)
        op.final = final
        op.phase = self.phase
        deps = {}
        for r in reads:
            if r.w is not None:
                deps[id(r.w)] = r.w
        for w in writes:
            if w.w is not None:
                deps[id(w.w)] = w.w
            for x in w.r:
                deps[id(x)] = x
        for d in deps.values():
            if d is op:
                continue
            if (not dma) and (not d.dma) and d.eng == "pe" and eng == "pe":
                continue
            assert d.dma or d.phase == self.phase, "compute dep crosses phase"
            op.deps.append(d)
            d.sig = True
        for r in reads:
            if not dma:
                r.r = [x for x in r.r if x.dma or x.eng != eng]
            r.r.append(op)
        for w in writes:
            w.w = op
            w.r = []
        self.q[eng].append(op)
        self.n += 1
        return op

    def emit_phase(self, last=False):
        nc = self.nc
        di = 0
        for e in ENG:
            for op in self.q[e]:
                if op.dma:
                    if e == "pool":
                        self.nsw = getattr(self, "nsw", 0) + 1
                        op.sem = self.es.enter_context(nc.semaphore("dsw%d" % self.nsw))
                        op.semval = 16
                    else:
                        if di == len(self.dsems):
                            self.dsems.append([self.es.enter_context(nc.semaphore("d%d" % di)), 0])
                        ent = self.dsems[di]
                        di += 1
                        ent[1] += 1
                        op.sem = ent[0]
                        op.semval = 16 * ent[1]
                    if op.final:
                        self.finals.append(op)
                elif op.sig:
                    self.cnt[e] += 1
                    op.count = self.cnt[e]
        self.phase += 1
        k = self.phase
        phase_dmas = [op for e in ENG for op in self.q[e] if op.dma]
        with nc.Block() as block:
            def run(e, eng):
                waited = {}
                for op in self.q[e]:
                    for d in op.deps:
                        if d.dma:
                            key = id(d)
                            if key not in waited:
                                eng.wait_ge(d.sem, d.semval)
                                waited[key] = 1
                        else:
                            if waited.get(d.eng, 0) < d.count:
                                eng.wait_ge(self.sems[d.eng], d.count)
                                waited[d.eng] = d.count
                    inst = op.emit(eng)
                    if op.dma:
                        inst.then_inc(op.sem, 16)
                    elif op.sig:
                        inst.then_inc(self.sems[e], 1)
                if e == "sp":
                    for d in phase_dmas:
                        eng.wait_ge(d.sem, d.semval)
                eng.drain().then_inc(self.bsem, 1)
                eng.wait_ge(self.bsem, len(ENG) * k)

            @block.tensor
            def _(eng):
                run("pe", eng)

            @block.scalar
            def _(eng):
                run("act", eng)

            @block.vector
            def _(eng):
                run("dve", eng)

            @block.gpsimd
            def _(eng):
                run("pool", eng)

            @block.sync
            def _(eng):
                run("sp", eng)
        self.q = {e: [] for e in ENG}
        for r in self.res:
            r.w = None
            r.r = []

class PsumPool:
    def __init__(self, banks, res):
        self.free_list = list(zip(banks, res))

    def get(self):
        assert self.free_list, "out of PSUM banks"
        return self.free_list.pop(0)

    def free(self, bk, res):
        self.free_list.append((bk, res))


def ada_matvec(R, wdram, ncols, sc, sc_res, pp, stage, stage_res, col_out, col_out_res, rowtmp, rowtmp_res, one11, one_res,
               q="act", hook=None, maxw=2048):
    colbank, colbank_res = pp.get()
    i = 0
    for c0 in range(0, ncols, maxw):
        width = min(maxw, ncols - c0)
        nb = width // 512
        rbs = [pp.get() for _ in range(nb)]
        for kc in range(KC):
            st, st_res = stage[i % 2], stage_res[i % 2]
            i += 1
            R.add(q, lambda e, st=st, kc=kc, c0=c0, width=width: e.dma_start(out=st[:, 0:width],
                                                                              in_=wdram[128 * kc:128 * kc + 128, c0:c0 + width]),
                  writes=st_res, dma=True)
            if hook is not None:
                hook(kc)
            for bi, (rb, rb_r) in enumerate(rbs):
                R.add("pe", lambda e, st=st, kc=kc, rb=rb, bi=bi: e.matmul(rb[0:1, 0:512], lhsT=sc[:, kc:kc + 1],
                                                                            rhs=st[:, 512 * bi:512 * bi + 512],
                                                                            start=(kc == 0), stop=(kc == KC - 1)),
                      reads=st_res + [sc_res], writes=[rb_r])
        for bi, (rb, rb_r) in enumerate(rbs):
            for m4 in range(4):
                g = (c0 + 512 * bi) // 128 + m4
                m = g % 2
                R.add("act", lambda e, m=m, rb=rb, m4=m4: e.activation(out=rowtmp[m], in_=rb[0:1, 128 * m4:128 * m4 + 128], func=AF.Copy),
                      reads=[rb_r], writes=[rowtmp_res[m]])
                R.add("pe", lambda e, m=m, g=g: e.matmul(colbank[:, g:g + 1], lhsT=rowtmp[m], rhs=one11, start=True, stop=True),
                      reads=[rowtmp_res[m], one_res], writes=[colbank_res])
            pp.free(rb, rb_r)
    ng = ncols // 128
    R.add("act", lambda e: e.activation(out=col_out[:, 0:ng], in_=colbank[:, 0:ng], func=AF.Copy),
          reads=[colbank_res], writes=[col_out_res])
    pp.free(colbank, colbank_res)


def bcast_rows(R, pp, src, src_res, ncols8, tmpg, tmpg_res, ones128, ident, cst_res, out, out_res):
    for half in range(ncols8 // 4):
        bk, bk_r = pp.get()
        for q4 in range(4):
            kc = 4 * half + q4
            tg, tg_r = tmpg[kc % 2], tmpg_res[kc % 2]
            R.add("dve", lambda e, tg=tg, kc=kc: e.tensor_scalar_mul(out=tg, in0=ones128, scalar1=src[:, kc:kc + 1]),
                  reads=[src_res, cst_res], writes=[tg_r])
            R.add("pe", lambda e, bk=bk, q4=q4, tg=tg: e.matmul(bk[:, 128 * q4:128 * q4 + 128], lhsT=tg, rhs=ident,
                                                                 start=True, stop=True),
                  reads=[tg_r, cst_res], writes=[bk_r])
        R.add("act", lambda e, bk=bk, half=half: e.activation(out=out[:, 512 * half:512 * half + 512], in_=bk[:, :], func=AF.Copy),
              reads=[bk_r], writes=[out_res])
        pp.free(bk, bk_r)


def silu_small(R, c_ap, c_res, tmp, tmp_res, out, out_res, n):
    R.add("act", lambda e: e.activation(out=tmp[:, 0:n], in_=c_ap, func=AF.Exp, scale=-1.0),
          reads=[c_res], writes=[tmp_res])
    R.add("act", lambda e: e.activation(out=tmp[:, n:2 * n], in_=tmp[:, 0:n], func=AF.Ln, bias=1.0),
          reads=[tmp_res], writes=[tmp_res])
    R.add("act", lambda e: e.activation(out=tmp[:, 0:n], in_=tmp[:, n:2 * n], func=AF.Exp, scale=-1.0),
          reads=[tmp_res], writes=[tmp_res])
    R.add("dve", lambda e: e.tensor_tensor(out=out, in0=c_ap, in1=tmp[:, 0:n], op=ALU.mult),
          reads=[c_res, tmp_res], writes=[out_res])


def build_fused(nc):
    es = ExitStack()
    R = Rec(nc, es)

    def dram(name, shape, dt=F32, kind="ExternalInput"):
        return nc.dram_tensor(name, list(shape), dt, kind=kind).ap()

    x_d = dram("x", [T, D])
    vec_d = dram("vec", [128, 100])
    ogain_d = dram("ogain_rep", [128, 256])
    cst_d = dram("consts", [128, 896])
    cstb_d = dram("constsb", [128, 384], BF16)
    fg_d = dram("fg_rep", [128, D])
    wada0_d = dram("w_ada0", [D, 3 * D])
    wada1_d = dram("w_ada1", [D, 3 * D])
    wadak_d = dram("kv_w_ada", [D, 2 * D])
    win_d = dram("gla_w_in", [D, GIN])
    wgk2_d = dram("w_gk2", [16, 512])
    wout_d = dram("gla_w_out", [D, D])
    wkv_d = dram("w_kv", [D, 2 * D])
    sbwin_d = dram("sb_w_in", [D, 2 * D])
    sbwout_d = dram("sb_w_out", [D, D])
    out_d = dram("out", [T, D], kind="ExternalOutput")
    xv = x_d.rearrange("(n p) d -> p n d", p=128)
    outv = out_d.rearrange("(n p) d -> p n d", p=128)

    def psb(name, shape, dt=F32):
        return es.enter_context(nc.sbuf_tensor("sb_" + name, list(shape), dt))

    x_sb = psb("x_sb", [128, NB, D])
    vec = psb("vec", [128, 100])
    cst = psb("cst", [128, 256])
    cstb = psb("cstb", [128, 384], BF16)
    sc = psb("sc", [128, 8])
    rstd1 = psb("rstd1", [128, NB])
    cond1 = psb("cond1", [128, 24])
    ident = cst[:, 0:128]
    ones128 = cst[:, 128:256]
    identb = cstb[:, 0:128]
    maskb = cstb[:, 128:256]
    negidentb = cstb[:, 256:384]

    banks = [es.enter_context(nc.psum_tensor("bank%d" % i, [128, 512], F32)) for i in range(8)]
    pp = PsumPool(banks, [R.R("bank%d" % i) for i in range(8)])
    r_x = [R.R("x%d" % n) for n in range(NB)]
    rp = {n: R.R(n) for n in ["vec", "cst", "cstb", "sc", "rstd1", "cond1"]}

    R.add("sp", lambda e: e.dma_start(out=vec[:], in_=vec_d), writes=[rp["vec"]], dma=True)
    R.add("sp", lambda e: e.dma_start(out=cst[:, 0:128], in_=cst_d[:, 0:128]), writes=[rp["cst"]], dma=True)
    R.add("sp", lambda e: e.dma_start(out=cst[:, 128:256], in_=cst_d[:, 640:768]), writes=[rp["cst"]], dma=True)
    R.add("sp", lambda e: e.dma_start(out=cstb[:], in_=cstb_d), writes=[rp["cstb"]], dma=True)

    esA = ExitStack()

    def sb(name, shape, dt=F32):
        return esA.enter_context(nc.sbuf_tensor("a_" + name, list(shape), dt))

    win_sb = sb("win_sb", [128, KC, GIN], BF16)
    wout_sb = sb("wout_sb", [128, KC, D], BF16)
    ogain = sb("ogain", [128, 256])
    wgk2 = sb("wgk2", [16, 512])
    small = sb("small", [128, 96])
    gate_rep = sb("gate_rep", [128, 1024])
    ssx = sb("ssx", [128, 2 * NB])
    rstd = sb("rstd", [128, NB])
    junk = sb("junk", [128, 1024], BF16)
    S = sb("S", [128, 1024])
    S_bf = sb("S_bf", [128, 1024], BF16)
    arena = sb("arena", [128, 4096])
    tmpg = [sb("tmpg%d" % i, [128, 128]) for i in range(2)]
    NBUF = 2
    G2 = [arena[:, 0:1024], sb("G2b", [128, 1024])]
    gA = arena[:, 1024:1536]
    gB = arena[:, 1536:2048]
    ytmp = arena[:, 2048:2560]
    xs = arena[:, 2560:3584]
    stage = [arena[:, 0:2048], arena[:, 2048:4096]]
    hT = [sb("hT%d" % i, [128, KC, 128], BF16) for i in range(NBUF)]
    v_sb = [sb("v_sb%d" % i, [128, 1024], BF16) for i in range(NBUF)]
    gklr = sb("gklr", [16, 128])
    ebuf = sb("ebuf", [128, 512])
    spb = sb("spb", [128, 512])
    csb = sb("csb", [128, 512])
    E1 = sb("E1", [128, 512])
    E2 = sb("E2", [128, 512])
    qe = sb("qe", [128, 512], BF16)
    ke = sb("ke", [128, 512], BF16)
    kteT = sb("kteT", [128, 512], BF16)
    kte = sb("kte", [128, 512], BF16)
    sTm = sb("sTm", [128, 512], BF16)
    hT32 = sb("hT32", [128, KC, 128])
    qe32 = sb("qe32", [128, 512])
    ke32 = sb("ke32", [128, 512])
    w32slot = [arena[:, 2048:3072], arena[:, 3072:4096]]
    oss = sb("oss", [128, 12])
    og = sb("og", [128, 1024], BF16)
    ogT = sb("ogT", [128, KC * 128], BF16)

    r = {n: R.R(n) for n in ["ogain", "wgk2", "small", "gate_rep", "ssx", "rstd", "S", "S_bf",
                             "gA", "gB", "ytmp", "xs", "arena_tail", "gklr", "e", "sp", "cs", "E1", "E2", "qe", "ke", "kteT", "kte",
                             "sTm", "oss", "og", "ogT", "tg0", "tg1", "hT32", "qe32", "ke32"]}
    r["cst"] = rp["cst"]
    r["identb"] = rp["cstb"]
    r["vec"] = rp["vec"]
    w32_res = [[r["ytmp"], r["xs"]], [r["xs"], r["arena_tail"]]]
    r_win = [R.R("win%d" % k) for k in range(KC)]
    r_wout = [R.R("wout%d" % k) for k in range(KC)]
    r_G2 = [R.R("G2_0"), R.R("G2_1")]
    r_hT = [R.R("hT0"), R.R("hT1")]
    r_v = [R.R("v0"), R.R("v1")]
    stage_res = [[r_G2[0], r["gA"], r["gB"]], [r["ytmp"], r["xs"], r["arena_tail"]]]

    maskT4t = sb("maskT4", [128, 512])
    maskT4 = maskT4t[:, :]
    R.add("sp", lambda e: e.dma_start(out=maskT4t[:], in_=cst_d[:, 128:640]), writes=[r["cst"]], dma=True)
    R.add("sp", lambda e: e.dma_start(out=wgk2[:], in_=wgk2_d), writes=[r["wgk2"]], dma=True)
    R.add("sp", lambda e: e.dma_start(out=ogain[:], in_=ogain_d), writes=[r["ogain"]], dma=True)
    winv = win_d.rearrange("(k p) f -> p k f", p=128)
    woutv = wout_d.rearrange("(k p) f -> p k f", p=128)
    for kc in range(KC):
        R.add("pool", lambda e, kc=kc: e.dma_start(out=win_sb[:, kc, :], in_=winv[:, kc, :]),
              writes=[r_win[kc]], dma=True)
    for n in range(0, NB, 2):
        R.add("sp", lambda e, n=n: e.dma_start(out=x_sb[:, n:n + 2, :], in_=xv[:, n:n + 2, :]), writes=[r_x[n], r_x[n + 1]], dma=True)
        if n == 0:
            for kc in range(KC):
                R.add("pool", lambda e, kc=kc: e.dma_start(out=wout_sb[:, kc, :], in_=woutv[:, kc, :]),
                      writes=[r_wout[kc]], dma=True)

    R.add("dve", lambda e: e.memset(ssx[:], 0.0), writes=[r["ssx"]])
    R.add("dve", lambda e: e.memset(S[:], 0.0), writes=[r["S"]])
    R.add("dve", lambda e: e.memset(S_bf[:], 0.0), writes=[r["S_bf"]])
    R.add("dve", lambda e: e.memset(oss[:], 0.0), writes=[r["oss"]])
    silu_small(R, vec[:, 0:8], r["vec"], small[:, 8:24], r["small"], sc[:, 0:8], rp["sc"], 8)
    def sq_hook(kc):
        for n in (2 * kc, 2 * kc + 1):
            R.add("act", lambda e, n=n: e.activation(out=junk[:], in_=x_sb[:, n, :], func=AF.Square, accum_out=ssx[:, n:n + 1]),
                  reads=[r_x[n]], writes=[r["ssx"]])

    ada_matvec(R, wada0_d[:, 0:2 * D], 2 * D, sc, rp["sc"], pp, stage, stage_res, small[:, 24:40], r["small"],
               [tmpg[0][0:1, :], tmpg[1][0:1, :]], [r["tg0"], r["tg1"]], ones128[0:1, 0:1], r["cst"], hook=sq_hook)
    R.add("act", lambda e: e.activation(out=ssx[:, NB:2 * NB], in_=ssx[:, 0:NB], func=AF.Ln, scale=1.0 / D, bias=EPS),
          reads=[r["ssx"]], writes=[r["ssx"]])
    R.add("act", lambda e: e.activation(out=rstd[:, 0:NB], in_=ssx[:, NB:2 * NB], func=AF.Exp, scale=-0.5),
          reads=[r["ssx"]], writes=[r["rstd"]])
    R.add("dve", lambda e: e.tensor_tensor(out=small[:, 48:56], in0=small[:, 24:32], in1=vec[:, 16:24], op=ALU.add),
          reads=[r["small"], r["vec"]], writes=[r["small"]])
    R.add("dve", lambda e: e.scalar_tensor_tensor(out=small[:, 56:64], in0=small[:, 32:40], scalar=1.0, in1=vec[:, 24:32],
                                                  op0=ALU.add, op1=ALU.add),
          reads=[r["small"], r["vec"]], writes=[r["small"]])
    R.add("dve", lambda e: e.tensor_tensor(out=small[:, 56:64], in0=small[:, 56:64], in1=vec[:, 8:16], op=ALU.mult),
          reads=[r["small"], r["vec"]], writes=[r["small"]])
    R.add("dve", lambda e: e.tensor_scalar_mul(out=small[:, 72:76], in0=vec[:, 40:44], scalar1=-1.0),
          reads=[r["vec"]], writes=[r["small"]])
    r["small2"] = R.R("small2")
    small2 = sb("small2", [128, 16])

    def gate_setup():
        ada_matvec(R, wada0_d[:, 2 * D:3 * D], D, sc, rp["sc"], pp, stage, stage_res, small2[:, 0:8], r["small2"],
                   [tmpg[0][0:1, :], tmpg[1][0:1, :]], [r["tg0"], r["tg1"]], ones128[0:1, 0:1], r["cst"])
        R.add("dve", lambda e: e.tensor_tensor(out=small2[:, 8:16], in0=small2[:, 0:8], in1=vec[:, 32:40], op=ALU.add),
              reads=[r["small2"], r["vec"]], writes=[r["small2"]])
        bcast_rows(R, pp, small2[:, 8:16], r["small2"], 8, [t[:] for t in tmpg], [r["tg0"], r["tg1"]], ones128, ident, r["cst"],
                   gate_rep, r["gate_rep"])

    shift0 = small[:, 48:56]
    A0 = small[:, 56:64]
    negbgk = small[:, 72:76]

    def stage1_xs(n):
        R.add("dve", lambda e: e.tensor_scalar_mul(out=xs, in0=x_sb[:, n, :], scalar1=rstd[:, n:n + 1]),
              reads=[r_x[n], r["rstd"]], writes=[r["xs"]])

    def stage1(n):
        b = n % NBUF
        for half in range(2):
            bk, bk_r = pp.get()
            for q in range(4):
                kc = 4 * half + q
                R.add("pe", lambda e, bk=bk, q=q, kc=kc: e.matmul(bk[:, 128 * q:128 * q + 128],
                                                                    lhsT=xs[:, 128 * kc:128 * kc + 128], rhs=ident,
                                                                    start=True, stop=True),
                      reads=[r["xs"], r["cst"]], writes=[bk_r])
            for q in range(4):
                kc = 4 * half + q
                R.add("act", lambda e, bk=bk, q=q, kc=kc: e.activation(out=hT[b][:, kc, :], in_=bk[:, 128 * q:128 * q + 128],
                                                                        func=AF.Identity, scale=A0[:, kc:kc + 1],
                                                                        bias=shift0[:, kc:kc + 1]),
                      reads=[bk_r, r["small"]], writes=[r_hT[b]])
                if n == 0:
                    R.add("act", lambda e, bk=bk, q=q, kc=kc: e.activation(out=hT32[:, kc, :], in_=bk[:, 128 * q:128 * q + 128],
                                                                            func=AF.Identity, scale=A0[:, kc:kc + 1],
                                                                            bias=shift0[:, kc:kc + 1]),
                          reads=[bk_r, r["small"]], writes=[r["hT32"]])
            pp.free(bk, bk_r)

    def proj_qk(n):
        b = n % NBUF
        st = {}
        for name, c0 in (("q", 0), ("k", 512)):
            bk, bk_r = pp.get()
            for hh in range(4):
                if n == 0:
                    si = (hh + (0 if name == "q" else 4)) % 2
                    ws = w32slot[si].rearrange("p (k f) -> p k f", k=KC)
                    R.add("sp", lambda e, ws=ws, hh=hh, c0=c0: e.dma_start(out=ws, in_=winv[:, :, c0 + 128 * hh:c0 + 128 * hh + 128]),
                          writes=w32_res[si], dma=True)
                    for kc in range(KC):
                        R.add("pe", lambda e, bk=bk, hh=hh, kc=kc, ws=ws: e.matmul(
                            bk[:, 128 * hh:128 * hh + 128], lhsT=ws[:, kc, :], rhs=hT32[:, kc, :],
                            start=(kc == 0), stop=(kc == KC - 1)),
                            reads=w32_res[si] + [r["hT32"]], writes=[bk_r])
                    continue
                for kc in range(KC):
                    R.add("pe", lambda e, bk=bk, hh=hh, kc=kc, c0=c0: e.matmul(
                        bk[:, 128 * hh:128 * hh + 128], lhsT=win_sb[:, kc, c0 + 128 * hh:c0 + 128 * hh + 128],
                        rhs=hT[b][:, kc, :], start=(kc == 0), stop=(kc == KC - 1)),
                        reads=[r_win[kc], r_hT[b]], writes=[bk_r])
            st[name] = (bk, bk_r)
        bk, bk_r = pp.get()
        for kc in range(KC):
            R.add("pe", lambda e, bk=bk, kc=kc: e.matmul(bk[0:16, 0:128], lhsT=win_sb[:, kc, 3072:3088], rhs=hT[b][:, kc, :],
                                                         start=(kc == 0), stop=(kc == KC - 1)),
                  reads=[r_win[kc], r_hT[b]], writes=[bk_r])
        R.add("act", lambda e, bk=bk: e.activation(out=gklr[:], in_=bk[0:16, 0:128], func=AF.Copy),
              reads=[bk_r], writes=[r["gklr"]])
        pp.free(bk, bk_r)
        return st

    def proj_vg(n, which):
        b = n % NBUF
        for name, c0 in ((which, 1024 if which == "v" else 2048),):
            lst = []
            for half in range(2):
                bk, bk_r = pp.get()
                for kc in range(KC):
                    R.add("pe", lambda e, bk=bk, kc=kc, c0=c0, half=half: e.matmul(
                        bk[:, :], lhsT=hT[b][:, kc, :], rhs=win_sb[:, kc, c0 + 512 * half:c0 + 512 * half + 512],
                        start=(kc == 0), stop=(kc == KC - 1)),
                        reads=[r_win[kc], r_hT[b]], writes=[bk_r])
                lst.append((bk, bk_r))
                if name == "v":
                    R.add("act", lambda e, bk=bk, half=half: e.activation(out=v_sb[b][:, 512 * half:512 * half + 512], in_=bk[:, :],
                                                                           func=AF.Copy),
                          reads=[bk_r], writes=[r_v[b]])
                else:
                    sl = slice(512 * half, 512 * half + 512)
                    R.add("act", lambda e, bk=bk, sl=sl: e.activation(out=gA, in_=bk[:, :], func=AF.Exp, scale=-1.0),
                          reads=[bk_r], writes=[r["gA"]])
                    R.add("act", lambda e, sl=sl: e.activation(out=gB, in_=gA, func=AF.Ln, bias=1.0),
                          reads=[r["gA"]], writes=[r["gB"]])
                    R.add("act", lambda e, sl=sl: e.activation(out=gA, in_=gB, func=AF.Exp, scale=-1.0),
                          reads=[r["gB"]], writes=[r["gA"]])
                    R.add("dve", lambda e, bk=bk, sl=sl: e.tensor_tensor(out=gB, in0=bk[:, :], in1=gA, op=ALU.mult),
                          reads=[bk_r, r["gA"]], writes=[r["gB"]])
                    for q2 in range(2):
                        c1 = 512 * half + 256 * q2
                        R.add("dve", lambda e, c1=c1, q2=q2: e.tensor_tensor(out=G2[b][:, c1:c1 + 256], in0=gB[:, 256 * q2:256 * q2 + 256],
                                                                              in1=ogain[:, 0:256], op=ALU.mult),
                              reads=[r["gB"], r["ogain"]], writes=[r_G2[b]])
                pp.free(bk, bk_r)


    def gates(n, st):
        b = n % NBUF
        qb, qb_r = st["q"]
        kb, kb_r = st["k"]
        bk, bk_r = pp.get()
        for hh in range(4):
            R.add("pe", lambda e, hh=hh: e.matmul(bk[:, 128 * hh:128 * hh + 128], lhsT=wgk2[0:16, 128 * hh:128 * hh + 128],
                                                  rhs=gklr[:], start=True, stop=True),
                  reads=[r["wgk2"], r["gklr"]], writes=[bk_r])
        for hh in range(4):
            R.add("act", lambda e, hh=hh: e.activation(out=ebuf[:, 128 * hh:128 * hh + 128], in_=bk[:, 128 * hh:128 * hh + 128],
                                                       func=AF.Exp, scale=-1.0, bias=negbgk[:, hh:hh + 1]),
                  reads=[bk_r, r["small"]], writes=[r["e"]])
        pp.free(bk, bk_r)
        R.add("act", lambda e: e.activation(out=spb[:], in_=ebuf[:], func=AF.Ln, bias=1.0),
              reads=[r["e"]], writes=[r["sp"]])
        for hh in range(4):
            sl = slice(128 * hh, 128 * hh + 128)
            R.add("dve", lambda e, sl=sl: e.tensor_tensor_scan(out=csb[:, sl], data0=spb[:, sl], data1=spb[:, sl], initial=0.0,
                                                              op0=ALU.add, op1=ALU.max),
                  reads=[r["sp"]], writes=[r["cs"]])
        R.add("act", lambda e: e.activation(out=E1[:], in_=csb[:], func=AF.Exp, scale=-1.0 / 16),
              reads=[r["cs"]], writes=[r["E1"]])
        R.add("act", lambda e: e.activation(out=E2[:], in_=csb[:], func=AF.Exp, scale=1.0 / 16),
              reads=[r["cs"]], writes=[r["E2"]])
        R.add("dve", lambda e: e.scalar_tensor_tensor(out=qe[:], in0=qb[:, :], scalar=128 ** -0.5, in1=E1[:],
                                                      op0=ALU.mult, op1=ALU.mult),
              reads=[qb_r, r["E1"]], writes=[r["qe"]])
        R.add("dve", lambda e: e.tensor_tensor(out=ke[:], in0=kb[:, :], in1=E2[:], op=ALU.mult),
              reads=[kb_r, r["E2"]], writes=[r["ke"]])
        if n == 0:
            R.add("dve", lambda e: e.scalar_tensor_tensor(out=qe32[:], in0=qb[:, :], scalar=128 ** -0.5, in1=E1[:],
                                                          op0=ALU.mult, op1=ALU.mult),
                  reads=[qb_r, r["E1"]], writes=[r["qe32"]])
            R.add("dve", lambda e: e.tensor_tensor(out=ke32[:], in0=kb[:, :], in1=E2[:], op=ALU.mult),
                  reads=[kb_r, r["E2"]], writes=[r["ke32"]])
        for hh in range(4):
            sl = slice(128 * hh, 128 * hh + 128)
            R.add("dve", lambda e, sl=sl, hh=hh: e.scalar_tensor_tensor(
                out=kteT[:, sl], in0=kb[:, sl], scalar=E1[:, 128 * hh + 127:128 * hh + 128], in1=E2[:, sl],
                op0=ALU.mult, op1=ALU.mult),
                reads=[kb_r, r["E1"], r["E2"]], writes=[r["kteT"]])
        pp.free(qb, qb_r)
        pp.free(kb, kb_r)

    def attn_a(n):
        b = n % NBUF
        sb_, sb_r = pp.get()
        for hh in range(4):
            sl = slice(128 * hh, 128 * hh + 128)
            if n == 0:
                R.add("pe", lambda e, sl=sl: e.matmul(sb_[:, sl], lhsT=ke32[:, sl], rhs=qe32[:, sl], start=True, stop=True),
                      reads=[r["ke32"], r["qe32"]], writes=[sb_r])
            else:
                R.add("pe", lambda e, sl=sl: e.matmul(sb_[:, sl], lhsT=ke[:, sl], rhs=qe[:, sl], start=True, stop=True),
                      reads=[r["ke"], r["qe"]], writes=[sb_r])
        R.add("dve", lambda e: e.tensor_tensor(out=sTm[:], in0=sb_[:, :], in1=maskT4, op=ALU.mult),
              reads=[sb_r, r["cst"]], writes=[r["sTm"]])
        pp.free(sb_, sb_r)
        tb, tb_r = pp.get()
        for hh in range(4):
            sl = slice(128 * hh, 128 * hh + 128)
            R.add("pe", lambda e, sl=sl: e.matmul(tb[:, sl], lhsT=kteT[:, sl], rhs=identb, start=True, stop=True),
                  reads=[r["kteT"], r["identb"]], writes=[tb_r])
        R.add("act", lambda e: e.activation(out=kte[:], in_=tb[:, :], func=AF.Copy),
              reads=[tb_r], writes=[r["kte"]])
        pp.free(tb, tb_r)

    def attn_b(n):
        b = n % NBUF
        obanks = []
        for half in range(2):
            ob, ob_r = pp.get()
            for q in range(2):
                hh = 2 * half + q
                sl = slice(128 * hh, 128 * hh + 128)
                vs = slice(256 * hh, 256 * hh + 256)
                R.add("pe", lambda e, ob=ob, q=q, sl=sl, vs=vs: e.matmul(ob[:, 256 * q:256 * q + 256], lhsT=sTm[:, sl],
                                                                          rhs=v_sb[b][:, vs], start=True, stop=False),
                      reads=[r["sTm"], r_v[b]], writes=[ob_r])
                R.add("pe", lambda e, ob=ob, q=q, sl=sl, vs=vs: e.matmul(ob[:, 256 * q:256 * q + 256], lhsT=qe[:, sl],
                                                                          rhs=S_bf[:, vs], start=False, stop=True),
                      reads=[r["qe"], r["S_bf"]], writes=[ob_r])
            obanks.append((ob, ob_r))
        ibanks = []
        for half in range(2):
            ib, ib_r = pp.get()
            for q in range(2):
                hh = 2 * half + q
                sl = slice(128 * hh, 128 * hh + 128)
                vs = slice(256 * hh, 256 * hh + 256)
                R.add("pe", lambda e, ib=ib, q=q, sl=sl, vs=vs: e.matmul(ib[:, 256 * q:256 * q + 256], lhsT=kte[:, sl],
                                                                          rhs=v_sb[b][:, vs], start=True, stop=True),
                      reads=[r["kte"], r_v[b]], writes=[ib_r])
            ibanks.append((ib, ib_r))
        for hh in range(4):
            ob, ob_r = obanks[hh // 2]
            q = hh % 2
            R.add("act", lambda e, ob=ob, q=q, hh=hh: e.activation(out=junk[:, 256 * hh:256 * hh + 256], in_=ob[:, 256 * q:256 * q + 256],
                                                                    func=AF.Square, accum_out=oss[:, hh:hh + 1]),
                  reads=[ob_r], writes=[r["oss"]])
        R.add("act", lambda e: e.activation(out=oss[:, 4:8], in_=oss[:, 0:4], func=AF.Ln, scale=1.0 / 256, bias=EPS),
              reads=[r["oss"]], writes=[r["oss"]])
        R.add("act", lambda e: e.activation(out=oss[:, 8:12], in_=oss[:, 4:8], func=AF.Exp, scale=-0.5),
              reads=[r["oss"]], writes=[r["oss"]])
        for hh in range(4):
            ob, ob_r = obanks[hh // 2]
            q = hh % 2
            vs = slice(256 * hh, 256 * hh + 256)
            R.add("dve", lambda e, ob=ob, q=q, hh=hh, vs=vs: e.scalar_tensor_tensor(
                out=og[:, vs], in0=ob[:, 256 * q:256 * q + 256], scalar=oss[:, 8 + hh:9 + hh], in1=G2[b][:, vs],
                op0=ALU.mult, op1=ALU.mult),
                reads=[ob_r, r["oss"], r_G2[b]], writes=[r["og"]])
        R.add("dve", lambda e: e.memset(oss[:, 0:4], 0.0), reads=[], writes=[r["oss"]])
        for ob, ob_r in obanks:
            pp.free(ob, ob_r)
        for hh in range(4):
            ib, ib_r = ibanks[hh // 2]
            q = hh % 2
            vs = slice(256 * hh, 256 * hh + 256)
            R.add("dve", lambda e, ib=ib, q=q, hh=hh, vs=vs: e.scalar_tensor_tensor(
                out=S[:, vs], in0=S[:, vs], scalar=E1[:, 128 * hh + 127:128 * hh + 128], in1=ib[:, 256 * q:256 * q + 256],
                op0=ALU.mult, op1=ALU.add),
                reads=[ib_r, r["E1"], r["S"]], writes=[r["S"]])
        R.add("act", lambda e: e.activation(out=S_bf[:], in_=S[:], func=AF.Copy), reads=[r["S"]], writes=[r["S_bf"]])
        for ib, ib_r in ibanks:
            pp.free(ib, ib_r)

    def outp_a(n):
        b = n % NBUF
        for half in range(2):
            tb, tb_r = pp.get()
            for q in range(4):
                fc = 4 * half + q
                R.add("pe", lambda e, tb=tb, q=q, fc=fc: e.matmul(tb[:, 128 * q:128 * q + 128], lhsT=og[:, 128 * fc:128 * fc + 128],
                                                                    rhs=identb, start=True, stop=True),
                      reads=[r["og"], r["identb"]], writes=[tb_r])
            R.add("act", lambda e, tb=tb, half=half: e.activation(out=ogT[:, 512 * half:512 * half + 512], in_=tb[:, :], func=AF.Copy),
                  reads=[tb_r], writes=[r["ogT"]])
            pp.free(tb, tb_r)

    def outp_b(n):
        b = n % NBUF
        for half in range(2):
            yb, yb_r = pp.get()
            sl = slice(512 * half, 512 * half + 512)
            for fc in range(KC):
                R.add("pe", lambda e, yb=yb, fc=fc, sl=sl: e.matmul(yb[:, :], lhsT=ogT[:, 128 * fc:128 * fc + 128], rhs=wout_sb[:, fc, sl],
                                                                     start=(fc == 0), stop=(fc == KC - 1)),
                      reads=[r["ogT"], r_wout[fc]], writes=[yb_r])
            R.add("dve", lambda e, yb=yb, sl=sl: e.tensor_tensor(out=ytmp, in0=yb[:, :], in1=gate_rep[:, sl], op=ALU.mult),
                  reads=[yb_r, r["gate_rep"]], writes=[r["ytmp"]])
            pp.free(yb, yb_r)
            R.add("dve", lambda e, sl=sl: e.tensor_tensor(out=x_sb[:, n, sl], in0=x_sb[:, n, sl], in1=ytmp, op=ALU.add),
                  reads=[r_x[n], r["ytmp"]], writes=[r_x[n]])

    sts = {}
    stage1_xs(0)
    stage1(0)
    gate_setup()
    sts[0] = proj_qk(0)
    proj_vg(0, "v")
    proj_vg(0, "g")
    stage1_xs(1)
    stage1(1)
    for n in range(NB):
        nxt = n + 1 < NB
        if n + 2 < NB:
            stage1_xs(n + 2)
        gates(n, sts[n])
        if nxt:
            sts[n + 1] = proj_qk(n + 1)
        attn_a(n)
        if nxt:
            proj_vg(n + 1, "v")
        attn_b(n)
        if n + 2 < NB:
            stage1(n + 2)
        outp_a(n)
        if nxt:
            proj_vg(n + 1, "g")
        outp_b(n)

    R.emit_phase()
    esA.close()

    esBC = ExitStack()
    kT_all = esBC.enter_context(nc.sbuf_tensor("kT_all", [128, 8, T], BF16))
    v_all = esBC.enter_context(nc.sbuf_tensor("v_all", [128, NB, D], BF16))
    r_kT = [R.R("kT%d" % g) for g in range(4)]
    r_vb = [R.R("vb%d" % j) for j in range(NB)]
    esB = ExitStack()

    def sb(name, shape, dt=F32):
        return esB.enter_context(nc.sbuf_tensor("b_" + name, list(shape), dt))

    wkv_sb = sb("wkv_sb", [128, KC, 2 * D], BF16)
    Jt = sb("Jt", [128, 128])
    Jm = Jt[:, :]
    small = sb("small", [128, 96])
    ssx = sb("ssx", [128, 2 * NB])
    junk = sb("junk", [128, 1024], BF16)
    arena = sb("arena", [128, 4096])
    stage = [arena[:, 0:2048], arena[:, 2048:4096]]
    xs = [arena[:, 0:1024], arena[:, 1024:2048]]
    hkT = [sb("hkT%d" % i, [128, KC, 512], BF16) for i in range(2)]
    rstd = rstd1
    r = {n: R.R(n) for n in ["small", "ssx", "xs0", "xs1", "hkT0", "hkT1", "st1"]}
    r["cst"] = rp["cst"]
    r["vec"] = rp["vec"]
    r["rstd"] = rp["rstd1"]
    r["J"] = R.R("J")
    R.add("sp", lambda e: e.dma_start(out=Jt[:], in_=cst_d[:, 768:896]), writes=[r["J"]], dma=True)
    r_wkv = [R.R("wkv%d" % k) for k in range(KC)]
    r_v = r_vb
    r_xs = [r["xs0"], r["xs1"]]
    r_hkT = [r["hkT0"], r["hkT1"]]
    stage_res = [[r["xs0"], r["xs1"]], [r["st1"]]]
    wkvv = wkv_d.rearrange("(k p) f -> p k f", p=128)
    r_wkvV = [R.R("wkvV%d" % k) for k in range(KC)]
    for kc in range(KC):
        R.add("pool", lambda e, kc=kc: e.dma_start(out=wkv_sb[:, kc, 0:D], in_=wkvv[:, kc, 0:D]), writes=[r_wkv[kc]], dma=True)
    for kc in range(KC):
        R.add("pool", lambda e, kc=kc: e.dma_start(out=wkv_sb[:, kc, D:2 * D], in_=wkvv[:, kc, D:2 * D]), writes=[r_wkvV[kc]], dma=True)
    R.add("dve", lambda e: e.memset(ssx[:], 0.0), writes=[r["ssx"]])
    rowt = sb("rowt", [1, 256])
    r["rt0"] = R.R("rt0")
    r["rt1"] = R.R("rt1")
    ada_matvec(R, wadak_d, 2 * D, sc, rp["sc"], pp, stage, stage_res, small[:, 24:40], r["small"],
               [rowt[0:1, 0:128], rowt[0:1, 128:256]], [r["rt0"], r["rt1"]], ones128[0:1, 0:1], r["cst"], q="sp")
    stg1 = sb("stg1", [128, 2048])
    r["s1a"] = R.R("s1a")
    r["s1b"] = R.R("s1b")
    ada_matvec(R, wada1_d, 3 * D, sc, rp["sc"], pp, [stg1[:, 0:1024], stg1[:, 1024:2048]], [[r["s1a"]], [r["s1b"]]],
               cond1[:, 0:24], rp["cond1"], [rowt[0:1, 0:128], rowt[0:1, 128:256]], [r["rt0"], r["rt1"]],
               ones128[0:1, 0:1], r["cst"], q="sp", maxw=1024)
    R.add("dve", lambda e: e.tensor_tensor(out=small[:, 40:48], in0=small[:, 24:32], in1=vec[:, 52:60], op=ALU.add),
          reads=[r["small"], r["vec"]], writes=[r["small"]])
    R.add("dve", lambda e: e.scalar_tensor_tensor(out=small[:, 48:56], in0=small[:, 32:40], scalar=1.0, in1=vec[:, 60:68],
                                                  op0=ALU.add, op1=ALU.add),
          reads=[r["small"], r["vec"]], writes=[r["small"]])
    R.add("dve", lambda e: e.tensor_tensor(out=small[:, 48:56], in0=small[:, 48:56], in1=vec[:, 44:52], op=ALU.mult),
          reads=[r["small"], r["vec"]], writes=[r["small"]])
    shiftk = small[:, 40:48]
    Ak = small[:, 48:56]

    def norm_block(n, slot):
        b = n % 2
        g = (NB - 1 - n) // 4
        hb = g % 2
        R.add("act", lambda e: e.activation(out=junk[:], in_=x_sb[:, n, :], func=AF.Square, accum_out=ssx[:, n:n + 1]),
              reads=[r_x[n]], writes=[r["ssx"]])
        R.add("act", lambda e: e.activation(out=ssx[:, NB + n:NB + n + 1], in_=ssx[:, n:n + 1], func=AF.Ln, scale=1.0 / D, bias=EPS),
              reads=[r["ssx"]], writes=[r["ssx"]])
        R.add("act", lambda e: e.activation(out=rstd[:, n:n + 1], in_=ssx[:, NB + n:NB + n + 1], func=AF.Exp, scale=-0.5),
              reads=[r["ssx"]], writes=[r["rstd"]])
        R.add("dve", lambda e: e.tensor_scalar_mul(out=xs[b], in0=x_sb[:, n, :], scalar1=rstd[:, n:n + 1]),
              reads=[r_x[n], r["rstd"]], writes=[r_xs[b]])
        for half in range(2):
            bk, bk_r = pp.get()
            for q in range(4):
                kc = 4 * half + q
                R.add("pe", lambda e, bk=bk, q=q, kc=kc: e.matmul(bk[:, 128 * q:128 * q + 128],
                                                                    lhsT=xs[b][:, 128 * kc:128 * kc + 128], rhs=Jm,
                                                                    start=True, stop=True),
                      reads=[r_xs[b], r["J"]], writes=[bk_r])
            for q in range(4):
                kc = 4 * half + q
                R.add("act", lambda e, bk=bk, q=q, kc=kc: e.activation(
                    out=hkT[hb][:, kc, 128 * slot:128 * slot + 128], in_=bk[:, 128 * q:128 * q + 128],
                    func=AF.Identity, scale=Ak[:, kc:kc + 1], bias=shiftk[:, kc:kc + 1]),
                    reads=[bk_r, r["small"]], writes=[r_hkT[hb]])
            pp.free(bk, bk_r)

    def kv_group(g):
        hb = g % 2
        for hp in range(8):
            bk, bk_r = pp.get()
            for kc in range(KC):
                R.add("pe", lambda e, bk=bk, hp=hp, kc=kc: e.matmul(bk[:, :], lhsT=wkv_sb[:, kc, 128 * hp:128 * hp + 128],
                                                                     rhs=hkT[hb][:, kc, :], start=(kc == 0), stop=(kc == KC - 1)),
                      reads=[r_wkv[kc], r_hkT[hb]], writes=[bk_r])
            eng = "act" if hp % 2 == 0 else "dve"
            if eng == "act":
                R.add("act", lambda e, bk=bk, hp=hp: e.activation(out=kT_all[:, hp, 512 * g:512 * g + 512], in_=bk[:, :], func=AF.Copy),
                      reads=[bk_r], writes=[r_kT[g]])
            else:
                R.add("dve", lambda e, bk=bk, hp=hp: e.tensor_copy(out=kT_all[:, hp, 512 * g:512 * g + 512], in_=bk[:, :]),
                      reads=[bk_r], writes=[r_kT[g]])
            pp.free(bk, bk_r)
        for j in range(4):
            jb = 4 * g + j
            for half in range(2):
                bk, bk_r = pp.get()
                for kc in range(KC):
                    R.add("pe", lambda e, bk=bk, kc=kc, j=j, half=half: e.matmul(
                        bk[:, :], lhsT=hkT[hb][:, kc, 128 * j:128 * j + 128],
                        rhs=wkv_sb[:, kc, 1024 + 512 * half:1024 + 512 * half + 512], start=(kc == 0), stop=(kc == KC - 1)),
                        reads=[r_wkvV[kc], r_hkT[hb]], writes=[bk_r])
                if half == 0:
                    R.add("act", lambda e, bk=bk, jb=jb, half=half: e.activation(out=v_all[:, jb, 512 * half:512 * half + 512], in_=bk[:, :],
                                                                                  func=AF.Copy),
                          reads=[bk_r], writes=[r_v[jb]])
                else:
                    R.add("dve", lambda e, bk=bk, jb=jb, half=half: e.tensor_copy(out=v_all[:, jb, 512 * half:512 * half + 512], in_=bk[:, :]),
                          reads=[bk_r], writes=[r_v[jb]])
                pp.free(bk, bk_r)

    for g in range(4):
        for slot in range(4):
            n = NB - 1 - (4 * g + slot)
            norm_block(n, slot)
        kv_group(g)

    R.emit_phase()
    esB.close()

    esC = ExitStack()

    def sb(name, shape, dt=F32):
        return esC.enter_context(nc.sbuf_tensor("c_" + name, list(shape), dt))

    kT = kT_all
    v_sb = v_all
    r_v = [R.R("vg%d" % g) for g in range(4)]
    win_sb = sb("win_sb", [128, KC, 2 * D], BF16)
    wout_sb = sb("wout_sb", [128, KC, D], BF16)
    small = sb("small", [128, 96])
    ssx = sb("ssx", [128, 4 * NB])
    NPIPE = 4
    arena = sb("arena", [128, 4352])
    stage = [arena[:, 0:2048], arena[:, 2048:4096]]
    sg = arena[:, 0:1024]
    xs = arena[:, 1024:2048]
    abuf = [arena[:, 2048:2560], arena[:, 2560:3072]]
    h1og = sb("h1og", [128, D], BF16)
    h1T = h1og.rearrange("p (k t) -> p k t", k=KC)
    og = h1og
    qTp = [sb("qT%d" % i, [128, KC * 128], BF16) for i in range(2)]
    wn2 = sb("wn2", [128, 1024], BF16)
    ogT = wn2
    pbv = arena[:, 3072:4352].bitcast(BF16)
    Pb = [pbv[:, 516 * i:516 * i + 516] for i in range(NPIPE)]
    wT = [sb("wT%d" % i, [128, 512], BF16) for i in range(2)]

    r = {n: R.R(n) for n in ["small", "ssx", "sg", "xs", "ab0", "ab1", "ab2", "h1T", "qT", "wn0", "wn1", "wn2", "wT0", "wT1", "wT2", "P0", "P1", "P2", "rt", "rt2"]}
    r["cst"] = rp["cst"]
    r["cstb"] = rp["cstb"]
    r["vec"] = rp["vec"]
    r["rstd1"] = rp["rstd1"]
    r_win = [R.R("cwin%d" % k) for k in range(KC)]
    r_wout = [R.R("cwout%d" % k) for k in range(KC)]
    r_ab = [r["ab0"], r["ab1"]]
    r_wT = [r["wT0"], r["wT1"]]
    r_P = [r["P0"], r["P1"], r["P2"], r["rt"]]
    stage_res = [[r["sg"], r["xs"]], [r["ab0"], r["ab1"], r["P0"], r["P1"], r["P2"]]]

    winv = sbwin_d.rearrange("(k p) f -> p k f", p=128)
    woutv = sbwout_d.rearrange("(k p) f -> p k f", p=128)
    r_winG = [R.R("cwinG%d" % k) for k in range(KC)]
    for kc in range(KC):
        R.add("pool", lambda e, kc=kc: e.dma_start(out=win_sb[:, kc, 0:D], in_=winv[:, kc, 0:D]), writes=[r_win[kc]], dma=True)
    for kc in range(KC):
        R.add("pool", lambda e, kc=kc: e.dma_start(out=win_sb[:, kc, D:2 * D], in_=winv[:, kc, D:2 * D]), writes=[r_winG[kc]], dma=True)
    R.add("dve", lambda e: e.memset(ssx[:], 0.0), writes=[r["ssx"]])
    R.add("dve", lambda e: e.memset(qTp[0][64:128, :], 0.0), writes=[r["qT"]])
    R.add("dve", lambda e: e.memset(qTp[1][0:64, :], 0.0), writes=[r["qT"]])
    R.add("dve", lambda e: e.tensor_copy(out=small[:, 24:48], in_=cond1[:, 0:24]), reads=[rp["cond1"]], writes=[r["small"]])
    R.add("dve", lambda e: e.tensor_tensor(out=small[:, 48:56], in0=small[:, 24:32], in1=vec[:, 76:84], op=ALU.add),
          reads=[r["small"], r["vec"]], writes=[r["small"]])
    R.add("dve", lambda e: e.scalar_tensor_tensor(out=small[:, 56:64], in0=small[:, 32:40], scalar=1.0, in1=vec[:, 84:92],
                                                  op0=ALU.add, op1=ALU.add),
          reads=[r["small"], r["vec"]], writes=[r["small"]])
    R.add("dve", lambda e: e.tensor_tensor(out=small[:, 56:64], in0=small[:, 56:64], in1=vec[:, 68:76], op=ALU.mult),
          reads=[r["small"], r["vec"]], writes=[r["small"]])
    R.add("dve", lambda e: e.tensor_tensor(out=small[:, 64:72], in0=small[:, 40:48], in1=vec[:, 92:100], op=ALU.add),
          reads=[r["small"], r["vec"]], writes=[r["small"]])
    tg = [abuf[0][:, 0:128], abuf[1][:, 0:128]]
    bcast_rows(R, pp, small[:, 64:72], r["small"], 8, tg, [r["ab0"], r["ab1"]], ones128, ident, r["cst"], xs, r["xs"])
    wst = [sg, arena[:, 2048:3072]]
    wst_res = [[r["sg"]], [r["ab0"], r["ab1"]]]
    for kc in range(KC):
        si = kc % 2
        R.add("sp", lambda e, kc=kc, si=si: e.dma_start(out=wst[si], in_=woutv[:, kc, :]), writes=wst_res[si], dma=True)
        eng = "dve" if kc % 2 == 0 else "pool"
        R.add(eng, lambda e, kc=kc, si=si: e.tensor_tensor(out=wout_sb[:, kc, :], in0=wst[si], in1=xs, op=ALU.mult),
              reads=wst_res[si] + [r["xs"]], writes=[r_wout[kc]])
    shift1 = small[:, 48:56]
    A1 = small[:, 56:64]

    def prologue(i):
        R.add("dve", lambda e: e.tensor_scalar_mul(out=xs, in0=x_sb[:, i, :], scalar1=rstd1[:, i:i + 1]),
              reads=[r_x[i], r["rstd1"]], writes=[r["xs"]])
        for half in range(2):
            bk, bk_r = pp.get()
            for q in range(4):
                kc = 4 * half + q
                R.add("pe", lambda e, bk=bk, q=q, kc=kc: e.matmul(bk[:, 128 * q:128 * q + 128], lhsT=xs[:, 128 * kc:128 * kc + 128],
                                                                    rhs=ident, start=True, stop=True),
                      reads=[r["xs"], r["cst"]], writes=[bk_r])
            for q in range(4):
                kc = 4 * half + q
                R.add("act", lambda e, bk=bk, q=q, kc=kc: e.activation(out=h1T[:, kc, :], in_=bk[:, 128 * q:128 * q + 128],
                                                                        func=AF.Identity, scale=A1[:, kc:kc + 1], bias=shift1[:, kc:kc + 1]),
                      reads=[bk_r, r["small"]], writes=[r["h1T"]])
            pp.free(bk, bk_r)
        for half in range(2):
            bk, bk_r = pp.get()
            for q in range(4):
                hp = 4 * half + q
                for kc in range(KC):
                    R.add("pe", lambda e, bk=bk, q=q, hp=hp, kc=kc: e.matmul(bk[:, 128 * q:128 * q + 128],
                                                                              lhsT=win_sb[:, kc, 128 * hp:128 * hp + 128],
                                                                              rhs=h1T[:, kc, :], start=(kc == 0), stop=(kc == KC - 1)),
                          reads=[r_win[kc], r["h1T"]], writes=[bk_r])
            R.add("dve", lambda e, bk=bk, half=half: e.tensor_copy(out=qTp[0][0:64, 512 * half:512 * half + 512], in_=bk[0:64, :]),
                  reads=[bk_r], writes=[r["qT"]])
            R.add("dve", lambda e, bk=bk, half=half: e.tensor_copy(out=qTp[1][64:128, 512 * half:512 * half + 512], in_=bk[64:128, :]),
                  reads=[bk_r], writes=[r["qT"]])
            pp.free(bk, bk_r)
        for half in range(2):
            bk, bk_r = pp.get()
            sl = slice(512 * half, 512 * half + 512)
            for kc in range(KC):
                R.add("pe", lambda e, bk=bk, kc=kc, half=half: e.matmul(bk[:, :], lhsT=h1T[:, kc, :],
                                                                         rhs=win_sb[:, kc, 1024 + 512 * half:1024 + 512 * half + 512],
                                                                         start=(kc == 0), stop=(kc == KC - 1)),
                      reads=[r_winG[kc], r["h1T"]], writes=[bk_r])
            R.add("act", lambda e, bk=bk, sl=sl: e.activation(out=sg[:, sl], in_=bk[:, :], func=AF.Sigmoid),
                  reads=[bk_r], writes=[r["sg"]])
            R.add("dve", lambda e, bk=bk, sl=sl: e.tensor_tensor(out=sg[:, sl], in0=bk[:, :], in1=sg[:, sl], op=ALU.mult),
                  reads=[bk_r, r["sg"]], writes=[r["sg"]])
            pp.free(bk, bk_r)

    def attention(i):
        L = 128 * (i + 1)
        base = T - L
        nseg = (L + 511) // 512
        obanks = [pp.get(), pp.get()]
        units = [(h, k) for hp in range(8) for k in range(nseg) for h in (hp, hp + 8)]
        state = {}

        def qk(u, idx):
            h, k = u
            c0 = base + 512 * k
            w = min(512, L - 512 * k)
            zb, zb_r = pp.get()
            ps = slice(64 * (h % 2), 64 * (h % 2) + 64)
            hp = h // 2
            gset = sorted(set([(c0) // 512, (c0 + w - 1) // 512]))
            R.add("pe", lambda e: e.matmul(zb[:, 0:w], lhsT=qTp[h % 2][:, 128 * hp:128 * hp + 128], rhs=kT[:, hp, c0:c0 + w],
                                           start=True, stop=(k != 0)),
                  reads=[r["qT"]] + [r_kT[g] for g in gset], writes=[zb_r])
            if k == 0:
                R.add("pe", lambda e: e.matmul(zb[:, 0:128], lhsT=identb, rhs=maskb, start=False, stop=True),
                      reads=[r["cstb"]], writes=[zb_r])
            a = idx % 2
            R.add("act", lambda e: e.activation(out=abuf[a][:, 0:w], in_=zb[:, 0:w], func=AF.Sigmoid, scale=-0.125),
                  reads=[zb_r], writes=[r_ab[a]])
            pp.free(zb, zb_r)
            pb = idx % NPIPE
            if k > 0:
                pprev = (idx - 2) % NPIPE
                R.add("pool", lambda e: e.tensor_copy(out=Pb[pb][:, 0:1], in_=Pb[pprev][:, 512:513]),
                      reads=[r_P[pprev]], writes=[r_P[pb]])
            else:
                R.add("pool", lambda e: e.memset(Pb[pb][:, 0:1], 1.0), writes=[r_P[pb]])
            R.add("dve", lambda e: e.tensor_tensor_scan(out=Pb[pb][:, 1:1 + w], data0=abuf[a][:, 0:w], data1=abuf[a][:, 0:w],
                                                        initial=Pb[pb][:, 0:1], op0=ALU.mult, op1=ALU.min),
                  reads=[r_ab[a], r_P[pb]], writes=[r_P[pb]])
            state[idx] = (h, k, c0, w, a)

        def tr(idx):
            h, k, c0, w, a = state[idx]
            tb, tb_r = pp.get()
            for jb in range(w // 128):
                R.add("pe", lambda e, jb=jb: e.matmul(tb[:, 128 * jb:128 * jb + 128], lhsT=Pb[idx % NPIPE][:, 1 + 128 * jb:129 + 128 * jb], rhs=identb,
                                                      start=True, stop=False),
                      reads=[r_P[idx % NPIPE], r["cstb"]], writes=[tb_r])
                R.add("pe", lambda e, jb=jb: e.matmul(tb[:, 128 * jb:128 * jb + 128], lhsT=Pb[idx % NPIPE][:, 128 * jb:128 + 128 * jb], rhs=negidentb,
                                                      start=False, stop=True),
                      reads=[r_P[idx % NPIPE], r["cstb"]], writes=[tb_r])
            a2 = idx % 2
            R.add("act", lambda e: e.activation(out=wT[a2][:, 0:w], in_=tb[:, 0:w], func=AF.Copy),
                  reads=[tb_r], writes=[r_wT[a2]])
            pp.free(tb, tb_r)

        def pv(idx):
            h, k, c0, w, a = state[idx]
            ob, ob_r = obanks[h // 8]
            a2 = idx % 2
            nb_ = w // 128
            for jb in range(nb_):
                blk = (c0 + 128 * jb) // 128
                first = (k == 0 and jb == 0)
                last = (k == nseg - 1 and jb == nb_ - 1)
                R.add("pe", lambda e, jb=jb, blk=blk, first=first, last=last: e.matmul(
                    ob[:, 64 * (h % 8):64 * (h % 8) + 64], lhsT=wT[a2][:, 128 * jb:128 * jb + 128], rhs=v_sb[:, blk, 64 * h:64 * h + 64],
                    start=first, stop=last),
                    reads=[r_wT[a2], r_v[blk // 4]], writes=[ob_r])

        n = len(units)
        for s in range(n + 4):
            if s < n:
                qk(units[s], s)
            if 0 <= s - 3 < n:
                tr(s - 3)
            if 0 <= s - 4 < n:
                pv(s - 4)
        return obanks

    def epilogue(i, obanks):
        for half in range(2):
            ob, ob_r = obanks[half]
            sl = slice(512 * half, 512 * half + 512)
            R.add("dve", lambda e, ob=ob, sl=sl: e.scalar_tensor_tensor(out=og[:, sl], in0=ob[:, :], scalar=-1.0, in1=sg[:, sl],
                                                                        op0=ALU.mult, op1=ALU.mult),
                  reads=[ob_r, r["sg"]], writes=[r["h1T"]])
            pp.free(ob, ob_r)
        for half in range(2):
            tb, tb_r = pp.get()
            for q in range(4):
                fc = 4 * half + q
                R.add("pe", lambda e, tb=tb, q=q, fc=fc: e.matmul(tb[:, 128 * q:128 * q + 128], lhsT=og[:, 128 * fc:128 * fc + 128],
                                                                    rhs=identb, start=True, stop=True),
                      reads=[r["h1T"], r["cstb"]], writes=[tb_r])
            R.add("act", lambda e, tb=tb, half=half: e.activation(out=ogT[:, 512 * half:512 * half + 512], in_=tb[:, :], func=AF.Copy),
                  reads=[tb_r], writes=[r["wn0"], r["wn1"]])
            pp.free(tb, tb_r)

    def epilogue2(i):
        for half in range(2):
            yb, yb_r = pp.get()
            sl = slice(512 * half, 512 * half + 512)
            for fc in range(KC):
                R.add("pe", lambda e, yb=yb, fc=fc, sl=sl: e.matmul(yb[:, :], lhsT=ogT[:, 128 * fc:128 * fc + 128], rhs=wout_sb[:, fc, sl],
                                                                     start=(fc == 0), stop=(fc == KC - 1)),
                      reads=[r["wn0"], r["wn1"], r_wout[fc]], writes=[yb_r])
            R.add("dve", lambda e, yb=yb, sl=sl: e.tensor_tensor(out=x_sb[:, i, sl], in0=yb[:, :], in1=x_sb[:, i, sl], op=ALU.add),
                  reads=[yb_r, r_x[i]], writes=[r_x[i]])
            pp.free(yb, yb_r)
            R.add("act", lambda e, half=half, sl=sl: e.activation(out=wT[half][:, :], in_=x_sb[:, i, sl], func=AF.Square,
                                                                  accum_out=ssx[:, 32 * half + i:32 * half + i + 1]),
                  reads=[r_x[i]], writes=[r_wT[half], r["ssx"]])

    prologue(0)
    for i in range(NB):
        ob = attention(i)
        epilogue(i, ob)
        if i + 1 < NB:
            prologue(i + 1)
        epilogue2(i)
    R.add("dve", lambda e: e.tensor_tensor(out=ssx[:, 0:16], in0=ssx[:, 0:16], in1=ssx[:, 32:48], op=ALU.add),
          reads=[r["ssx"]], writes=[r["ssx"]])

    R.add("act", lambda e: e.activation(out=ssx[:, 16:32], in_=ssx[:, 0:16], func=AF.Ln, scale=1.0 / D, bias=EPS),
          reads=[r["ssx"]], writes=[r["ssx"]])
    R.add("act", lambda e: e.activation(out=ssx[:, 48:64], in_=ssx[:, 16:32], func=AF.Exp, scale=-0.5),
          reads=[r["ssx"]], writes=[r["ssx"]])
    fg = sg
    R.add("sp", lambda e: e.dma_start(out=fg, in_=fg_d), writes=[r["sg"]], dma=True)
    for n in range(NB):
        R.add("act", lambda e, n=n: e.activation(out=x_sb[:, n, :], in_=x_sb[:, n, :], func=AF.Identity, scale=ssx[:, 48 + n:49 + n]),
              reads=[r_x[n], r["ssx"]], writes=[r_x[n]])
        eng = "dve" if n % 2 == 0 else "pool"
        R.add(eng, lambda e, n=n: e.tensor_tensor(out=x_sb[:, n, :], in0=x_sb[:, n, :], in1=fg, op=ALU.mult),
              reads=[r_x[n], r["sg"]], writes=[r_x[n]])
        R.add("sp", lambda e, n=n: e.dma_start(out=outv[:, n, :], in_=x_sb[:, n, :]), reads=[r_x[n]], dma=True, final=True)
    R.emit_phase(last=True)
    esC.close()
    esBC.close()
    es.close()
    return R


def host_inputs(inp, b):
    import ml_dtypes
    f = np.float32
    fm = lambda v: np.ascontiguousarray(np.asarray(v, f).reshape(-1, 128).T)
    vec = np.zeros((128, 100), f)
    vec[:, 0:8] = fm(inp["c"][b])
    vec[:, 8:16] = fm(inp["norm_gain"][0])
    vec[:, 16:40] = fm(inp["b_ada"][0])
    vec[:, 40:44] = fm(inp["gla_b_gk"][0])
    vec[:, 44:52] = fm(inp["kv_gain"])
    vec[:, 52:68] = fm(inp["kv_b_ada"])
    vec[:, 68:76] = fm(inp["norm_gain"][1])
    vec[:, 76:100] = fm(inp["b_ada"][1])
    cst = np.zeros((128, 896), f)
    cst[:, 0:128] = np.eye(128, dtype=f)
    mt = (np.arange(128)[:, None] <= np.arange(128)[None, :]).astype(f)
    cst[:, 128:640] = np.tile(mt, (1, 4))
    cst[:, 640:768] = 1.0
    cst[:, 768:896] = np.eye(128, dtype=f)[::-1]
    cb = np.zeros((128, 384), f)
    cb[:, 0:128] = np.eye(128, dtype=f)
    rr = np.arange(128)[:, None]
    cc = np.arange(128)[None, :]
    cb[:, 128:256] = np.where(cc <= 127 - rr, -30000.0, 0.0)
    cb[:, 256:384] = -np.eye(128, dtype=f)
    return {
        "x": np.ascontiguousarray(inp["x"][b], dtype=f),
        "vec": vec,
        "ogain_rep": np.ascontiguousarray(np.broadcast_to(np.asarray(inp["gla_o_gain"][0], f)[None, :], (128, 256))),
        "consts": cst,
        "constsb": cb.astype(ml_dtypes.bfloat16),
        "fg_rep": np.ascontiguousarray(np.broadcast_to(np.asarray(inp["final_gain"], f)[None, :], (128, D))),
        "w_ada0": np.ascontiguousarray(inp["w_ada"][0], dtype=f),
        "w_ada1": np.ascontiguousarray(inp["w_ada"][1], dtype=f),
        "kv_w_ada": np.ascontiguousarray(inp["kv_w_ada"], dtype=f),
        "gla_w_in": np.ascontiguousarray(inp["gla_w_in"][0], dtype=f),
        "w_gk2": np.ascontiguousarray(inp["gla_w_gk2"][0], dtype=f),
        "gla_w_out": np.ascontiguousarray(inp["gla_w_out"][0], dtype=f),
        "w_kv": np.ascontiguousarray(inp["w_kv"], dtype=f),
        "sb_w_in": np.ascontiguousarray(inp["sb_w_in"][0], dtype=f),
        "sb_w_out": np.ascontiguousarray(inp["sb_w_out"][0], dtype=f),
    }


NCORES = 8


def kernel(**inputs):
    inp = {k: np.asarray(v) for k, v in inputs.items()}
    nc = bass.Bass("TRN2", target_bir_lowering=False)
    build_fused(nc)
    in_maps = [host_inputs(inp, b) for b in range(NCORES)]
    res = run_bass_kernel_spmd(nc, in_maps, core_ids=list(range(NCORES)))
    return np.stack([r["out"] for r in res.results]).astype(np.float32)
```

```python
import numpy as np
from contextlib import ExitStack
import concourse.bass as bass
import concourse.mybir as mybir
from concourse.bass_utils import run_bass_kernel_spmd

F32 = mybir.dt.float32
BF16 = mybir.dt.bfloat16
AF = mybir.ActivationFunctionType
ALU = mybir.AluOpType

T = 2048
D = 1024
NB = 16
KC = 8
EPS = 1e-6
GIN = 3088


ENG = ("pe", "act", "dve", "pool", "sp")


class Res:
    __slots__ = ("name", "w", "r")

    def __init__(self, name):
        self.name = name
        self.w = None
        self.r = []


class Op:
    __slots__ = ("eng", "emit", "deps", "sig", "count", "dma", "sem", "semval", "final", "phase")

    def __init__(self, eng, emit, dma):
        self.eng = eng
        self.emit = emit
        self.dma = dma
        self.deps = []
        self.sig = False
        self.count = 0
        self.sem = None
        self.final = False
        self.phase = 0


class Rec:
    def __init__(self, nc, es):
        self.nc = nc
        self.es = es
        self.q = {e: [] for e in ENG}
        self.n = 0
        self.phase = 0
        self.sems = {e: es.enter_context(nc.semaphore("s_" + e)) for e in ENG}
        self.bsem = es.enter_context(nc.semaphore("s_bar"))
        self.cnt = {e: 0 for e in ENG}
        self.finals = []
        self.dsems = []
        self.res = []

    def R(self, name):
        r = Res(name)
        self.res.append(r)
        return r

    def add(self, eng, emit, reads=(), writes=(), dma=False, final=False):
        op = Op(eng, emit, dma)
        op.final = final
        op.phase = self.phase
        deps = {}
        for r in reads:
            if r.w is not None:
                deps[id(r.w)] = r.w
        for w in writes:
            if w.w is not None:
                deps[id(w.w)] = w.w
            for x in w.r:
                deps[id(x)] = x
        for d in deps.values():
            if d is op:
                continue
            if (not dma) and (not d.dma) and d.eng == "pe" and eng == "pe":
                continue
            assert d.dma or d.phase == self.phase, "compute dep crosses phase"
            op.deps.append(d)
            d.sig = True
        for r in reads:
            if not dma:
                r.r = [x for x in r.r if x.dma or x.eng != eng]
            r.r.append(op)
        for w in writes:
            w.w = op
            w.r = []
        self.q[eng].append(op)
        self.n += 1
        return op

    def emit_phase(self, last=False):
        nc = self.nc
        di = 0
        for e in ENG:
            for op in self.q[e]:
                if op.dma:
                    if e == "pool":
                        self.nsw = getattr(self, "nsw", 0) + 1
                        op.sem = self.es.enter_context(nc.semaphore("dsw%d" % self.nsw))
                        op.semval = 16
                    else:
                        if di == len(self.dsems):
                            self.dsems.append([self.es.enter_context(nc.semaphore("d%d" % di)), 0])
                        ent = self.dsems[di]
                        di += 1
                        ent[1] += 1
                        op.sem = ent[0]
                        op.semval = 16 * ent[1]
                    if op.final:
                        self.finals.append(op)
                elif op.sig:
                    self.cnt[e] += 1
                    op.count = self.cnt[e]
        self.phase += 1
        k = self.phase
        phase_dmas = [op for e in ENG for op in self.q[e] if op.dma]
        with nc.Block() as block:
            def run(e, eng):
                waited = {}
                for op in self.q[e]:
                    for d in op.deps:
                        if d.dma:
                            key = id(d)
                            if key not in waited:
                                eng.wait_ge(d.sem, d.semval)
                                waited[key] = 1
                        else:
                            if waited.get(d.eng, 0) < d.count:
                                eng.wait_ge(self.sems[d.eng], d.count)
                                waited[d.eng] = d.count
                    inst = op.emit(eng)
                    if op.dma:
                        inst.then_inc(op.sem, 16)
                    elif op.sig:
                        inst.then_inc(self.sems[e], 1)
                if e == "sp":
                    for d in phase_dmas:
                        eng.wait_ge(d.sem, d.semval)
                eng.drain().then_inc(self.bsem, 1)
                eng.wait_ge(self.bsem, len(ENG) * k)

            @block.tensor
            def _(eng):
                run("pe", eng)

            @block.scalar
            def _(eng):
                run("act", eng)

            @block.vector
            def _(eng):
                run("dve", eng)

            @block.gpsimd
            def _(eng):
                run("pool", eng)

            @block.sync
            def _(eng):
                run("sp", eng)
        self.q = {e: [] for e in ENG}
        for r in self.res:
            r.w = None
            r.r = []

class PsumPool:
    def __init__(self, banks, res):
        self.free_list = list(zip(banks, res))

    def get(self):
        assert self.free_list, "out of PSUM banks"
        return self.free_list.pop(0)

    def free(self, bk, res):
        self.free_list.append((bk, res))


def ada_matvec(R, wdram, ncols, sc, sc_res, pp, stage, stage_res, col_out, col_out_res, rowtmp, rowtmp_res, one11, one_res,
               q="act", hook=None, maxw=2048):
    colbank, colbank_res = pp.get()
    i = 0
    for c0 in range(0, ncols, maxw):
        width = min(maxw, ncols - c0)
        nb = width // 512
        rbs = [pp.get() for _ in range(nb)]
        for kc in range(KC):
            st, st_res = stage[i % 2], stage_res[i % 2]
            i += 1
            R.add(q, lambda e, st=st, kc=kc, c0=c0, width=width: e.dma_start(out=st[:, 0:width],
                                                                              in_=wdram[128 * kc:128 * kc + 128, c0:c0 + width]),
                  writes=st_res, dma=True)
            if hook is not None:
                hook(kc)
            for bi, (rb, rb_r) in enumerate(rbs):
                R.add("pe", lambda e, st=st, kc=kc, rb=rb, bi=bi: e.matmul(rb[0:1, 0:512], lhsT=sc[:, kc:kc + 1],
                                                                            rhs=st[:, 512 * bi:512 * bi + 512],
                                                                            start=(kc == 0), stop=(kc == KC - 1)),
                      reads=st_res + [sc_res], writes=[rb_r])
        for bi, (rb, rb_r) in enumerate(rbs):
            for m4 in range(4):
                g = (c0 + 512 * bi) // 128 + m4
                m = g % 2
                R.add("act", lambda e, m=m, rb=rb, m4=m4: e.activation(out=rowtmp[m], in_=rb[0:1, 128 * m4:128 * m4 + 128], func=AF.Copy),
                      reads=[rb_r], writes=[rowtmp_res[m]])
                R.add("pe", lambda e, m=m, g=g: e.matmul(colbank[:, g:g + 1], lhsT=rowtmp[m], rhs=one11, start=True, stop=True),
                      reads=[rowtmp_res[m], one_res], writes=[colbank_res])
            pp.free(rb, rb_r)
    ng = ncols // 128
    R.add("act", lambda e: e.activation(out=col_out[:, 0:ng], in_=colbank[:, 0:ng], func=AF.Copy),
          reads=[colbank_res], writes=[col_out_res])
    pp.free(colbank, colbank_res)


def bcast_rows(R, pp, src, src_res, ncols8, tmpg, tmpg_res, ones128, ident, cst_res, out, out_res):
    for half in range(ncols8 // 4):
        bk, bk_r = pp.get()
        for q4 in range(4):
            kc = 4 * half + q4
            tg, tg_r = tmpg[kc % 2], tmpg_res[kc % 2]
            R.add("dve", lambda e, tg=tg, kc=kc: e.tensor_scalar_mul(out=tg, in0=ones128, scalar1=src[:, kc:kc + 1]),
                  reads=[src_res, cst_res], writes=[tg_r])
            R.add("pe", lambda e, bk=bk, q4=q4, tg=tg: e.matmul(bk[:, 128 * q4:128 * q4 + 128], lhsT=tg, rhs=ident,
                                                                 start=True, stop=True),
                  reads=[tg_r, cst_res], writes=[bk_r])
        R.add("act", lambda e, bk=bk, half=half: e.activation(out=out[:, 512 * half:512 * half + 512], in_=bk[:, :], func=AF.Copy),
              reads=[bk_r], writes=[out_res])
        pp.free(bk, bk_r)


def silu_small(R, c_ap, c_res, tmp, tmp_res, out, out_res, n):
    R.add("act", lambda e: e.activation(out=tmp[:, 0:n], in_=c_ap, func=AF.Exp, scale=-1.0),
          reads=[c_res], writes=[tmp_res])
    R.add("act", lambda e: e.activation(out=tmp[:, n:2 * n], in_=tmp[:, 0:n], func=AF.Ln, bias=1.0),
          reads=[tmp_res], writes=[tmp_res])
    R.add("act", lambda e: e.activation(out=tmp[:, 0:n], in_=tmp[:, n:2 * n], func=AF.Exp, scale=-1.0),
          reads=[tmp_res], writes=[tmp_res])
    R.add("dve", lambda e: e.tensor_tensor(out=out, in0=c_ap, in1=tmp[:, 0:n], op=ALU.mult),
          reads=[c_res, tmp_res], writes=[out_res])


def build_fused(nc):
    es = ExitStack()
    R = Rec(nc, es)

    def dram(name, shape, dt=F32, kind="ExternalInput"):
        return nc.dram_tensor(name, list(shape), dt, kind=kind).ap()

    x_d = dram("x", [T, D])
    vec_d = dram("vec", [128, 100])
    ogain_d = dram("ogain_rep", [128, 256])
    cst_d = dram("consts", [128, 896])
    cstb_d = dram("constsb", [128, 384], BF16)
    fg_d = dram("fg_rep", [128, D])
    wada0_d = dram("w_ada0", [D, 3 * D])
    wada1_d = dram("w_ada1", [D, 3 * D])
    wadak_d = dram("kv_w_ada", [D, 2 * D])
    win_d = dram("gla_w_in", [D, GIN])
    wgk2_d = dram("w_gk2", [16, 512])
    wout_d = dram("gla_w_out", [D, D])
    wkv_d = dram("w_kv", [D, 2 * D])
    sbwin_d = dram("sb_w_in", [D, 2 * D])
    sbwout_d = dram("sb_w_out", [D, D])
    out_d = dram("out", [T, D], kind="ExternalOutput")
    xv = x_d.rearrange("(n p) d -> p n d", p=128)
    outv = out_d.rearrange("(n p) d -> p n d", p=128)

    def psb(name, shape, dt=F32):
        return es.enter_context(nc.sbuf_tensor("sb_" + name, list(shape), dt))

    x_sb = psb("x_sb", [128, NB, D])
    vec = psb("vec", [128, 100])
    cst = psb("cst", [128, 256])
    cstb = psb("cstb", [128, 384], BF16)
    sc = psb("sc", [128, 8])
    rstd1 = psb("rstd1", [128, NB])
    cond1 = psb("cond1", [128, 24])
    ident = cst[:, 0:128]
    ones128 = cst[:, 128:256]
    identb = cstb[:, 0:128]
    maskb = cstb[:, 128:256]
    negidentb = cstb[:, 256:384]

    banks = [es.enter_context(nc.psum_tensor("bank%d" % i, [128, 512], F32)) for i in range(8)]
    pp = PsumPool(banks, [R.R("bank%d" % i) for i in range(8)])
    r_x = [R.R("x%d" % n) for n in range(NB)]
    rp = {n: R.R(n) for n in ["vec", "cst", "cstb", "sc", "rstd1", "cond1"]}

    R.add("sp", lambda e: e.dma_start(out=vec[:], in_=vec_d), writes=[rp["vec"]], dma=True)
    R.add("sp", lambda e: e.dma_start(out=cst[:, 0:128], in_=cst_d[:, 0:128]), writes=[rp["cst"]], dma=True)
    R.add("sp", lambda e: e.dma_start(out=cst[:, 128:256], in_=cst_d[:, 640:768]), writes=[rp["cst"]], dma=True)
    R.add("sp", lambda e: e.dma_start(out=cstb[:], in_=cstb_d), writes=[rp["cstb"]], dma=True)

    esA = ExitStack()

    def sb(name, shape, dt=F32):
        return esA.enter_context(nc.sbuf_tensor("a_" + name, list(shape), dt))

    win_sb = sb("win_sb", [128, KC, GIN], BF16)
    wout_sb = sb("wout_sb", [128, KC, D], BF16)
    ogain = sb("ogain", [128, 256])
    wgk2 = sb("wgk2", [16, 512])
    small = sb("small", [128, 96])
    gate_rep = sb("gate_rep", [128, 1024])
    ssx = sb("ssx", [128, 2 * NB])
    rstd = sb("rstd", [128, NB])
    junk = sb("junk", [128, 1024], BF16)
    S = sb("S", [128, 1024])
    S_bf = sb("S_bf", [128, 1024], BF16)
    arena = sb("arena", [128, 4096])
    tmpg = [sb("tmpg%d" % i, [128, 128]) for i in range(2)]
    NBUF = 2
    G2 = [arena[:, 0:1024], sb("G2b", [128, 1024])]
    gA = arena[:, 1024:1536]
    gB = arena[:, 1536:2048]
    ytmp = arena[:, 2048:2560]
    xs = arena[:, 2560:3584]
    stage = [arena[:, 0:2048], arena[:, 2048:4096]]
    hT = [sb("hT%d" % i, [128, KC, 128], BF16) for i in range(NBUF)]
    v_sb = [sb("v_sb%d" % i, [128, 1024], BF16) for i in range(NBUF)]
    gklr = sb("gklr", [16, 128])
    ebuf = sb("ebuf", [128, 512])
    spb = sb("spb", [128, 512])
    csb = sb("csb", [128, 512])
    E1 = sb("E1", [128, 512])
    E2 = sb("E2", [128, 512])
    qe = sb("qe", [128, 512], BF16)
    ke = sb("ke", [128, 512], BF16)
    kteT = sb("kteT", [128, 512], BF16)
    kte = sb("kte", [128, 512], BF16)
    sTm = sb("sTm", [128, 512], BF16)
    hT32 = sb("hT32", [128, KC, 128])
    qe32 = sb("qe32", [128, 512])
    ke32 = sb("ke32", [128, 512])
    w32slot = [arena[:, 2048:3072], arena[:, 3072:4096]]
    oss = sb("oss", [128, 12])
    og = sb("og", [128, 1024], BF16)
    ogT = sb("ogT", [128, KC * 128], BF16)

    r = {n: R.R(n) for n in ["ogain", "wgk2", "small", "gate_rep", "ssx", "rstd", "S", "S_bf",
                             "gA", "gB", "ytmp", "xs", "arena_tail", "gklr", "e", "sp", "cs", "E1", "E2", "qe", "ke", "kteT", "kte",
                             "sTm", "oss", "og", "ogT", "tg0", "tg1", "hT32", "qe32", "ke32"]}
    r["cst"] = rp["cst"]
    r["identb"] = rp["cstb"]
    r["vec"] = rp["vec"]
    w32_res = [[r["ytmp"], r["xs"]], [r["xs"], r["arena_tail"]]]
    r_win = [R.R("win%d" % k) for k in range(KC)]
    r_wout = [R.R("wout%d" % k) for k in range(KC)]
    r_G2 = [R.R("G2_0"), R.R("G2_1")]
    r_hT = [R.R("hT0"), R.R("hT1")]
    r_v = [R.R("v0"), R.R("v1")]
    stage_res = [[r_G2[0], r["gA"], r["gB"]], [r["ytmp"], r["xs"], r["arena_tail"]]]

    maskT4t = sb("maskT4", [128, 512])
    maskT4 = maskT4t[:, :]
    R.add("sp", lambda e: e.dma_start(out=maskT4t[:], in_=cst_d[:, 128:640]), writes=[r["cst"]], dma=True)
    R.add("sp", lambda e: e.dma_start(out=wgk2[:], in_=wgk2_d), writes=[r["wgk2"]], dma=True)
    R.add("sp", lambda e: e.dma_start(out=ogain[:], in_=ogain_d), writes=[r["ogain"]], dma=True)
    winv = win_d.rearrange("(k p) f -> p k f", p=128)
    woutv = wout_d.rearrange("(k p) f -> p k f", p=128)
    for kc in range(KC):
        R.add("pool", lambda e, kc=kc: e.dma_start(out=win_sb[:, kc, :], in_=winv[:, kc, :]),
              writes=[r_win[kc]], dma=True)
    for n in range(0, NB, 2):
        R.add("sp", lambda e, n=n: e.dma_start(out=x_sb[:, n:n + 2, :], in_=xv[:, n:n + 2, :]), writes=[r_x[n], r_x[n + 1]], dma=True)
        if n == 0:
            for kc in range(KC):
                R.add("pool", lambda e, kc=kc: e.dma_start(out=wout_sb[:, kc, :], in_=woutv[:, kc, :]),
                      writes=[r_wout[kc]], dma=True)

    R.add("dve", lambda e: e.memset(ssx[:], 0.0), writes=[r["ssx"]])
    R.add("dve", lambda e: e.memset(S[:], 0.0), writes=[r["S"]])
    R.add("dve", lambda e: e.memset(S_bf[:], 0.0), writes=[r["S_bf"]])
    R.add("dve", lambda e: e.memset(oss[:], 0.0), writes=[r["oss"]])
    silu_small(R, vec[:, 0:8], r["vec"], small[:, 8:24], r["small"], sc[:, 0:8], rp["sc"], 8)
    def sq_hook(kc):
        for n in (2 * kc, 2 * kc + 1):
            R.add("act", lambda e, n=n: e.activation(out=junk[:], in_=x_sb[:, n, :], func=AF.Square, accum_out=ssx[:, n:n + 1]),
                  reads=[r_x[n]], writes=[r["ssx"]])

    ada_matvec(R, wada0_d[:, 0:2 * D], 2 * D, sc, rp["sc"], pp, stage, stage_res, small[:, 24:40], r["small"],
               [tmpg[0][0:1, :], tmpg[1][0:1, :]], [r["tg0"], r["tg1"]], ones128[0:1, 0:1], r["cst"], hook=sq_hook)
    R.add("act", lambda e: e.activation(out=ssx[:, NB:2 * NB], in_=ssx[:, 0:NB], func=AF.Ln, scale=1.0 / D, bias=EPS),
          reads=[r["ssx"]], writes=[r["ssx"]])
    R.add("act", lambda e: e.activation(out=rstd[:, 0:NB], in_=ssx[:, NB:2 * NB], func=AF.Exp, scale=-0.5),
          reads=[r["ssx"]], writes=[r["rstd"]])
    R.add("dve", lambda e: e.tensor_tensor(out=small[:, 48:56], in0=small[:, 24:32], in1=vec[:, 16:24], op=ALU.add),
          reads=[r["small"], r["vec"]], writes=[r["small"]])
    R.add("dve", lambda e: e.scalar_tensor_tensor(out=small[:, 56:64], in0=small[:, 32:40], scalar=1.0, in1=vec[:, 24:32],
                                                  op0=ALU.add, op1=ALU.add),
          reads=[r["small"], r["vec"]], writes=[r["small"]])
    R.add("dve", lambda e: e.tensor_tensor(out=small[:, 56:64], in0=small[:, 56:64], in1=vec[:, 8:16], op=ALU.mult),
          reads=[r["small"], r["vec"]], writes=[r["small"]])
    R.add("dve", lambda e: e.tensor_scalar_mul(out=small[:, 72:76], in0=vec[:, 40:44], scalar1=-1.0),
          reads=[r["vec"]], writes=[r["small"]])
    r["small2"] = R.R("small2")
    small2 = sb("small2", [128, 16])

    def gate_setup():
        ada_matvec(R, wada0_d[:, 2 * D:3 * D], D, sc, rp["sc"], pp, stage, stage_res, small2[:, 0:8], r["small2"],
                   [tmpg[0][0:1, :], tmpg[1][0:1, :]], [r["tg0"], r["tg1"]], ones128[0:1, 0:1], r["cst"])
        R.add("dve", lambda e: e.tensor_tensor(out=small2[:, 8:16], in0=small2[:, 0:8], in1=vec[:, 32:40], op=ALU.add),
              reads=[r["small2"], r["vec"]], writes=[r["small2"]])
        bcast_rows(R, pp, small2[:, 8:16], r["small2"], 8, [t[:] for t in tmpg], [r["tg0"], r["tg1"]], ones128, ident, r["cst"],
                   gate_rep, r["gate_rep"])

    shift0 = small[:, 48:56]
    A0 = small[:, 56:64]
    negbgk = small[:, 72:76]

    def stage1_xs(n):
        R.add("dve", lambda e: e.tensor_scalar_mul(out=xs, in0=x_sb[:, n, :], scalar1=rstd[:, n:n + 1]),
              reads=[r_x[n], r["rstd"]], writes=[r["xs"]])

    def stage1(n):
        b = n % NBUF
        for half in range(2):
            bk, bk_r = pp.get()
            for q in range(4):
                kc = 4 * half + q
                R.add("pe", lambda e, bk=bk, q=q, kc=kc: e.matmul(bk[:, 128 * q:128 * q + 128],
                                                                    lhsT=xs[:, 128 * kc:128 * kc + 128], rhs=ident,
                                                                    start=True, stop=True),
                      reads=[r["xs"], r["cst"]], writes=[bk_r])
            for q in range(4):
                kc = 4 * half + q
                R.add("act", lambda e, bk=bk, q=q, kc=kc: e.activation(out=hT[b][:, kc, :], in_=bk[:, 128 * q:128 * q + 128],
                                                                        func=AF.Identity, scale=A0[:, kc:kc + 1],
                                                                        bias=shift0[:, kc:kc + 1]),
                      reads=[bk_r, r["small"]], writes=[r_hT[b]])
                if n == 0:
                    R.add("act", lambda e, bk=bk, q=q, kc=kc: e.activation(out=hT32[:, kc, :], in_=bk[:, 128 * q:128 * q + 128],
                                                                            func=AF.Identity, scale=A0[:, kc:kc + 1],
                                                                            bias=shift0[:, kc:kc + 1]),
                          reads=[bk_r, r["small"]], writes=[r["hT32"]])
            pp.free(bk, bk_r)

    def proj_qk(n):
        b = n % NBUF
        st = {}
        for name, c0 in (("q", 0), ("k", 512)):
            bk, bk_r = pp.get()
            for hh in range(4):
                if n == 0:
                    si = (hh + (0 if name == "q" else 4)) % 2
                    ws = w32slot[si].rearrange("p (k f) -> p k f", k=KC)
                    R.add("sp", lambda e, ws=ws, hh=hh, c0=c0: e.dma_start(out=ws, in_=winv[:, :, c0 + 128 * hh:c0 + 128 * hh + 128]),
                          writes=w32_res[si], dma=True)
                    for kc in range(KC):
                        R.add("pe", lambda e, bk=bk, hh=hh, kc=kc, ws=ws: e.matmul(
                            bk[:, 128 * hh:128 * hh + 128], lhsT=ws[:, kc, :], rhs=hT32[:, kc, :],
                            start=(kc == 0), stop=(kc == KC - 1)),
                            reads=w32_res[si] + [r["hT32"]], writes=[bk_r])
                    continue
                for kc in range(KC):
                    R.add("pe", lambda e, bk=bk, hh=hh, kc=kc, c0=c0: e.matmul(
                        bk[:, 128 * hh:128 * hh + 128], lhsT=win_sb[:, kc, c0 + 128 * hh:c0 + 128 * hh + 128],
                        rhs=hT[b][:, kc, :], start=(kc == 0), stop=(kc == KC - 1)),
                        reads=[r_win[kc], r_hT[b]], writes=[bk_r])
            st[name] = (bk, bk_r)
        bk, bk_r = pp.get()
        for kc in range(KC):
            R.add("pe", lambda e, bk=bk, kc=kc: e.matmul(bk[0:16, 0:128], lhsT=win_sb[:, kc, 3072:3088], rhs=hT[b][:, kc, :],
                                                         start=(kc == 0), stop=(kc == KC - 1)),
                  reads=[r_win[kc], r_hT[b]], writes=[bk_r])
        R.add("act", lambda e, bk=bk: e.activation(out=gklr[:], in_=bk[0:16, 0:128], func=AF.Copy),
              reads=[bk_r], writes=[r["gklr"]])
        pp.free(bk, bk_r)
        return st

    def proj_vg(n, which):
        b = n % NBUF
        for name, c0 in ((which, 1024 if which == "v" else 2048),):
            lst = []
            for half in range(2):
                bk, bk_r = pp.get()
                for kc in range(KC):
                    R.add("pe", lambda e, bk=bk, kc=kc, c0=c0, half=half: e.matmul(
                        bk[:, :], lhsT=hT[b][:, kc, :], rhs=win_sb[:, kc, c0 + 512 * half:c0 + 512 * half + 512],
                        start=(kc == 0), stop=(kc == KC - 1)),
                        reads=[r_win[kc], r_hT[b]], writes=[bk_r])
                lst.append((bk, bk_r))
                if name == "v":
                    R.add("act", lambda e, bk=bk, half=half: e.activation(out=v_sb[b][:, 512 * half:512 * half + 512], in_=bk[:, :],
                                                                           func=AF.Copy),
                          reads=[bk_r], writes=[r_v[b]])
                else:
                    sl = slice(512 * half, 512 * half + 512)
                    R.add("act", lambda e, bk=bk, sl=sl: e.activation(out=gA, in_=bk[:, :], func=AF.Exp, scale=-1.0),
                          reads=[bk_r], writes=[r["gA"]])
                    R.add("act", lambda e, sl=sl: e.activation(out=gB, in_=gA, func=AF.Ln, bias=1.0),
                          reads=[r["gA"]], writes=[r["gB"]])
                    R.add("act", lambda e, sl=sl: e.activation(out=gA, in_=gB, func=AF.Exp, scale=-1.0),
                          reads=[r["gB"]], writes=[r["gA"]])
                    R.add("dve", lambda e, bk=bk, sl=sl: e.tensor_tensor(out=gB, in0=bk[:, :], in1=gA, op=ALU.mult),
                          reads=[bk_r, r["gA"]], writes=[r["gB"]])
                    for q2 in range(2):
                        c1 = 512 * half + 256 * q2
                        R.add("dve", lambda e, c1=c1, q2=q2: e.tensor_tensor(out=G2[b][:, c1:c1 + 256], in0=gB[:, 256 * q2:256 * q2 + 256],
                                                                              in1=ogain[:, 0:256], op=ALU.mult),
                              reads=[r["gB"], r["ogain"]], writes=[r_G2[b]])
                pp.free(bk, bk_r)


    def gates(n, st):
        b = n % NBUF
        qb, qb_r = st["q"]
        kb, kb_r = st["k"]
        bk, bk_r = pp.get()
        for hh in range(4):
            R.add("pe", lambda e, hh=hh: e.matmul(bk[:, 128 * hh:128 * hh + 128], lhsT=wgk2[0:16, 128 * hh:128 * hh + 128],
                                                  rhs=gklr[:], start=True, stop=True),
                  reads=[r["wgk2"], r["gklr"]], writes=[bk_r])
        for hh in range(4):
            R.add("act", lambda e, hh=hh: e.activation(out=ebuf[:, 128 * hh:128 * hh + 128], in_=bk[:, 128 * hh:128 * hh + 128],
                                                       func=AF.Exp, scale=-1.0, bias=negbgk[:, hh:hh + 1]),
                  reads=[bk_r, r["small"]], writes=[r["e"]])
        pp.free(bk, bk_r)
        R.add("act", lambda e: e.activation(out=spb[:], in_=ebuf[:], func=AF.Ln, bias=1.0),
              reads=[r["e"]], writes=[r["sp"]])
        for hh in range(4):
            sl = slice(128 * hh, 128 * hh + 128)
            R.add("dve", lambda e, sl=sl: e.tensor_tensor_scan(out=csb[:, sl], data0=spb[:, sl], data1=spb[:, sl], initial=0.0,
                                                              op0=ALU.add, op1=ALU.max),
                  reads=[r["sp"]], writes=[r["cs"]])
        R.add("act", lambda e: e.activation(out=E1[:], in_=csb[:], func=AF.Exp, scale=-1.0 / 16),
              reads=[r["cs"]], writes=[r["E1"]])
        R.add("act", lambda e: e.activation(out=E2[:], in_=csb[:], func=AF.Exp, scale=1.0 / 16),
              reads=[r["cs"]], writes=[r["E2"]])
        R.add("dve", lambda e: e.scalar_tensor_tensor(out=qe[:], in0=qb[:, :], scalar=128 ** -0.5, in1=E1[:],
                                                      op0=ALU.mult, op1=ALU.mult),
              reads=[qb_r, r["E1"]], writes=[r["qe"]])
        R.add("dve", lambda e: e.tensor_tensor(out=ke[:], in0=kb[:, :], in1=E2[:], op=ALU.mult),
              reads=[kb_r, r["E2"]], writes=[r["ke"]])
        if n == 0:
            R.add("dve", lambda e: e.scalar_tensor_tensor(out=qe32[:], in0=qb[:, :], scalar=128 ** -0.5, in1=E1[:],
                                                          op0=ALU.mult, op1=ALU.mult),
                  reads=[qb_r, r["E1"]], writes=[r["qe32"]])
            R.add("dve", lambda e: e.tensor_tensor(out=ke32[:], in0=kb[:, :], in1=E2[:], op=ALU.mult),
                  reads=[kb_r, r["E2"]], writes=[r["ke32"]])
        for hh in range(4):
            sl = slice(128 * hh, 128 * hh + 128)
            R.add("dve", lambda e, sl=sl, hh=hh: e.scalar_tensor_tensor(
                out=kteT[:, sl], in0=kb[:, sl], scalar=E1[:, 128 * hh + 127:128 * hh + 128], in1=E2[:, sl],
                op0=ALU.mult, op1=ALU.mult),
                reads=[kb_r, r["E1"], r["E2"]], writes=[r["kteT"]])
        pp.free(qb, qb_r)
        pp.free(kb, kb_r)

    def attn_a(n):
        b = n % NBUF
        sb_, sb_r = pp.get()
        for hh in range(4):
            sl = slice(128 * hh, 128 * hh + 128)
            if n == 0:
                R.add("pe", lambda e, sl=sl: e.matmul(sb_[:, sl], lhsT=ke32[:, sl], rhs=qe32[:, sl], start=True, stop=True),
                      reads=[r["ke32"], r["qe32"]], writes=[sb_r])
            else:
                R.add("pe", lambda e, sl=sl: e.matmul(sb_[:, sl], lhsT=ke[:, sl], rhs=qe[:, sl], start=True, stop=True),
                      reads=[r["ke"], r["qe"]], writes=[sb_r])
        R.add("dve", lambda e: e.tensor_tensor(out=sTm[:], in0=sb_[:, :], in1=maskT4, op=ALU.mult),
              reads=[sb_r, r["cst"]], writes=[r["sTm"]])
        pp.free(sb_, sb_r)
        tb, tb_r = pp.get()
        for hh in range(4):
            sl = slice(128 * hh, 128 * hh + 128)
            R.add("pe", lambda e, sl=sl: e.matmul(tb[:, sl], lhsT=kteT[:, sl], rhs=identb, start=True, stop=True),
                  reads=[r["kteT"], r["identb"]], writes=[tb_r])
        R.add("act", lambda e: e.activation(out=kte[:], in_=tb[:, :], func=AF.Copy),
              reads=[tb_r], writes=[r["kte"]])
        pp.free(tb, tb_r)

    def attn_b(n):
        b = n % NBUF
        obanks = []
        for half in range(2):
            ob, ob_r = pp.get()
            for q in range(2):
                hh = 2 * half + q
                sl = slice(128 * hh, 128 * hh + 128)
                vs = slice(256 * hh, 256 * hh + 256)
                R.add("pe", lambda e, ob=ob, q=q, sl=sl, vs=vs: e.matmul(ob[:, 256 * q:256 * q + 256], lhsT=sTm[:, sl],
                                                                          rhs=v_sb[b][:, vs], start=True, stop=False),
                      reads=[r["sTm"], r_v[b]], writes=[ob_r])
                R.add("pe", lambda e, ob=ob, q=q, sl=sl, vs=vs: e.matmul(ob[:, 256 * q:256 * q + 256], lhsT=qe[:, sl],
                                                                          rhs=S_bf[:, vs], start=False, stop=True),
                      reads=[r["qe"], r["S_bf"]], writes=[ob_r])
            obanks.append((ob, ob_r))
        ibanks = []
        for half in range(2):
            ib, ib_r = pp.get()
            for q in range(2):
                hh = 2 * half + q
                sl = slice(128 * hh, 128 * hh + 128)
                vs = slice(256 * hh, 256 * hh + 256)
                R.add("pe", lambda e, ib=ib, q=q, sl=sl, vs=vs: e.matmul(ib[:, 256 * q:256 * q + 256], lhsT=kte[:, sl],
                                                                          rhs=v_sb[b][:, vs], start=True, stop=True),
                      reads=[r["kte"], r_v[b]], writes=[ib_r])
            ibanks.append((ib, ib_r))
        for hh in range(4):
            ob, ob_r = obanks[hh // 2]
            q = hh % 2
            R.add("act", lambda e, ob=ob, q=q, hh=hh: e.activation(out=junk[:, 256 * hh:256 * hh + 256], in_=ob[:, 256 * q:256 * q + 256],
                                                                    func=AF.Square, accum_out=oss[:, hh:hh + 1]),
                  reads=[ob_r], writes=[r["oss"]])
        R.add("act", lambda e: e.activation(out=oss[:, 4:8], in_=oss[:, 0:4], func=AF.Ln, scale=1.0 / 256, bias=EPS),
              reads=[r["oss"]], writes=[r["oss"]])
        R.add("act", lambda e: e.activation(out=oss[:, 8:12], in_=oss[:, 4:8], func=AF.Exp, scale=-0.5),
              reads=[r["oss"]], writes=[r["oss"]])
        for hh in range(4):
            ob, ob_r = obanks[hh // 2]
            q = hh % 2
            vs = slice(256 * hh, 256 * hh + 256)
            R.add("dve", lambda e, ob=ob, q=q, hh=hh, vs=vs: e.scalar_tensor_tensor(
                out=og[:, vs], in0=ob[:, 256 * q:256 * q + 256], scalar=oss[:, 8 + hh:9 + hh], in1=G2[b][:, vs],
                op0=ALU.mult, op1=ALU.mult),
                reads=[ob_r, r["oss"], r_G2[b]], writes=[r["og"]])
        R.add("dve", lambda e: e.memset(oss[:, 0:4], 0.0), reads=[], writes=[r["oss"]])
        for ob, ob_r in obanks:
            pp.free(ob, ob_r)
        for hh in range(4):
            ib, ib_r = ibanks[hh // 2]
            q = hh % 2
            vs = slice(256 * hh, 256 * hh + 256)
            R.add("dve", lambda e, ib=ib, q=q, hh=hh, vs=vs: e.scalar_tensor_tensor(
                out=S[:, vs], in0=S[:, vs], scalar=E1[:, 128 * hh + 127:128 * hh + 128], in1=ib[:, 256 * q:256 * q + 256],
                op0=ALU.mult, op1=ALU.add),
                reads=[ib_r, r["E1"], r["S"]], writes=[r["S"]])
        R.add("act", lambda e: e.activation(out=S_bf[:], in_=S[:], func=AF.Copy), reads=[r["S"]], writes=[r["S_bf"]])
        for ib, ib_r in ibanks:
            pp.free(ib, ib_r)

    def outp_a(n):
        b = n % NBUF
        for half in range(2):
            tb, tb_r = pp.get()
            for q in range(4):
                fc = 4 * half + q
                R.add("pe", lambda e, tb=tb, q=q, fc=fc: e.matmul(tb[:, 128 * q:128 * q + 128], lhsT=og[:, 128 * fc:128 * fc + 128],
                                                                    rhs=identb, start=True, stop=True),
                      reads=[r["og"], r["identb"]], writes=[tb_r])
            R.add("act", lambda e, tb=tb, half=half: e.activation(out=ogT[:, 512 * half:512 * half + 512], in_=tb[:, :], func=AF.Copy),
                  reads=[tb_r], writes=[r["ogT"]])
            pp.free(tb, tb_r)

    def outp_b(n):
        b = n % NBUF
        for half in range(2):
            yb, yb_r = pp.get()
            sl = slice(512 * half, 512 * half + 512)
            for fc in range(KC):
                R.add("pe", lambda e, yb=yb, fc=fc, sl=sl: e.matmul(yb[:, :], lhsT=ogT[:, 128 * fc:128 * fc + 128], rhs=wout_sb[:, fc, sl],
                                                                     start=(fc == 0), stop=(fc == KC - 1)),
                      reads=[r["ogT"], r_wout[fc]], writes=[yb_r])
            R.add("dve", lambda e, yb=yb, sl=sl: e.tensor_tensor(out=ytmp, in0=yb[:, :], in1=gate_rep[:, sl], op=ALU.mult),
                  reads=[yb_r, r["gate_rep"]], writes=[r["ytmp"]])
            pp.free(yb, yb_r)
            R.add("dve", lambda e, sl=sl: e.tensor_tensor(out=x_sb[:, n, sl], in0=x_sb[:, n, sl], in1=ytmp, op=ALU.add),
                  reads=[r_x[n], r["ytmp"]], writes=[r_x[n]])

    sts = {}
    stage1_xs(0)
    stage1(0)
    gate_setup()
    sts[0] = proj_qk(0)
    proj_vg(0, "v")
    proj_vg(0, "g")
    stage1_xs(1)
    stage1(1)
    for n in range(NB):
        nxt = n + 1 < NB
        if n + 2 < NB:
            stage1_xs(n + 2)
        gates(n, sts[n])
        if nxt:
            sts[n + 1] = proj_qk(n + 1)
        if n + 2 < NB:
            stage1(n + 2)
        attn_a(n)
        if nxt:
            proj_vg(n + 1, "v")
        attn_b(n)
        outp_a(n)
        if nxt:
            proj_vg(n + 1, "g")
        outp_b(n)

    R.emit_phase()
    esA.close()

    esBC = ExitStack()
    kT_all = esBC.enter_context(nc.sbuf_tensor("kT_all", [128, 8, T], BF16))
    v_all = esBC.enter_context(nc.sbuf_tensor("v_all", [128, NB, D], BF16))
    r_kT = [R.R("kT%d" % g) for g in range(4)]
    r_vb = [R.R("vb%d" % j) for j in range(NB)]
    esB = ExitStack()

    def sb(name, shape, dt=F32):
        return esB.enter_context(nc.sbuf_tensor("b_" + name, list(shape), dt))

    wkv_sb = sb("wkv_sb", [128, KC, 2 * D], BF16)
    Jt = sb("Jt", [128, 128])
    Jm = Jt[:, :]
    small = sb("small", [128, 96])
    ssx = sb("ssx", [128, 2 * NB])
    junk = sb("junk", [128, 1024], BF16)
    arena = sb("arena", [128, 4096])
    stage = [arena[:, 0:2048], arena[:, 2048:4096]]
    xs = [arena[:, 0:1024], arena[:, 1024:2048]]
    hkT = [sb("hkT%d" % i, [128, KC, 512], BF16) for i in range(2)]
    rstd = rstd1
    r = {n: R.R(n) for n in ["small", "ssx", "xs0", "xs1", "hkT0", "hkT1", "st1"]}
    r["cst"] = rp["cst"]
    r["vec"] = rp["vec"]
    r["rstd"] = rp["rstd1"]
    r["J"] = R.R("J")
    R.add("sp", lambda e: e.dma_start(out=Jt[:], in_=cst_d[:, 768:896]), writes=[r["J"]], dma=True)
    r_wkv = [R.R("wkv%d" % k) for k in range(KC)]
    r_v = r_vb
    r_xs = [r["xs0"], r["xs1"]]
    r_hkT = [r["hkT0"], r["hkT1"]]
    stage_res = [[r["xs0"], r["xs1"]], [r["st1"]]]
    wkvv = wkv_d.rearrange("(k p) f -> p k f", p=128)
    r_wkvV = [R.R("wkvV%d" % k) for k in range(KC)]
    for kc in range(KC):
        R.add("pool", lambda e, kc=kc: e.dma_start(out=wkv_sb[:, kc, 0:D], in_=wkvv[:, kc, 0:D]), writes=[r_wkv[kc]], dma=True)
    for kc in range(KC):
        R.add("pool", lambda e, kc=kc: e.dma_start(out=wkv_sb[:, kc, D:2 * D], in_=wkvv[:, kc, D:2 * D]), writes=[r_wkvV[kc]], dma=True)
    R.add("dve", lambda e: e.memset(ssx[:], 0.0), writes=[r["ssx"]])
    rowt = sb("rowt", [1, 256])
    r["rt0"] = R.R("rt0")
    r["rt1"] = R.R("rt1")
    ada_matvec(R, wadak_d, 2 * D, sc, rp["sc"], pp, stage, stage_res, small[:, 24:40], r["small"],
               [rowt[0:1, 0:128], rowt[0:1, 128:256]], [r["rt0"], r["rt1"]], ones128[0:1, 0:1], r["cst"], q="sp")
    stg1 = sb("stg1", [128, 2048])
    r["s1a"] = R.R("s1a")
    r["s1b"] = R.R("s1b")
    ada_matvec(R, wada1_d, 3 * D, sc, rp["sc"], pp, [stg1[:, 0:1024], stg1[:, 1024:2048]], [[r["s1a"]], [r["s1b"]]],
               cond1[:, 0:24], rp["cond1"], [rowt[0:1, 0:128], rowt[0:1, 128:256]], [r["rt0"], r["rt1"]],
               ones128[0:1, 0:1], r["cst"], q="sp", maxw=1024)
    R.add("dve", lambda e: e.tensor_tensor(out=small[:, 40:48], in0=small[:, 24:32], in1=vec[:, 52:60], op=ALU.add),
          reads=[r["small"], r["vec"]], writes=[r["small"]])
    R.add("dve", lambda e: e.scalar_tensor_tensor(out=small[:, 48:56], in0=small[:, 32:40], scalar=1.0, in1=vec[:, 60:68],
                                                  op0=ALU.add, op1=ALU.add),
          reads=[r["small"], r["vec"]], writes=[r["small"]])
    R.add("dve", lambda e: e.tensor_tensor(out=small[:, 48:56], in0=small[:, 48:56], in1=vec[:, 44:52], op=ALU.mult),
          reads=[r["small"], r["vec"]], writes=[r["small"]])
    shiftk = small[:, 40:48]
    Ak = small[:, 48:56]

    def norm_block(n, slot):
        b = n % 2
        g = (NB - 1 - n) // 4
        hb = g % 2
        R.add("act", lambda e: e.activation(out=junk[:], in_=x_sb[:, n, :], func=AF.Square, accum_out=ssx[:, n:n + 1]),
              reads=[r_x[n]], writes=[r["ssx"]])
        R.add("act", lambda e: e.activation(out=ssx[:, NB + n:NB + n + 1], in_=ssx[:, n:n + 1], func=AF.Ln, scale=1.0 / D, bias=EPS),
              reads=[r["ssx"]], writes=[r["ssx"]])
        R.add("act", lambda e: e.activation(out=rstd[:, n:n + 1], in_=ssx[:, NB + n:NB + n + 1], func=AF.Exp, scale=-0.5),
              reads=[r["ssx"]], writes=[r["rstd"]])
        R.add("dve", lambda e: e.tensor_scalar_mul(out=xs[b], in0=x_sb[:, n, :], scalar1=rstd[:, n:n + 1]),
              reads=[r_x[n], r["rstd"]], writes=[r_xs[b]])
        for half in range(2):
            bk, bk_r = pp.get()
            for q in range(4):
                kc = 4 * half + q
                R.add("pe", lambda e, bk=bk, q=q, kc=kc: e.matmul(bk[:, 128 * q:128 * q + 128],
                                                                    lhsT=xs[b][:, 128 * kc:128 * kc + 128], rhs=Jm,
                                                                    start=True, stop=True),
                      reads=[r_xs[b], r["J"]], writes=[bk_r])
            for q in range(4):
                kc = 4 * half + q
                R.add("act", lambda e, bk=bk, q=q, kc=kc: e.activation(
                    out=hkT[hb][:, kc, 128 * slot:128 * slot + 128], in_=bk[:, 128 * q:128 * q + 128],
                    func=AF.Identity, scale=Ak[:, kc:kc + 1], bias=shiftk[:, kc:kc + 1]),
                    reads=[bk_r, r["small"]], writes=[r_hkT[hb]])
            pp.free(bk, bk_r)

    def kv_group(g):
        hb = g % 2
        for hp in range(8):
            bk, bk_r = pp.get()
            for kc in range(KC):
                R.add("pe", lambda e, bk=bk, hp=hp, kc=kc: e.matmul(bk[:, :], lhsT=wkv_sb[:, kc, 128 * hp:128 * hp + 128],
                                                                     rhs=hkT[hb][:, kc, :], start=(kc == 0), stop=(kc == KC - 1)),
                      reads=[r_wkv[kc], r_hkT[hb]], writes=[bk_r])
            eng = "act" if hp % 2 == 0 else "dve"
            if eng == "act":
                R.add("act", lambda e, bk=bk, hp=hp: e.activation(out=kT_all[:, hp, 512 * g:512 * g + 512], in_=bk[:, :], func=AF.Copy),
                      reads=[bk_r], writes=[r_kT[g]])
            else:
                R.add("dve", lambda e, bk=bk, hp=hp: e.tensor_copy(out=kT_all[:, hp, 512 * g:512 * g + 512], in_=bk[:, :]),
                      reads=[bk_r], writes=[r_kT[g]])
            pp.free(bk, bk_r)
        for j in range(4):
            jb = 4 * g + j
            for half in range(2):
                bk, bk_r = pp.get()
                for kc in range(KC):
                    R.add("pe", lambda e, bk=bk, kc=kc, j=j, half=half: e.matmul(
                        bk[:, :], lhsT=hkT[hb][:, kc, 128 * j:128 * j + 128],
                        rhs=wkv_sb[:, kc, 1024 + 512 * half:1024 + 512 * half + 512], start=(kc == 0), stop=(kc == KC - 1)),
                        reads=[r_wkvV[kc], r_hkT[hb]], writes=[bk_r])
                if half == 0:
                    R.add("act", lambda e, bk=bk, jb=jb, half=half: e.activation(out=v_all[:, jb, 512 * half:512 * half + 512], in_=bk[:, :],
                                                                                  func=AF.Copy),
                          reads=[bk_r], writes=[r_v[jb]])
                else:
                    R.add("dve", lambda e, bk=bk, jb=jb, half=half: e.tensor_copy(out=v_all[:, jb, 512 * half:512 * half + 512], in_=bk[:, :]),
                          reads=[bk_r], writes=[r_v[jb]])
                pp.free(bk, bk_r)

    for g in range(4):
        for slot in range(4):
            n = NB - 1 - (4 * g + slot)
            norm_block(n, slot)
        kv_group(g)

    R.emit_phase()
    esB.close()

    esC = ExitStack()

    def sb(name, shape, dt=F32):
        return esC.enter_context(nc.sbuf_tensor("c_" + name, list(shape), dt))

    kT = kT_all
    v_sb = v_all
    r_v = [R.R("vg%d" % g) for g in range(4)]
    win_sb = sb("win_sb", [128, KC, 2 * D], BF16)
    wout_sb = sb("wout_sb", [128, KC, D], BF16)
    small = sb("small", [128, 96])
    ssx = sb("ssx", [128, 4 * NB])
    NPIPE = 4
    arena = sb("arena", [128, 4352])
    stage = [arena[:, 0:2048], arena[:, 2048:4096]]
    sg = arena[:, 0:1024]
    xs = arena[:, 1024:2048]
    abuf = [arena[:, 2048:2560], arena[:, 2560:3072]]
    h1og = sb("h1og", [128, D], BF16)
    h1T = h1og.rearrange("p (k t) -> p k t", k=KC)
    og = h1og
    qTp = [sb("qT%d" % i, [128, KC * 128], BF16) for i in range(2)]
    wn2 = sb("wn2", [128, 1024], BF16)
    ogT = wn2
    pbv = arena[:, 3072:4352].bitcast(BF16)
    Pb = [pbv[:, 516 * i:516 * i + 516] for i in range(NPIPE)]
    wT = [sb("wT%d" % i, [128, 512], BF16) for i in range(2)]

    r = {n: R.R(n) for n in ["small", "ssx", "sg", "xs", "ab0", "ab1", "ab2", "h1T", "qT", "wn0", "wn1", "wn2", "wT0", "wT1", "wT2", "P0", "P1", "P2", "rt", "rt2"]}
    r["cst"] = rp["cst"]
    r["cstb"] = rp["cstb"]
    r["vec"] = rp["vec"]
    r["rstd1"] = rp["rstd1"]
    r_win = [R.R("cwin%d" % k) for k in range(KC)]
    r_wout = [R.R("cwout%d" % k) for k in range(KC)]
    r_ab = [r["ab0"], r["ab1"]]
    r_wT = [r["wT0"], r["wT1"]]
    r_P = [r["P0"], r["P1"], r["P2"], r["rt"]]
    stage_res = [[r["sg"], r["xs"]], [r["ab0"], r["ab1"], r["P0"], r["P1"], r["P2"]]]

    winv = sbwin_d.rearrange("(k p) f -> p k f", p=128)
    woutv = sbwout_d.rearrange("(k p) f -> p k f", p=128)
    r_winG = [R.R("cwinG%d" % k) for k in range(KC)]
    for kc in range(KC):
        R.add("pool", lambda e, kc=kc: e.dma_start(out=win_sb[:, kc, 0:D], in_=winv[:, kc, 0:D]), writes=[r_win[kc]], dma=True)
    for kc in range(KC):
        R.add("pool", lambda e, kc=kc: e.dma_start(out=win_sb[:, kc, D:2 * D], in_=winv[:, kc, D:2 * D]), writes=[r_winG[kc]], dma=True)
    R.add("dve", lambda e: e.memset(ssx[:], 0.0), writes=[r["ssx"]])
    R.add("dve", lambda e: e.memset(qTp[0][64:128, :], 0.0), writes=[r["qT"]])
    R.add("dve", lambda e: e.memset(qTp[1][0:64, :], 0.0), writes=[r["qT"]])
    R.add("dve", lambda e: e.tensor_copy(out=small[:, 24:48], in_=cond1[:, 0:24]), reads=[rp["cond1"]], writes=[r["small"]])
    R.add("dve", lambda e: e.tensor_tensor(out=small[:, 48:56], in0=small[:, 24:32], in1=vec[:, 76:84], op=ALU.add),
          reads=[r["small"], r["vec"]], writes=[r["small"]])
    R.add("dve", lambda e: e.scalar_tensor_tensor(out=small[:, 56:64], in0=small[:, 32:40], scalar=1.0, in1=vec[:, 84:92],
                                                  op0=ALU.add, op1=ALU.add),
          reads=[r["small"], r["vec"]], writes=[r["small"]])
    R.add("dve", lambda e: e.tensor_tensor(out=small[:, 56:64], in0=small[:, 56:64], in1=vec[:, 68:76], op=ALU.mult),
          reads=[r["small"], r["vec"]], writes=[r["small"]])
    R.add("dve", lambda e: e.tensor_tensor(out=small[:, 64:72], in0=small[:, 40:48], in1=vec[:, 92:100], op=ALU.add),
          reads=[r["small"], r["vec"]], writes=[r["small"]])
    tg = [abuf[0][:, 0:128], abuf[1][:, 0:128]]
    bcast_rows(R, pp, small[:, 64:72], r["small"], 8, tg, [r["ab0"], r["ab1"]], ones128, ident, r["cst"], xs, r["xs"])
    wst = [sg, arena[:, 2048:3072]]
    wst_res = [[r["sg"]], [r["ab0"], r["ab1"]]]
    for kc in range(KC):
        si = kc % 2
        R.add("sp", lambda e, kc=kc, si=si: e.dma_start(out=wst[si], in_=woutv[:, kc, :]), writes=wst_res[si], dma=True)
        eng = "dve" if kc % 2 == 0 else "pool"
        R.add(eng, lambda e, kc=kc, si=si: e.tensor_tensor(out=wout_sb[:, kc, :], in0=wst[si], in1=xs, op=ALU.mult),
              reads=wst_res[si] + [r["xs"]], writes=[r_wout[kc]])
    shift1 = small[:, 48:56]
    A1 = small[:, 56:64]

    def prologue(i):
        R.add("dve", lambda e: e.tensor_scalar_mul(out=xs, in0=x_sb[:, i, :], scalar1=rstd1[:, i:i + 1]),
              reads=[r_x[i], r["rstd1"]], writes=[r["xs"]])
        for half in range(2):
            bk, bk_r = pp.get()
            for q in range(4):
                kc = 4 * half + q
                R.add("pe", lambda e, bk=bk, q=q, kc=kc: e.matmul(bk[:, 128 * q:128 * q + 128], lhsT=xs[:, 128 * kc:128 * kc + 128],
                                                                    rhs=ident, start=True, stop=True),
                      reads=[r["xs"], r["cst"]], writes=[bk_r])
            for q in range(4):
                kc = 4 * half + q
                R.add("act", lambda e, bk=bk, q=q, kc=kc: e.activation(out=h1T[:, kc, :], in_=bk[:, 128 * q:128 * q + 128],
                                                                        func=AF.Identity, scale=A1[:, kc:kc + 1], bias=shift1[:, kc:kc + 1]),
                      reads=[bk_r, r["small"]], writes=[r["h1T"]])
            pp.free(bk, bk_r)

    def prologue_b(i):
        for half in range(2):
            bk, bk_r = pp.get()
            for q in range(4):
                hp = 4 * half + q
                for kc in range(KC):
                    R.add("pe", lambda e, bk=bk, q=q, hp=hp, kc=kc: e.matmul(bk[:, 128 * q:128 * q + 128],
                                                                              lhsT=win_sb[:, kc, 128 * hp:128 * hp + 128],
                                                                              rhs=h1T[:, kc, :], start=(kc == 0), stop=(kc == KC - 1)),
                          reads=[r_win[kc], r["h1T"]], writes=[bk_r])
            R.add("dve", lambda e, bk=bk, half=half: e.tensor_copy(out=qTp[0][0:64, 512 * half:512 * half + 512], in_=bk[0:64, :]),
                  reads=[bk_r], writes=[r["qT"]])
            R.add("dve", lambda e, bk=bk, half=half: e.tensor_copy(out=qTp[1][64:128, 512 * half:512 * half + 512], in_=bk[64:128, :]),
                  reads=[bk_r], writes=[r["qT"]])
            pp.free(bk, bk_r)
        for half in range(2):
            bk, bk_r = pp.get()
            sl = slice(512 * half, 512 * half + 512)
            for kc in range(KC):
                R.add("pe", lambda e, bk=bk, kc=kc, half=half: e.matmul(bk[:, :], lhsT=h1T[:, kc, :],
                                                                         rhs=win_sb[:, kc, 1024 + 512 * half:1024 + 512 * half + 512],
                                                                         start=(kc == 0), stop=(kc == KC - 1)),
                      reads=[r_winG[kc], r["h1T"]], writes=[bk_r])
            R.add("act", lambda e, bk=bk, sl=sl: e.activation(out=sg[:, sl], in_=bk[:, :], func=AF.Sigmoid),
                  reads=[bk_r], writes=[r["sg"]])
            R.add("dve", lambda e, bk=bk, sl=sl: e.tensor_tensor(out=sg[:, sl], in0=bk[:, :], in1=sg[:, sl], op=ALU.mult),
                  reads=[bk_r, r["sg"]], writes=[r["sg"]])
            pp.free(bk, bk_r)

    def attention(i):
        L = 128 * (i + 1)
        base = T - L
        nseg = (L + 511) // 512
        obanks = [pp.get(), pp.get()]
        units = [(h, k) for hp in range(8) for k in range(nseg) for h in (hp, hp + 8)]
        state = {}

        def qk(u, idx):
            h, k = u
            c0 = base + 512 * k
            w = min(512, L - 512 * k)
            zb, zb_r = pp.get()
            ps = slice(64 * (h % 2), 64 * (h % 2) + 64)
            hp = h // 2
            gset = sorted(set([(c0) // 512, (c0 + w - 1) // 512]))
            R.add("pe", lambda e: e.matmul(zb[:, 0:w], lhsT=qTp[h % 2][:, 128 * hp:128 * hp + 128], rhs=kT[:, hp, c0:c0 + w],
                                           start=True, stop=(k != 0)),
                  reads=[r["qT"]] + [r_kT[g] for g in gset], writes=[zb_r])
            if k == 0:
                R.add("pe", lambda e: e.matmul(zb[:, 0:128], lhsT=identb, rhs=maskb, start=False, stop=True),
                      reads=[r["cstb"]], writes=[zb_r])
            a = idx % 2
            R.add("act", lambda e: e.activation(out=abuf[a][:, 0:w], in_=zb[:, 0:w], func=AF.Sigmoid, scale=-0.125),
                  reads=[zb_r], writes=[r_ab[a]])
            pp.free(zb, zb_r)
            pb = idx % NPIPE
            if k > 0:
                pprev = (idx - 2) % NPIPE
                R.add("pool", lambda e: e.tensor_copy(out=Pb[pb][:, 0:1], in_=Pb[pprev][:, 512:513]),
                      reads=[r_P[pprev]], writes=[r_P[pb]])
            else:
                R.add("pool", lambda e: e.memset(Pb[pb][:, 0:1], 1.0), writes=[r_P[pb]])
            R.add("dve", lambda e: e.tensor_tensor_scan(out=Pb[pb][:, 1:1 + w], data0=abuf[a][:, 0:w], data1=abuf[a][:, 0:w],
                                                        initial=Pb[pb][:, 0:1], op0=ALU.mult, op1=ALU.min),
                  reads=[r_ab[a], r_P[pb]], writes=[r_P[pb]])
            state[idx] = (h, k, c0, w, a)

        def tr(idx):
            h, k, c0, w, a = state[idx]
            tb, tb_r = pp.get()
            for jb in range(w // 128):
                R.add("pe", lambda e, jb=jb: e.matmul(tb[:, 128 * jb:128 * jb + 128], lhsT=Pb[idx % NPIPE][:, 1 + 128 * jb:129 + 128 * jb], rhs=identb,
                                                      start=True, stop=False),
                      reads=[r_P[idx % NPIPE], r["cstb"]], writes=[tb_r])
                R.add("pe", lambda e, jb=jb: e.matmul(tb[:, 128 * jb:128 * jb + 128], lhsT=Pb[idx % NPIPE][:, 128 * jb:128 + 128 * jb], rhs=negidentb,
                                                      start=False, stop=True),
                      reads=[r_P[idx % NPIPE], r["cstb"]], writes=[tb_r])
            a2 = idx % 2
            R.add("act", lambda e: e.activation(out=wT[a2][:, 0:w], in_=tb[:, 0:w], func=AF.Copy),
                  reads=[tb_r], writes=[r_wT[a2]])
            pp.free(tb, tb_r)

        def pv(idx):
            h, k, c0, w, a = state[idx]
            ob, ob_r = obanks[h // 8]
            a2 = idx % 2
            nb_ = w // 128
            for jb in range(nb_):
                blk = (c0 + 128 * jb) // 128
                first = (k == 0 and jb == 0)
                last = (k == nseg - 1 and jb == nb_ - 1)
                R.add("pe", lambda e, jb=jb, blk=blk, first=first, last=last: e.matmul(
                    ob[:, 64 * (h % 8):64 * (h % 8) + 64], lhsT=wT[a2][:, 128 * jb:128 * jb + 128], rhs=v_sb[:, blk, 64 * h:64 * h + 64],
                    start=first, stop=last),
                    reads=[r_wT[a2], r_v[blk // 4]], writes=[ob_r])

        n = len(units)
        for s in range(n + 4):
            if s < n:
                qk(units[s], s)
            if 0 <= s - 3 < n:
                tr(s - 3)
            if 0 <= s - 4 < n:
                pv(s - 4)
        return obanks

    def epilogue(i, obanks):
        for half in range(2):
            ob, ob_r = obanks[half]
            sl = slice(512 * half, 512 * half + 512)
            R.add("dve", lambda e, ob=ob, sl=sl: e.scalar_tensor_tensor(out=og[:, sl], in0=ob[:, :], scalar=-1.0, in1=sg[:, sl],
                                                                        op0=ALU.mult, op1=ALU.mult),
                  reads=[ob_r, r["sg"]], writes=[r["h1T"]])
            pp.free(ob, ob_r)
        for half in range(2):
            tb, tb_r = pp.get()
            for q in range(4):
                fc = 4 * half + q
                R.add("pe", lambda e, tb=tb, q=q, fc=fc: e.matmul(tb[:, 128 * q:128 * q + 128], lhsT=og[:, 128 * fc:128 * fc + 128],
                                                                    rhs=identb, start=True, stop=True),
                      reads=[r["h1T"], r["cstb"]], writes=[tb_r])
            R.add("act", lambda e, tb=tb, half=half: e.activation(out=ogT[:, 512 * half:512 * half + 512], in_=tb[:, :], func=AF.Copy),
                  reads=[tb_r], writes=[r["wn0"], r["wn1"]])
            pp.free(tb, tb_r)

    def epilogue2(i):
        for half in range(2):
            yb, yb_r = pp.get()
            sl = slice(512 * half, 512 * half + 512)
            for fc in range(KC):
                R.add("pe", lambda e, yb=yb, fc=fc, sl=sl: e.matmul(yb[:, :], lhsT=ogT[:, 128 * fc:128 * fc + 128], rhs=wout_sb[:, fc, sl],
                                                                     start=(fc == 0), stop=(fc == KC - 1)),
                      reads=[r["wn0"], r["wn1"], r_wout[fc]], writes=[yb_r])
            R.add("dve", lambda e, yb=yb, sl=sl: e.tensor_tensor(out=x_sb[:, i, sl], in0=yb[:, :], in1=x_sb[:, i, sl], op=ALU.add),
                  reads=[yb_r, r_x[i]], writes=[r_x[i]])
            pp.free(yb, yb_r)
            R.add("act", lambda e, half=half, sl=sl: e.activation(out=wT[half][:, :], in_=x_sb[:, i, sl], func=AF.Square,
                                                                  accum_out=ssx[:, 32 * half + i:32 * half + i + 1]),
                  reads=[r_x[i]], writes=[r_wT[half], r["ssx"]])

    prologue(0)
    prologue_b(0)
    for i in range(NB):
        ob = attention(i)
        epilogue(i, ob)
        if i + 1 < NB:
            prologue(i + 1)
        epilogue2(i)
        if i + 1 < NB:
            prologue_b(i + 1)
    R.add("dve", lambda e: e.tensor_tensor(out=ssx[:, 0:16], in0=ssx[:, 0:16], in1=ssx[:, 32:48], op=ALU.add),
          reads=[r["ssx"]], writes=[r["ssx"]])

    R.add("act", lambda e: e.activation(out=ssx[:, 16:32], in_=ssx[:, 0:16], func=AF.Ln, scale=1.0 / D, bias=EPS),
          reads=[r["ssx"]], writes=[r["ssx"]])
    R.add("act", lambda e: e.activation(out=ssx[:, 48:64], in_=ssx[:, 16:32], func=AF.Exp, scale=-0.5),
          reads=[r["ssx"]], writes=[r["ssx"]])
    fg = sg
    R.add("sp", lambda e: e.dma_start(out=fg, in_=fg_d), writes=[r["sg"]], dma=True)
    for n in range(NB):
        R.add("act", lambda e, n=n: e.activation(out=x_sb[:, n, :], in_=x_sb[:, n, :], func=AF.Identity, scale=ssx[:, 48 + n:49 + n]),
              reads=[r_x[n], r["ssx"]], writes=[r_x[n]])
        eng = "dve" if n % 2 == 0 else "pool"
        R.add(eng, lambda e, n=n: e.tensor_tensor(out=x_sb[:, n, :], in0=x_sb[:, n, :], in1=fg, op=ALU.mult),
              reads=[r_x[n], r["sg"]], writes=[r_x[n]])
        R.add("sp", lambda e, n=n: e.dma_start(out=outv[:, n, :], in_=x_sb[:, n, :]), reads=[r_x[n]], dma=True, final=True)
    R.emit_phase(last=True)
    esC.close()
    esBC.close()
    es.close()
    return R


def host_inputs(inp, b):
    import ml_dtypes
    f = np.float32
    fm = lambda v: np.ascontiguousarray(np.asarray(v, f).reshape(-1, 128).T)
    vec = np.zeros((128, 100), f)
    vec[:, 0:8] = fm(inp["c"][b])
    vec[:, 8:16] = fm(inp["norm_gain"][0])
    vec[:, 16:40] = fm(inp["b_ada"][0])
    vec[:, 40:44] = fm(inp["gla_b_gk"][0])
    vec[:, 44:52] = fm(inp["kv_gain"])
    vec[:, 52:68] = fm(inp["kv_b_ada"])
    vec[:, 68:76] = fm(inp["norm_gain"][1])
    vec[:, 76:100] = fm(inp["b_ada"][1])
    cst = np.zeros((128, 896), f)
    cst[:, 0:128] = np.eye(128, dtype=f)
    mt = (np.arange(128)[:, None] <= np.arange(128)[None, :]).astype(f)
    cst[:, 128:640] = np.tile(mt, (1, 4))
    cst[:, 640:768] = 1.0
    cst[:, 768:896] = np.eye(128, dtype=f)[::-1]
    cb = np.zeros((128, 384), f)
    cb[:, 0:128] = np.eye(128, dtype=f)
    rr = np.arange(128)[:, None]
    cc = np.arange(128)[None, :]
    cb[:, 128:256] = np.where(cc <= 127 - rr, -30000.0, 0.0)
    cb[:, 256:384] = -np.eye(128, dtype=f)
    return {
        "x": np.ascontiguousarray(inp["x"][b], dtype=f),
        "vec": vec,
        "ogain_rep": np.ascontiguousarray(np.broadcast_to(np.asarray(inp["gla_o_gain"][0], f)[None, :], (128, 256))),
        "consts": cst,
        "constsb": cb.astype(ml_dtypes.bfloat16),
        "fg_rep": np.ascontiguousarray(np.broadcast_to(np.asarray(inp["final_gain"], f)[None, :], (128, D))),
        "w_ada0": np.ascontiguousarray(inp["w_ada"][0], dtype=f),
        "w_ada1": np.ascontiguousarray(inp["w_ada"][1], dtype=f),
        "kv_w_ada": np.ascontiguousarray(inp["kv_w_ada"], dtype=f),
        "gla_w_in": np.ascontiguousarray(inp["gla_w_in"][0], dtype=f),
        "w_gk2": np.ascontiguousarray(inp["gla_w_gk2"][0], dtype=f),
        "gla_w_out": np.ascontiguousarray(inp["gla_w_out"][0], dtype=f),
        "w_kv": np.ascontiguousarray(inp["w_kv"], dtype=f),
        "sb_w_in": np.ascontiguousarray(inp["sb_w_in"][0], dtype=f),
        "sb_w_out": np.ascontiguousarray(inp["sb_w_out"][0], dtype=f),
    }


NCORES = 8


def kernel(**inputs):
    inp = {k: np.asarray(v) for k, v in inputs.items()}
    nc = bass.Bass("TRN2", target_bir_lowering=False)
    build_fused(nc)
    in_maps = [host_inputs(inp, b) for b in range(NCORES)]
    res = run_bass_kernel_spmd(nc, in_maps, core_ids=list(range(NCORES)))
    return np.stack([r["out"] for r in res.results]).astype(np.float32)
```

```python
import numpy as np
from contextlib import ExitStack
import concourse.bass as bass
import concourse.mybir as mybir
from concourse.bass_utils import run_bass_kernel_spmd

F32 = mybir.dt.float32
BF16 = mybir.dt.bfloat16
AF = mybir.ActivationFunctionType
ALU = mybir.AluOpType

T = 2048
D = 1024
NB = 16
KC = 8
EPS = 1e-6
GIN = 3088


ENG = ("pe", "act", "dve", "pool", "sp")


class Res:
    __slots__ = ("name", "w", "r")

    def __init__(self, name):
        self.name = name
        self.w = None
        self.r = []


class Op:
    __slots__ = ("eng", "emit", "deps", "sig", "count", "dma", "sem", "semval", "final", "phase")

    def __init__(self, eng, emit, dma):
        self.eng = eng
        self.emit = emit
        self.dma = dma
        self.deps = []
        self.sig = False
        self.count = 0
        self.sem = None
        self.final = False
        self.phase = 0


class Rec:
    def __init__(self, nc, es):
        self.nc = nc
        self.es = es
        self.q = {e: [] for e in ENG}
        self.n = 0
        self.phase = 0
        self.sems = {e: es.enter_context(nc.semaphore("s_" + e)) for e in ENG}
        self.bsem = es.enter_context(nc.semaphore("s_bar"))
        self.cnt = {e: 0 for e in ENG}
        self.finals = []
        self.dsems = []
        self.res = []

    def R(self, name):
        r = Res(name)
        self.res.append(r)
        return r

    def add(self, eng, emit, reads=(), writes=(), dma=False, final=False):
        op = Op(eng, emit, dma)
        op.final = final
        op.phase = self.phase
        deps = {}
        for r in reads:
            if r.w is not None:
                deps[id(r.w)] = r.w
        for w in writes:
            if w.w is not None:
                deps[id(w.w)] = w.w
            for x in w.r:
                deps[id(x)] = x
        for d in deps.values():
            if d is op:
                continue
            if (not dma) and (not d.dma) and d.eng == "pe" and eng == "pe":
                continue
            assert d.dma or d.phase == self.phase, "compute dep crosses phase"
            op.deps.append(d)
            d.sig = True
        for r in reads:
            if not dma:
                r.r = [x for x in r.r if x.dma or x.eng != eng]
            r.r.append(op)
        for w in writes:
            w.w = op
            w.r = []
        self.q[eng].append(op)
        self.n += 1
        return op

    def emit_phase(self, last=False):
        nc = self.nc
        di = 0
        for e in ENG:
            for op in self.q[e]:
                if op.dma:
                    if e == "pool":
                        self.nsw = getattr(self, "nsw", 0) + 1
                        op.sem = self.es.enter_context(nc.semaphore("dsw%d" % self.nsw))
                        op.semval = 16
                    else:
                        if di == len(self.dsems):
                            self.dsems.append([self.es.enter_context(nc.semaphore("d%d" % di)), 0])
                        ent = self.dsems[di]
                        di += 1
                        ent[1] += 1
                        op.sem = ent[0]
                        op.semval = 16 * ent[1]
                    if op.final:
                        self.finals.append(op)
                elif op.sig:
                    self.cnt[e] += 1
                    op.count = self.cnt[e]
        self.phase += 1
        k = self.phase
        phase_dmas = [op for e in ENG for op in self.q[e] if op.dma]
        with nc.Block() as block:
            def run(e, eng):
                waited = {}
                for op in self.q[e]:
                    for d in op.deps:
                        if d.dma:
                            key = id(d)
                            if key not in waited:
                                eng.wait_ge(d.sem, d.semval)
                                waited[key] = 1
                        else:
                            if waited.get(d.eng, 0) < d.count:
                                eng.wait_ge(self.sems[d.eng], d.count)
                                waited[d.eng] = d.count
                    inst = op.emit(eng)
                    if op.dma:
                        inst.then_inc(op.sem, 16)
                    elif op.sig:
                        inst.then_inc(self.sems[e], 1)
                if e == "sp":
                    for d in phase_dmas:
                        eng.wait_ge(d.sem, d.semval)
                eng.drain().then_inc(self.bsem, 1)
                eng.wait_ge(self.bsem, len(ENG) * k)

            @block.tensor
            def _(eng):
                run("pe", eng)

            @block.scalar
            def _(eng):
                run("act", eng)

            @block.vector
            def _(eng):
                run("dve", eng)

            @block.gpsimd
            def _(eng):
                run("pool", eng)

            @block.sync
            def _(eng):
                run("sp", eng)
        self.q = {e: [] for e in ENG}
        for r in self.res:
            r.w = None
            r.r = []

class PsumPool:
    def __init__(self, banks, res):
        self.free_list = list(zip(banks, res))

    def get(self):
        assert self.free_list, "out of PSUM banks"
        return self.free_list.pop(0)

    def free(self, bk, res):
        self.free_list.append((bk, res))


def ada_matvec(R, wdram, ncols, sc, sc_res, pp, stage, stage_res, col_out, col_out_res, rowtmp, rowtmp_res, one11, one_res,
               q="act", hook=None, maxw=2048):
    colbank, colbank_res = pp.get()
    i = 0
    for c0 in range(0, ncols, maxw):
        width = min(maxw, ncols - c0)
        nb = width // 512
        rbs = [pp.get() for _ in range(nb)]
        for kc in range(KC):
            st, st_res = stage[i % 2], stage_res[i % 2]
            i += 1
            R.add(q, lambda e, st=st, kc=kc, c0=c0, width=width: e.dma_start(out=st[:, 0:width],
                                                                              in_=wdram[128 * kc:128 * kc + 128, c0:c0 + width]),
                  writes=st_res, dma=True)
            if hook is not None:
                hook(kc)
            for bi, (rb, rb_r) in enumerate(rbs):
                R.add("pe", lambda e, st=st, kc=kc, rb=rb, bi=bi: e.matmul(rb[0:1, 0:512], lhsT=sc[:, kc:kc + 1],
                                                                            rhs=st[:, 512 * bi:512 * bi + 512],
                                                                            start=(kc == 0), stop=(kc == KC - 1)),
                      reads=st_res + [sc_res], writes=[rb_r])
        for bi, (rb, rb_r) in enumerate(rbs):
            for m4 in range(4):
                g = (c0 + 512 * bi) // 128 + m4
                m = g % 2
                R.add("act", lambda e, m=m, rb=rb, m4=m4: e.activation(out=rowtmp[m], in_=rb[0:1, 128 * m4:128 * m4 + 128], func=AF.Copy),
                      reads=[rb_r], writes=[rowtmp_res[m]])
                R.add("pe", lambda e, m=m, g=g: e.matmul(colbank[:, g:g + 1], lhsT=rowtmp[m], rhs=one11, start=True, stop=True),
                      reads=[rowtmp_res[m], one_res], writes=[colbank_res])
            pp.free(rb, rb_r)
    ng = ncols // 128
    R.add("act", lambda e: e.activation(out=col_out[:, 0:ng], in_=colbank[:, 0:ng], func=AF.Copy),
          reads=[colbank_res], writes=[col_out_res])
    pp.free(colbank, colbank_res)


def bcast_rows(R, pp, src, src_res, ncols8, tmpg, tmpg_res, ones128, ident, cst_res, out, out_res):
    for half in range(ncols8 // 4):
        bk, bk_r = pp.get()
        for q4 in range(4):
            kc = 4 * half + q4
            tg, tg_r = tmpg[kc % 2], tmpg_res[kc % 2]
            R.add("dve", lambda e, tg=tg, kc=kc: e.tensor_scalar_mul(out=tg, in0=ones128, scalar1=src[:, kc:kc + 1]),
                  reads=[src_res, cst_res], writes=[tg_r])
            R.add("pe", lambda e, bk=bk, q4=q4, tg=tg: e.matmul(bk[:, 128 * q4:128 * q4 + 128], lhsT=tg, rhs=ident,
                                                                 start=True, stop=True),
                  reads=[tg_r, cst_res], writes=[bk_r])
        R.add("act", lambda e, bk=bk, half=half: e.activation(out=out[:, 512 * half:512 * half + 512], in_=bk[:, :], func=AF.Copy),
              reads=[bk_r], writes=[out_res])
        pp.free(bk, bk_r)


def silu_small(R, c_ap, c_res, tmp, tmp_res, out, out_res, n):
    R.add("act", lambda e: e.activation(out=tmp[:, 0:n], in_=c_ap, func=AF.Exp, scale=-1.0),
          reads=[c_res], writes=[tmp_res])
    R.add("act", lambda e: e.activation(out=tmp[:, n:2 * n], in_=tmp[:, 0:n], func=AF.Ln, bias=1.0),
          reads=[tmp_res], writes=[tmp_res])
    R.add("act", lambda e: e.activation(out=tmp[:, 0:n], in_=tmp[:, n:2 * n], func=AF.Exp, scale=-1.0),
          reads=[tmp_res], writes=[tmp_res])
    R.add("dve", lambda e: e.tensor_tensor(out=out, in0=c_ap, in1=tmp[:, 0:n], op=ALU.mult),
          reads=[c_res, tmp_res], writes=[out_res])


def build_fused(nc):
    es = ExitStack()
    R = Rec(nc, es)

    def dram(name, shape, dt=F32, kind="ExternalInput"):
        return nc.dram_tensor(name, list(shape), dt, kind=kind).ap()

    x_d = dram("x", [T, D])
    vec_d = dram("vec", [128, 100])
    ogain_d = dram("ogain_rep", [128, 256])
    cst_d = dram("consts", [128, 896])
    cstb_d = dram("constsb", [128, 384], BF16)
    fg_d = dram("fg_rep", [128, D])
    wada0_d = dram("w_ada0", [D, 3 * D])
    wada1_d = dram("w_ada1", [D, 3 * D])
    wadak_d = dram("kv_w_ada", [D, 2 * D])
    win_d = dram("gla_w_in", [D, GIN])
    wgk2_d = dram("w_gk2", [16, 512])
    wout_d = dram("gla_w_out", [D, D])
    wkv_d = dram("w_kv", [D, 2 * D])
    sbwin_d = dram("sb_w_in", [D, 2 * D])
    sbwout_d = dram("sb_w_out", [D, D])
    out_d = dram("out", [T, D], kind="ExternalOutput")
    xv = x_d.rearrange("(n p) d -> p n d", p=128)
    outv = out_d.rearrange("(n p) d -> p n d", p=128)

    def psb(name, shape, dt=F32):
        return es.enter_context(nc.sbuf_tensor("sb_" + name, list(shape), dt))

    x_sb = psb("x_sb", [128, NB, D])
    vec = psb("vec", [128, 100])
    cst = psb("cst", [128, 256])
    cstb = psb("cstb", [128, 384], BF16)
    sc = psb("sc", [128, 8])
    rstd1 = psb("rstd1", [128, NB])
    cond1 = psb("cond1", [128, 24])
    ident = cst[:, 0:128]
    ones128 = cst[:, 128:256]
    identb = cstb[:, 0:128]
    maskb = cstb[:, 128:256]
    negidentb = cstb[:, 256:384]

    banks = [es.enter_context(nc.psum_tensor("bank%d" % i, [128, 512], F32)) for i in range(8)]
    pp = PsumPool(banks, [R.R("bank%d" % i) for i in range(8)])
    r_x = [R.R("x%d" % n) for n in range(NB)]
    rp = {n: R.R(n) for n in ["vec", "cst", "cstb", "sc", "rstd1", "cond1"]}

    R.add("sp", lambda e: e.dma_start(out=vec[:], in_=vec_d), writes=[rp["vec"]], dma=True)
    R.add("sp", lambda e: e.dma_start(out=cst[:, 0:128], in_=cst_d[:, 0:128]), writes=[rp["cst"]], dma=True)
    R.add("sp", lambda e: e.dma_start(out=cst[:, 128:256], in_=cst_d[:, 640:768]), writes=[rp["cst"]], dma=True)
    R.add("sp", lambda e: e.dma_start(out=cstb[:], in_=cstb_d), writes=[rp["cstb"]], dma=True)

    esA = ExitStack()

    def sb(name, shape, dt=F32):
        return esA.enter_context(nc.sbuf_tensor("a_" + name, list(shape), dt))

    win_sb = sb("win_sb", [128, KC, GIN], BF16)
    wout_sb = sb("wout_sb", [128, KC, D], BF16)
    ogain = sb("ogain", [128, 256])
    wgk2 = sb("wgk2", [16, 512])
    small = sb("small", [128, 96])
    gate_rep = sb("gate_rep", [128, 1024])
    ssx = sb("ssx", [128, 2 * NB])
    rstd = sb("rstd", [128, NB])
    junk = sb("junk", [128, 1024], BF16)
    S = sb("S", [128, 1024])
    S_bf = sb("S_bf", [128, 1024], BF16)
    arena = sb("arena", [128, 4096])
    tmpg = [sb("tmpg%d" % i, [128, 128]) for i in range(2)]
    NBUF = 2
    G2 = [arena[:, 0:1024], sb("G2b", [128, 1024])]
    gA = arena[:, 1024:1536]
    gB = arena[:, 1536:2048]
    ytmp = arena[:, 2048:2560]
    xs = arena[:, 2560:3584]
    stage = [arena[:, 0:2048], arena[:, 2048:4096]]
    hT = [sb("hT%d" % i, [128, KC, 128], BF16) for i in range(NBUF)]
    v_sb = [sb("v_sb%d" % i, [128, 1024], BF16) for i in range(NBUF)]
    gklr = sb("gklr", [16, 128])
    ebuf = sb("ebuf", [128, 512])
    spb = sb("spb", [128, 512])
    csb = sb("csb", [128, 512])
    E1 = sb("E1", [128, 512])
    E2 = sb("E2", [128, 512])
    qe = sb("qe", [128, 512], BF16)
    ke = sb("ke", [128, 512], BF16)
    kteT = sb("kteT", [128, 512], BF16)
    kte = sb("kte", [128, 512], BF16)
    sTm = sb("sTm", [128, 512], BF16)
    hT32 = sb("hT32", [128, KC, 128])
    qe32 = sb("qe32", [128, 512])
    ke32 = sb("ke32", [128, 512])
    w32slot = [arena[:, 2048:3072], arena[:, 3072:4096]]
    oss = sb("oss", [128, 12])
    og = sb("og", [128, 1024], BF16)
    ogT = sb("ogT", [128, KC * 128], BF16)

    r = {n: R.R(n) for n in ["ogain", "wgk2", "small", "gate_rep", "ssx", "rstd", "S", "S_bf",
                             "gA", "gB", "ytmp", "xs", "arena_tail", "gklr", "e", "sp", "cs", "E1", "E2", "qe", "ke", "kteT", "kte",
                             "sTm", "oss", "og", "ogT", "tg0", "tg1", "hT32", "qe32", "ke32"]}
    r["cst"] = rp["cst"]
    r["identb"] = rp["cstb"]
    r["vec"] = rp["vec"]
    w32_res = [[r["ytmp"], r["xs"]], [r["xs"], r["arena_tail"]]]
    r_win = [R.R("win%d" % k) for k in range(KC)]
    r_wout = [R.R("wout%d" % k) for k in range(KC)]
    r_G2 = [R.R("G2_0"), R.R("G2_1")]
    r_hT = [R.R("hT0"), R.R("hT1")]
    r_v = [R.R("v0"), R.R("v1")]
    stage_res = [[r_G2[0], r["gA"], r["gB"]], [r["ytmp"], r["xs"], r["arena_tail"]]]

    maskT4t = sb("maskT4", [128, 512])
    maskT4 = maskT4t[:, :]
    R.add("sp", lambda e: e.dma_start(out=maskT4t[:], in_=cst_d[:, 128:640]), writes=[r["cst"]], dma=True)
    R.add("sp", lambda e: e.dma_start(out=wgk2[:], in_=wgk2_d), writes=[r["wgk2"]], dma=True)
    R.add("sp", lambda e: e.dma_start(out=ogain[:], in_=ogain_d), writes=[r["ogain"]], dma=True)
    winv = win_d.rearrange("(k p) f -> p k f", p=128)
    woutv = wout_d.rearrange("(k p) f -> p k f", p=128)
    for kc in range(KC):
        R.add("pool", lambda e, kc=kc: e.dma_start(out=win_sb[:, kc, :], in_=winv[:, kc, :]),
              writes=[r_win[kc]], dma=True)
    for n in range(0, NB, 2):
        R.add("sp", lambda e, n=n: e.dma_start(out=x_sb[:, n:n + 2, :], in_=xv[:, n:n + 2, :]), writes=[r_x[n], r_x[n + 1]], dma=True)
        if n == 0:
            for kc in range(KC):
                R.add("pool", lambda e, kc=kc: e.dma_start(out=wout_sb[:, kc, :], in_=woutv[:, kc, :]),
                      writes=[r_wout[kc]], dma=True)

    R.add("dve", lambda e: e.memset(ssx[:], 0.0), writes=[r["ssx"]])
    R.add("dve", lambda e: e.memset(S[:], 0.0), writes=[r["S"]])
    R.add("dve", lambda e: e.memset(S_bf[:], 0.0), writes=[r["S_bf"]])
    R.add("dve", lambda e: e.memset(oss[:], 0.0), writes=[r["oss"]])
    silu_small(R, vec[:, 0:8], r["vec"], small[:, 8:24], r["small"], sc[:, 0:8], rp["sc"], 8)
    def sq_hook(kc):
        for n in (2 * kc, 2 * kc + 1):
            R.add("act", lambda e, n=n: e.activation(out=junk[:], in_=x_sb[:, n, :], func=AF.Square, accum_out=ssx[:, n:n + 1]),
                  reads=[r_x[n]], writes=[r["ssx"]])

    ada_matvec(R, wada0_d[:, 0:2 * D], 2 * D, sc, rp["sc"], pp, stage, stage_res, small[:, 24:40], r["small"],
               [tmpg[0][0:1, :], tmpg[1][0:1, :]], [r["tg0"], r["tg1"]], ones128[0:1, 0:1], r["cst"], hook=sq_hook)
    R.add("act", lambda e: e.activation(out=ssx[:, NB:2 * NB], in_=ssx[:, 0:NB], func=AF.Ln, scale=1.0 / D, bias=EPS),
          reads=[r["ssx"]], writes=[r["ssx"]])
    R.add("act", lambda e: e.activation(out=rstd[:, 0:NB], in_=ssx[:, NB:2 * NB], func=AF.Exp, scale=-0.5),
          reads=[r["ssx"]], writes=[r["rstd"]])
    R.add("dve", lambda e: e.tensor_tensor(out=small[:, 48:56], in0=small[:, 24:32], in1=vec[:, 16:24], op=ALU.add),
          reads=[r["small"], r["vec"]], writes=[r["small"]])
    R.add("dve", lambda e: e.scalar_tensor_tensor(out=small[:, 56:64], in0=small[:, 32:40], scalar=1.0, in1=vec[:, 24:32],
                                                  op0=ALU.add, op1=ALU.add),
          reads=[r["small"], r["vec"]], writes=[r["small"]])
    R.add("dve", lambda e: e.tensor_tensor(out=small[:, 56:64], in0=small[:, 56:64], in1=vec[:, 8:16], op=ALU.mult),
          reads=[r["small"], r["vec"]], writes=[r["small"]])
    R.add("dve", lambda e: e.tensor_scalar_mul(out=small[:, 72:76], in0=vec[:, 40:44], scalar1=-1.0),
          reads=[r["vec"]], writes=[r["small"]])
    r["small2"] = R.R("small2")
    small2 = sb("small2", [128, 16])

    def gate_setup():
        ada_matvec(R, wada0_d[:, 2 * D:3 * D], D, sc, rp["sc"], pp, stage, stage_res, small2[:, 0:8], r["small2"],
                   [tmpg[0][0:1, :], tmpg[1][0:1, :]], [r["tg0"], r["tg1"]], ones128[0:1, 0:1], r["cst"])
        R.add("dve", lambda e: e.tensor_tensor(out=small2[:, 8:16], in0=small2[:, 0:8], in1=vec[:, 32:40], op=ALU.add),
              reads=[r["small2"], r["vec"]], writes=[r["small2"]])
        bcast_rows(R, pp, small2[:, 8:16], r["small2"], 8, [t[:] for t in tmpg], [r["tg0"], r["tg1"]], ones128, ident, r["cst"],
                   gate_rep, r["gate_rep"])

    shift0 = small[:, 48:56]
    A0 = small[:, 56:64]
    negbgk = small[:, 72:76]

    def stage1_xs(n):
        R.add("dve", lambda e: e.tensor_scalar_mul(out=xs, in0=x_sb[:, n, :], scalar1=rstd[:, n:n + 1]),
              reads=[r_x[n], r["rstd"]], writes=[r["xs"]])

    def stage1(n):
        b = n % NBUF
        for half in range(2):
            bk, bk_r = pp.get()
            for q in range(4):
                kc = 4 * half + q
                R.add("pe", lambda e, bk=bk, q=q, kc=kc: e.matmul(bk[:, 128 * q:128 * q + 128],
                                                                    lhsT=xs[:, 128 * kc:128 * kc + 128], rhs=ident,
                                                                    start=True, stop=True),
                      reads=[r["xs"], r["cst"]], writes=[bk_r])
            for q in range(4):
                kc = 4 * half + q
                R.add("act", lambda e, bk=bk, q=q, kc=kc: e.activation(out=hT[b][:, kc, :], in_=bk[:, 128 * q:128 * q + 128],
                                                                        func=AF.Identity, scale=A0[:, kc:kc + 1],
                                                                        bias=shift0[:, kc:kc + 1]),
                      reads=[bk_r, r["small"]], writes=[r_hT[b]])
                if n == 0:
                    R.add("act", lambda e, bk=bk, q=q, kc=kc: e.activation(out=hT32[:, kc, :], in_=bk[:, 128 * q:128 * q + 128],
                                                                            func=AF.Identity, scale=A0[:, kc:kc + 1],
                                                                            bias=shift0[:, kc:kc + 1]),
                          reads=[bk_r, r["small"]], writes=[r["hT32"]])
            pp.free(bk, bk_r)

    def proj_qk(n):
        b = n % NBUF
        st = {}
        for name, c0 in (("q", 0), ("k", 512)):
            bk, bk_r = pp.get()
            for hh in range(4):
                if n == 0:
                    si = (hh + (0 if name == "q" else 4)) % 2
                    ws = w32slot[si].rearrange("p (k f) -> p k f", k=KC)
                    R.add("sp", lambda e, ws=ws, hh=hh, c0=c0: e.dma_start(out=ws, in_=winv[:, :, c0 + 128 * hh:c0 + 128 * hh + 128]),
                          writes=w32_res[si], dma=True)
                    for kc in range(KC):
                        R.add("pe", lambda e, bk=bk, hh=hh, kc=kc, ws=ws: e.matmul(
                            bk[:, 128 * hh:128 * hh + 128], lhsT=ws[:, kc, :], rhs=hT32[:, kc, :],
                            start=(kc == 0), stop=(kc == KC - 1)),
                            reads=w32_res[si] + [r["hT32"]], writes=[bk_r])
                    continue
                for kc in range(KC):
                    R.add("pe", lambda e, bk=bk, hh=hh, kc=kc, c0=c0: e.matmul(
                        bk[:, 128 * hh:128 * hh + 128], lhsT=win_sb[:, kc, c0 + 128 * hh:c0 + 128 * hh + 128],
                        rhs=hT[b][:, kc, :], start=(kc == 0), stop=(kc == KC - 1)),
                        reads=[r_win[kc], r_hT[b]], writes=[bk_r])
            st[name] = (bk, bk_r)
        bk, bk_r = pp.get()
        for kc in range(KC):
            R.add("pe", lambda e, bk=bk, kc=kc: e.matmul(bk[0:16, 0:128], lhsT=win_sb[:, kc, 3072:3088], rhs=hT[b][:, kc, :],
                                                         start=(kc == 0), stop=(kc == KC - 1)),
                  reads=[r_win[kc], r_hT[b]], writes=[bk_r])
        R.add("act", lambda e, bk=bk: e.activation(out=gklr[:], in_=bk[0:16, 0:128], func=AF.Copy),
              reads=[bk_r], writes=[r["gklr"]])
        pp.free(bk, bk_r)
        return st

    def proj_vg(n, which):
        b = n % NBUF
        for name, c0 in ((which, 1024 if which == "v" else 2048),):
            lst = []
            for half in range(2):
                bk, bk_r = pp.get()
                for kc in range(KC):
                    R.add("pe", lambda e, bk=bk, kc=kc, c0=c0, half=half: e.matmul(
                        bk[:, :], lhsT=hT[b][:, kc, :], rhs=win_sb[:, kc, c0 + 512 * half:c0 + 512 * half + 512],
                        start=(kc == 0), stop=(kc == KC - 1)),
                        reads=[r_win[kc], r_hT[b]], writes=[bk_r])
                lst.append((bk, bk_r))
                if name == "v":
                    R.add("act", lambda e, bk=bk, half=half: e.activation(out=v_sb[b][:, 512 * half:512 * half + 512], in_=bk[:, :],
                                                                           func=AF.Copy),
                          reads=[bk_r], writes=[r_v[b]])
                else:
                    sl = slice(512 * half, 512 * half + 512)
                    R.add("act", lambda e, bk=bk, sl=sl: e.activation(out=gA, in_=bk[:, :], func=AF.Exp, scale=-1.0),
                          reads=[bk_r], writes=[r["gA"]])
                    R.add("act", lambda e, sl=sl: e.activation(out=gB, in_=gA, func=AF.Ln, bias=1.0),
                          reads=[r["gA"]], writes=[r["gB"]])
                    R.add("act", lambda e, sl=sl: e.activation(out=gA, in_=gB, func=AF.Exp, scale=-1.0),
                          reads=[r["gB"]], writes=[r["gA"]])
                    R.add("dve", lambda e, bk=bk, sl=sl: e.tensor_tensor(out=gB, in0=bk[:, :], in1=gA, op=ALU.mult),
                          reads=[bk_r, r["gA"]], writes=[r["gB"]])
                    for q2 in range(2):
                        c1 = 512 * half + 256 * q2
                        R.add("dve", lambda e, c1=c1, q2=q2: e.tensor_tensor(out=G2[b][:, c1:c1 + 256], in0=gB[:, 256 * q2:256 * q2 + 256],
                                                                              in1=ogain[:, 0:256], op=ALU.mult),
                              reads=[r["gB"], r["ogain"]], writes=[r_G2[b]])
                pp.free(bk, bk_r)


    def gates(n, st):
        b = n % NBUF
        qb, qb_r = st["q"]
        kb, kb_r = st["k"]
        bk, bk_r = pp.get()
        for hh in range(4):
            R.add("pe", lambda e, hh=hh: e.matmul(bk[:, 128 * hh:128 * hh + 128], lhsT=wgk2[0:16, 128 * hh:128 * hh + 128],
                                                  rhs=gklr[:], start=True, stop=True),
                  reads=[r["wgk2"], r["gklr"]], writes=[bk_r])
        for hh in range(4):
            R.add("act", lambda e, hh=hh: e.activation(out=ebuf[:, 128 * hh:128 * hh + 128], in_=bk[:, 128 * hh:128 * hh + 128],
                                                       func=AF.Exp, scale=-1.0, bias=negbgk[:, hh:hh + 1]),
                  reads=[bk_r, r["small"]], writes=[r["e"]])
        pp.free(bk, bk_r)
        R.add("act", lambda e: e.activation(out=spb[:], in_=ebuf[:], func=AF.Ln, bias=1.0),
              reads=[r["e"]], writes=[r["sp"]])
        for hh in range(4):
            sl = slice(128 * hh, 128 * hh + 128)
            R.add("dve", lambda e, sl=sl: e.tensor_tensor_scan(out=csb[:, sl], data0=spb[:, sl], data1=spb[:, sl], initial=0.0,
                                                              op0=ALU.add, op1=ALU.max),
                  reads=[r["sp"]], writes=[r["cs"]])
        R.add("act", lambda e: e.activation(out=E1[:], in_=csb[:], func=AF.Exp, scale=-1.0 / 16),
              reads=[r["cs"]], writes=[r["E1"]])
        R.add("act", lambda e: e.activation(out=E2[:], in_=csb[:], func=AF.Exp, scale=1.0 / 16),
              reads=[r["cs"]], writes=[r["E2"]])
        R.add("dve", lambda e: e.scalar_tensor_tensor(out=qe[:], in0=qb[:, :], scalar=128 ** -0.5, in1=E1[:],
                                                      op0=ALU.mult, op1=ALU.mult),
              reads=[qb_r, r["E1"]], writes=[r["qe"]])
        R.add("dve", lambda e: e.tensor_tensor(out=ke[:], in0=kb[:, :], in1=E2[:], op=ALU.mult),
              reads=[kb_r, r["E2"]], writes=[r["ke"]])
        if n == 0:
            R.add("dve", lambda e: e.scalar_tensor_tensor(out=qe32[:], in0=qb[:, :], scalar=128 ** -0.5, in1=E1[:],
                                                          op0=ALU.mult, op1=ALU.mult),
                  reads=[qb_r, r["E1"]], writes=[r["qe32"]])
            R.add("dve", lambda e: e.tensor_tensor(out=ke32[:], in0=kb[:, :], in1=E2[:], op=ALU.mult),
                  reads=[kb_r, r["E2"]], writes=[r["ke32"]])
        for hh in range(4):
            sl = slice(128 * hh, 128 * hh + 128)
            R.add("dve", lambda e, sl=sl, hh=hh: e.scalar_tensor_tensor(
                out=kteT[:, sl], in0=kb[:, sl], scalar=E1[:, 128 * hh + 127:128 * hh + 128], in1=E2[:, sl],
                op0=ALU.mult, op1=ALU.mult),
                reads=[kb_r, r["E1"], r["E2"]], writes=[r["kteT"]])
        pp.free(qb, qb_r)
        pp.free(kb, kb_r)

    def attn_a(n):
        b = n % NBUF
        sb_, sb_r = pp.get()
        for hh in range(4):
            sl = slice(128 * hh, 128 * hh + 128)
            if n == 0:
                R.add("pe", lambda e, sl=sl: e.matmul(sb_[:, sl], lhsT=ke32[:, sl], rhs=qe32[:, sl], start=True, stop=True),
                      reads=[r["ke32"], r["qe32"]], writes=[sb_r])
            else:
                R.add("pe", lambda e, sl=sl: e.matmul(sb_[:, sl], lhsT=ke[:, sl], rhs=qe[:, sl], start=True, stop=True),
                      reads=[r["ke"], r["qe"]], writes=[sb_r])
        R.add("dve", lambda e: e.tensor_tensor(out=sTm[:], in0=sb_[:, :], in1=maskT4, op=ALU.mult),
              reads=[sb_r, r["cst"]], writes=[r["sTm"]])
        pp.free(sb_, sb_r)
        tb, tb_r = pp.get()
        for hh in range(4):
            sl = slice(128 * hh, 128 * hh + 128)
            R.add("pe", lambda e, sl=sl: e.matmul(tb[:, sl], lhsT=kteT[:, sl], rhs=identb, start=True, stop=True),
                  reads=[r["kteT"], r["identb"]], writes=[tb_r])
        R.add("act", lambda e: e.activation(out=kte[:], in_=tb[:, :], func=AF.Copy),
              reads=[tb_r], writes=[r["kte"]])
        pp.free(tb, tb_r)

    def attn_b(n):
        b = n % NBUF
        obanks = []
        for half in range(2):
            ob, ob_r = pp.get()
            for q in range(2):
                hh = 2 * half + q
                sl = slice(128 * hh, 128 * hh + 128)
                vs = slice(256 * hh, 256 * hh + 256)
                R.add("pe", lambda e, ob=ob, q=q, sl=sl, vs=vs: e.matmul(ob[:, 256 * q:256 * q + 256], lhsT=sTm[:, sl],
                                                                          rhs=v_sb[b][:, vs], start=True, stop=False),
                      reads=[r["sTm"], r_v[b]], writes=[ob_r])
                R.add("pe", lambda e, ob=ob, q=q, sl=sl, vs=vs: e.matmul(ob[:, 256 * q:256 * q + 256], lhsT=qe[:, sl],
                                                                          rhs=S_bf[:, vs], start=False, stop=True),
                      reads=[r["qe"], r["S_bf"]], writes=[ob_r])
            obanks.append((ob, ob_r))
        ibanks = []
        for half in range(2):
            ib, ib_r = pp.get()
            for q in range(2):
                hh = 2 * half + q
                sl = slice(128 * hh, 128 * hh + 128)
                vs = slice(256 * hh, 256 * hh + 256)
                R.add("pe", lambda e, ib=ib, q=q, sl=sl, vs=vs: e.matmul(ib[:, 256 * q:256 * q + 256], lhsT=kte[:, sl],
                                                                          rhs=v_sb[b][:, vs], start=True, stop=True),
                      reads=[r["kte"], r_v[b]], writes=[ib_r])
            ibanks.append((ib, ib_r))
        for hh in range(4):
            ob, ob_r = obanks[hh // 2]
            q = hh % 2
            R.add("act", lambda e, ob=ob, q=q, hh=hh: e.activation(out=junk[:, 256 * hh:256 * hh + 256], in_=ob[:, 256 * q:256 * q + 256],
                                                                    func=AF.Square, accum_out=oss[:, hh:hh + 1]),
                  reads=[ob_r], writes=[r["oss"]])
        R.add("act", lambda e: e.activation(out=oss[:, 4:8], in_=oss[:, 0:4], func=AF.Ln, scale=1.0 / 256, bias=EPS),
              reads=[r["oss"]], writes=[r["oss"]])
        R.add("act", lambda e: e.activation(out=oss[:, 8:12], in_=oss[:, 4:8], func=AF.Exp, scale=-0.5),
              reads=[r["oss"]], writes=[r["oss"]])
        for hh in range(4):
            ob, ob_r = obanks[hh // 2]
            q = hh % 2
            vs = slice(256 * hh, 256 * hh + 256)
            R.add("dve", lambda e, ob=ob, q=q, hh=hh, vs=vs: e.scalar_tensor_tensor(
                out=og[:, vs], in0=ob[:, 256 * q:256 * q + 256], scalar=oss[:, 8 + hh:9 + hh], in1=G2[b][:, vs],
                op0=ALU.mult, op1=ALU.mult),
                reads=[ob_r, r["oss"], r_G2[b]], writes=[r["og"]])
        R.add("dve", lambda e: e.memset(oss[:, 0:4], 0.0), reads=[], writes=[r["oss"]])
        for ob, ob_r in obanks:
            pp.free(ob, ob_r)
        for hh in range(4):
            ib, ib_r = ibanks[hh // 2]
            q = hh % 2
            vs = slice(256 * hh, 256 * hh + 256)
            R.add("dve", lambda e, ib=ib, q=q, hh=hh, vs=vs: e.scalar_tensor_tensor(
                out=S[:, vs], in0=S[:, vs], scalar=E1[:, 128 * hh + 127:128 * hh + 128], in1=ib[:, 256 * q:256 * q + 256],
                op0=ALU.mult, op1=ALU.add),
                reads=[ib_r, r["E1"], r["S"]], writes=[r["S"]])
        R.add("act", lambda e: e.activation(out=S_bf[:], in_=S[:], func=AF.Copy), reads=[r["S"]], writes=[r["S_bf"]])
        for ib, ib_r in ibanks:
            pp.free(ib, ib_r)

    def outp_a(n):
        b = n % NBUF
        for half in range(2):
            tb, tb_r = pp.get()
            for q in range(4):
                fc = 4 * half + q
                R.add("pe", lambda e, tb=tb, q=q, fc=fc: e.matmul(tb[:, 128 * q:128 * q + 128], lhsT=og[:, 128 * fc:128 * fc + 128],
                                                                    rhs=identb, start=True, stop=True),
                      reads=[r["og"], r["identb"]], writes=[tb_r])
            R.add("act", lambda e, tb=tb, half=half: e.activation(out=ogT[:, 512 * half:512 * half + 512], in_=tb[:, :], func=AF.Copy),
                  reads=[tb_r], writes=[r["ogT"]])
            pp.free(tb, tb_r)

    def outp_b(n):
        b = n % NBUF
        for half in range(2):
            yb, yb_r = pp.get()
            sl = slice(512 * half, 512 * half + 512)
            for fc in range(KC):
                R.add("pe", lambda e, yb=yb, fc=fc, sl=sl: e.matmul(yb[:, :], lhsT=ogT[:, 128 * fc:128 * fc + 128], rhs=wout_sb[:, fc, sl],
                                                                     start=(fc == 0), stop=(fc == KC - 1)),
                      reads=[r["ogT"], r_wout[fc]], writes=[yb_r])
            R.add("dve", lambda e, yb=yb, sl=sl: e.tensor_tensor(out=ytmp, in0=yb[:, :], in1=gate_rep[:, sl], op=ALU.mult),
                  reads=[yb_r, r["gate_rep"]], writes=[r["ytmp"]])
            pp.free(yb, yb_r)
            R.add("dve", lambda e, sl=sl: e.tensor_tensor(out=x_sb[:, n, sl], in0=x_sb[:, n, sl], in1=ytmp, op=ALU.add),
                  reads=[r_x[n], r["ytmp"]], writes=[r_x[n]])

    sts = {}
    stage1_xs(0)
    stage1(0)
    gate_setup()
    sts[0] = proj_qk(0)
    proj_vg(0, "v")
    proj_vg(0, "g")
    stage1_xs(1)
    stage1(1)
    for n in range(NB):
        nxt = n + 1 < NB
        if n + 2 < NB:
            stage1_xs(n + 2)
        gates(n, sts[n])
        if nxt:
            sts[n + 1] = proj_qk(n + 1)
        if n + 2 < NB:
            stage1(n + 2)
        attn_a(n)
        if nxt:
            proj_vg(n + 1, "v")
        attn_b(n)
        outp_a(n)
        if nxt:
            proj_vg(n + 1, "g")
        outp_b(n)

    R.emit_phase()
    esA.close()

    esBC = ExitStack()
    kT_all = esBC.enter_context(nc.sbuf_tensor("kT_all", [128, 8, T], BF16))
    v_all = esBC.enter_context(nc.sbuf_tensor("v_all", [128, NB, D], BF16))
    r_kT = [R.R("kT%d" % g) for g in range(4)]
    r_vb = [R.R("vb%d" % j) for j in range(NB)]
    esB = ExitStack()

    def sb(name, shape, dt=F32):
        return esB.enter_context(nc.sbuf_tensor("b_" + name, list(shape), dt))

    wkv_sb = sb("wkv_sb", [128, KC, 2 * D], BF16)
    Jt = sb("Jt", [128, 128])
    Jm = Jt[:, :]
    small = sb("small", [128, 96])
    ssx = sb("ssx", [128, 2 * NB])
    junk = sb("junk", [128, 1024], BF16)
    arena = sb("arena", [128, 4096])
    stage = [arena[:, 0:2048], arena[:, 2048:4096]]
    xs = [arena[:, 0:1024], arena[:, 1024:2048]]
    hkT = [sb("hkT%d" % i, [128, KC, 512], BF16) for i in range(2)]
    rstd = rstd1
    r = {n: R.R(n) for n in ["small", "ssx", "xs0", "xs1", "hkT0", "hkT1", "st1"]}
    r["cst"] = rp["cst"]
    r["vec"] = rp["vec"]
    r["rstd"] = rp["rstd1"]
    r["J"] = R.R("J")
    R.add("sp", lambda e: e.dma_start(out=Jt[:], in_=cst_d[:, 768:896]), writes=[r["J"]], dma=True)
    r_wkv = [R.R("wkv%d" % k) for k in range(KC)]
    r_v = r_vb
    r_xs = [r["xs0"], r["xs1"]]
    r_hkT = [r["hkT0"], r["hkT1"]]
    stage_res = [[r["xs0"], r["xs1"]], [r["st1"]]]
    wkvv = wkv_d.rearrange("(k p) f -> p k f", p=128)
    r_wkvV = [R.R("wkvV%d" % k) for k in range(KC)]
    for kc in range(KC):
        R.add("pool", lambda e, kc=kc: e.dma_start(out=wkv_sb[:, kc, 0:D], in_=wkvv[:, kc, 0:D]), writes=[r_wkv[kc]], dma=True)
    for kc in range(KC):
        R.add("pool", lambda e, kc=kc: e.dma_start(out=wkv_sb[:, kc, D:2 * D], in_=wkvv[:, kc, D:2 * D]), writes=[r_wkvV[kc]], dma=True)
    R.add("dve", lambda e: e.memset(ssx[:], 0.0), writes=[r["ssx"]])
    rowt = sb("rowt", [1, 256])
    r["rt0"] = R.R("rt0")
    r["rt1"] = R.R("rt1")
    ada_matvec(R, wadak_d, 2 * D, sc, rp["sc"], pp, stage, stage_res, small[:, 24:40], r["small"],
               [rowt[0:1, 0:128], rowt[0:1, 128:256]], [r["rt0"], r["rt1"]], ones128[0:1, 0:1], r["cst"], q="sp")
    stg1 = sb("stg1", [128, 2048])
    r["s1a"] = R.R("s1a")
    r["s1b"] = R.R("s1b")
    ada_matvec(R, wada1_d, 3 * D, sc, rp["sc"], pp, [stg1[:, 0:1024], stg1[:, 1024:2048]], [[r["s1a"]], [r["s1b"]]],
               cond1[:, 0:24], rp["cond1"], [rowt[0:1, 0:128], rowt[0:1, 128:256]], [r["rt0"], r["rt1"]],
               ones128[0:1, 0:1], r["cst"], q="sp", maxw=1024)
    R.add("dve", lambda e: e.tensor_tensor(out=small[:, 40:48], in0=small[:, 24:32], in1=vec[:, 52:60], op=ALU.add),
          reads=[r["small"], r["vec"]], writes=[r["small"]])
    R.add("dve", lambda e: e.scalar_tensor_tensor(out=small[:, 48:56], in0=small[:, 32:40], scalar=1.0, in1=vec[:, 60:68],
                                                  op0=ALU.add, op1=ALU.add),
          reads=[r["small"], r["vec"]], writes=[r["small"]])
    R.add("dve", lambda e: e.tensor_tensor(out=small[:, 48:56], in0=small[:, 48:56], in1=vec[:, 44:52], op=ALU.mult),
          reads=[r["small"], r["vec"]], writes=[r["small"]])
    shiftk = small[:, 40:48]
    Ak = small[:, 48:56]

    def norm_block(n, slot):
        b = n % 2
        g = (NB - 1 - n) // 4
        hb = g % 2
        R.add("act", lambda e: e.activation(out=junk[:], in_=x_sb[:, n, :], func=AF.Square, accum_out=ssx[:, n:n + 1]),
              reads=[r_x[n]], writes=[r["ssx"]])
        R.add("act", lambda e: e.activation(out=ssx[:, NB + n:NB + n + 1], in_=ssx[:, n:n + 1], func=AF.Ln, scale=1.0 / D, bias=EPS),
              reads=[r["ssx"]], writes=[r["ssx"]])
        R.add("act", lambda e: e.activation(out=rstd[:, n:n + 1], in_=ssx[:, NB + n:NB + n + 1], func=AF.Exp, scale=-0.5),
              reads=[r["ssx"]], writes=[r["rstd"]])
        R.add("dve", lambda e: e.tensor_scalar_mul(out=xs[b], in0=x_sb[:, n, :], scalar1=rstd[:, n:n + 1]),
              reads=[r_x[n], r["rstd"]], writes=[r_xs[b]])
        for half in range(2):
            bk, bk_r = pp.get()
            for q in range(4):
                kc = 4 * half + q
                R.add("pe", lambda e, bk=bk, q=q, kc=kc: e.matmul(bk[:, 128 * q:128 * q + 128],
                                                                    lhsT=xs[b][:, 128 * kc:128 * kc + 128], rhs=Jm,
                                                                    start=True, stop=True),
                      reads=[r_xs[b], r["J"]], writes=[bk_r])
            for q in range(4):
                kc = 4 * half + q
                R.add("act", lambda e, bk=bk, q=q, kc=kc: e.activation(
                    out=hkT[hb][:, kc, 128 * slot:128 * slot + 128], in_=bk[:, 128 * q:128 * q + 128],
                    func=AF.Identity, scale=Ak[:, kc:kc + 1], bias=shiftk[:, kc:kc + 1]),
                    reads=[bk_r, r["small"]], writes=[r_hkT[hb]])
            pp.free(bk, bk_r)

    def kv_group(g):
        hb = g % 2
        for hp in range(8):
            bk, bk_r = pp.get()
            for kc in range(KC):
                R.add("pe", lambda e, bk=bk, hp=hp, kc=kc: e.matmul(bk[:, :], lhsT=wkv_sb[:, kc, 128 * hp:128 * hp + 128],
                                                                     rhs=hkT[hb][:, kc, :], start=(kc == 0), stop=(kc == KC - 1)),
                      reads=[r_wkv[kc], r_hkT[hb]], writes=[bk_r])
            eng = "act" if hp % 2 == 0 else "dve"
            if eng == "act":
                R.add("act", lambda e, bk=bk, hp=hp: e.activation(out=kT_all[:, hp, 512 * g:512 * g + 512], in_=bk[:, :], func=AF.Copy),
                      reads=[bk_r], writes=[r_kT[g]])
            else:
                R.add("dve", lambda e, bk=bk, hp=hp: e.tensor_copy(out=kT_all[:, hp, 512 * g:512 * g + 512], in_=bk[:, :]),
                      reads=[bk_r], writes=[r_kT[g]])
            pp.free(bk, bk_r)
        for j in range(4):
            jb = 4 * g + j
            for half in range(2):
                bk, bk_r = pp.get()
                for kc in range(KC):
                    R.add("pe", lambda e, bk=bk, kc=kc, j=j, half=half: e.matmul(
                        bk[:, :], lhsT=hkT[hb][:, kc, 128 * j:128 * j + 128],
                        rhs=wkv_sb[:, kc, 1024 + 512 * half:1024 + 512 * half + 512], start=(kc == 0), stop=(kc == KC - 1)),
                        reads=[r_wkvV[kc], r_hkT[hb]], writes=[bk_r])
                if half == 0:
                    R.add("act", lambda e, bk=bk, jb=jb, half=half: e.activation(out=v_all[:, jb, 512 * half:512 * half + 512], in_=bk[:, :],
                                                                                  func=AF.Copy),
                          reads=[bk_r], writes=[r_v[jb]])
                else:
                    R.add("dve", lambda e, bk=bk, jb=jb, half=half: e.tensor_copy(out=v_all[:, jb, 512 * half:512 * half + 512], in_=bk[:, :]),
                          reads=[bk_r], writes=[r_v[jb]])
                pp.free(bk, bk_r)

    for g in range(4):
        for slot in range(4):
            n = NB - 1 - (4 * g + slot)
            norm_block(n, slot)
        kv_group(g)

    R.emit_phase()
    esB.close()

    esC = ExitStack()

    def sb(name, shape, dt=F32):
        return esC.enter_context(nc.sbuf_tensor("c_" + name, list(shape), dt))

    kT = kT_all
    v_sb = v_all
    r_v = [R.R("vg%d" % g) for g in range(4)]
    win_sb = sb("win_sb", [128, KC, 2 * D], BF16)
    wout_sb = sb("wout_sb", [128, KC, D], BF16)
    small = sb("small", [128, 96])
    ssx = sb("ssx", [128, 4 * NB])
    NPIPE = 4
    arena = sb("arena", [128, 4352])
    stage = [arena[:, 0:2048], arena[:, 2048:4096]]
    sg = arena[:, 0:1024]
    xs = arena[:, 1024:2048]
    abuf = [arena[:, 2048:2560], arena[:, 2560:3072]]
    h1og = sb("h1og", [128, D], BF16)
    h1T = h1og.rearrange("p (k t) -> p k t", k=KC)
    og = h1og
    qTp = [sb("qT%d" % i, [128, KC * 128], BF16) for i in range(2)]
    wn2 = sb("wn2", [128, 1024], BF16)
    ogT = wn2
    pbv = arena[:, 3072:4352].bitcast(BF16)
    Pb = [pbv[:, 516 * i:516 * i + 516] for i in range(NPIPE)]
    wT = [sb("wT%d" % i, [128, 512], BF16) for i in range(2)]

    r = {n: R.R(n) for n in ["small", "ssx", "sg", "xs", "ab0", "ab1", "ab2", "h1T", "qT", "wn0", "wn1", "wn2", "wT0", "wT1", "wT2", "P0", "P1", "P2", "rt", "rt2"]}
    r["cst"] = rp["cst"]
    r["cstb"] = rp["cstb"]
    r["vec"] = rp["vec"]
    r["rstd1"] = rp["rstd1"]
    r_win = [R.R("cwin%d" % k) for k in range(KC)]
    r_wout = [R.R("cwout%d" % k) for k in range(KC)]
    r_ab = [r["ab0"], r["ab1"]]
    r_wT = [r["wT0"], r["wT1"]]
    r_P = [r["P0"], r["P1"], r["P2"], r["rt"]]
    stage_res = [[r["sg"], r["xs"]], [r["ab0"], r["ab1"], r["P0"], r["P1"], r["P2"]]]

    winv = sbwin_d.rearrange("(k p) f -> p k f", p=128)
    woutv = sbwout_d.rearrange("(k p) f -> p k f", p=128)
    r_winG = [R.R("cwinG%d" % k) for k in range(KC)]
    for kc in range(KC):
        R.add("pool", lambda e, kc=kc: e.dma_start(out=win_sb[:, kc, 0:D], in_=winv[:, kc, 0:D]), writes=[r_win[kc]], dma=True)
    for kc in range(KC):
        R.add("pool", lambda e, kc=kc: e.dma_start(out=win_sb[:, kc, D:2 * D], in_=winv[:, kc, D:2 * D]), writes=[r_winG[kc]], dma=True)
    R.add("dve", lambda e: e.memset(ssx[:], 0.0), writes=[r["ssx"]])
    R.add("dve", lambda e: e.memset(qTp[0][64:128, :], 0.0), writes=[r["qT"]])
    R.add("dve", lambda e: e.memset(qTp[1][0:64, :], 0.0), writes=[r["qT"]])
    R.add("dve", lambda e: e.tensor_copy(out=small[:, 24:48], in_=cond1[:, 0:24]), reads=[rp["cond1"]], writes=[r["small"]])
    R.add("dve", lambda e: e.tensor_tensor(out=small[:, 48:56], in0=small[:, 24:32], in1=vec[:, 76:84], op=ALU.add),
          reads=[r["small"], r["vec"]], writes=[r["small"]])
    R.add("dve", lambda e: e.scalar_tensor_tensor(out=small[:, 56:64], in0=small[:, 32:40], scalar=1.0, in1=vec[:, 84:92],
                                                  op0=ALU.add, op1=ALU.add),
          reads=[r["small"], r["vec"]], writes=[r["small"]])
    R.add("dve", lambda e: e.tensor_tensor(out=small[:, 56:64], in0=small[:, 56:64], in1=vec[:, 68:76], op=ALU.mult),
          reads=[r["small"], r["vec"]], writes=[r["small"]])
    R.add("dve", lambda e: e.tensor_tensor(out=small[:, 64:72], in0=small[:, 40:48], in1=vec[:, 92:100], op=ALU.add),
          reads=[r["small"], r["vec"]], writes=[r["small"]])
    tg = [abuf[0][:, 0:128], abuf[1][:, 0:128]]
    bcast_rows(R, pp, small[:, 64:72], r["small"], 8, tg, [r["ab0"], r["ab1"]], ones128, ident, r["cst"], xs, r["xs"])
    wst = [sg, arena[:, 2048:3072]]
    wst_res = [[r["sg"]], [r["ab0"], r["ab1"]]]
    for kc in range(KC):
        si = kc % 2
        R.add("sp", lambda e, kc=kc, si=si: e.dma_start(out=wst[si], in_=woutv[:, kc, :]), writes=wst_res[si], dma=True)
        eng = "dve" if kc % 2 == 0 else "pool"
        R.add(eng, lambda e, kc=kc, si=si: e.tensor_tensor(out=wout_sb[:, kc, :], in0=wst[si], in1=xs, op=ALU.mult),
              reads=wst_res[si] + [r["xs"]], writes=[r_wout[kc]])
    shift1 = small[:, 48:56]
    A1 = small[:, 56:64]

    xsb = xs.bitcast(BF16)[:, 0:1024]

    def prologue(i):
        R.add("dve", lambda e: e.tensor_scalar_mul(out=xsb, in0=x_sb[:, i, :], scalar1=rstd1[:, i:i + 1]),
              reads=[r_x[i], r["rstd1"]], writes=[r["xs"]])
        for half in range(2):
            bk, bk_r = pp.get()
            for q in range(4):
                kc = 4 * half + q
                R.add("pe", lambda e, bk=bk, q=q, kc=kc: e.matmul(bk[:, 128 * q:128 * q + 128], lhsT=xsb[:, 128 * kc:128 * kc + 128],
                                                                    rhs=identb, start=True, stop=True),
                      reads=[r["xs"], r["cstb"]], writes=[bk_r])
            for q in range(4):
                kc = 4 * half + q
                R.add("act", lambda e, bk=bk, q=q, kc=kc: e.activation(out=h1T[:, kc, :], in_=bk[:, 128 * q:128 * q + 128],
                                                                        func=AF.Identity, scale=A1[:, kc:kc + 1], bias=shift1[:, kc:kc + 1]),
                      reads=[bk_r, r["small"]], writes=[r["h1T"]])
            pp.free(bk, bk_r)

    def prologue_b(i):
        for half in range(2):
            bk, bk_r = pp.get()
            for q in range(4):
                hp = 4 * half + q
                for kc in range(KC):
                    R.add("pe", lambda e, bk=bk, q=q, hp=hp, kc=kc: e.matmul(bk[:, 128 * q:128 * q + 128],
                                                                              lhsT=win_sb[:, kc, 128 * hp:128 * hp + 128],
                                                                              rhs=h1T[:, kc, :], start=(kc == 0), stop=(kc == KC - 1)),
                          reads=[r_win[kc], r["h1T"]], writes=[bk_r])
            R.add("dve", lambda e, bk=bk, half=half: e.tensor_copy(out=qTp[0][0:64, 512 * half:512 * half + 512], in_=bk[0:64, :]),
                  reads=[bk_r], writes=[r["qT"]])
            R.add("dve", lambda e, bk=bk, half=half: e.tensor_copy(out=qTp[1][64:128, 512 * half:512 * half + 512], in_=bk[64:128, :]),
                  reads=[bk_r], writes=[r["qT"]])
            pp.free(bk, bk_r)
        for half in range(2):
            bk, bk_r = pp.get()
            sl = slice(512 * half, 512 * half + 512)
            for kc in range(KC):
                R.add("pe", lambda e, bk=bk, kc=kc, half=half: e.matmul(bk[:, :], lhsT=h1T[:, kc, :],
                                                                         rhs=win_sb[:, kc, 1024 + 512 * half:1024 + 512 * half + 512],
                                                                         start=(kc == 0), stop=(kc == KC - 1)),
                      reads=[r_winG[kc], r["h1T"]], writes=[bk_r])
            R.add("act", lambda e, bk=bk, sl=sl: e.activation(out=sg[:, sl], in_=bk[:, :], func=AF.Sigmoid),
                  reads=[bk_r], writes=[r["sg"]])
            R.add("dve", lambda e, bk=bk, sl=sl: e.tensor_tensor(out=sg[:, sl], in0=bk[:, :], in1=sg[:, sl], op=ALU.mult),
                  reads=[bk_r, r["sg"]], writes=[r["sg"]])
            pp.free(bk, bk_r)

    def attention(i):
        L = 128 * (i + 1)
        base = T - L
        nseg = (L + 511) // 512
        obanks = [pp.get(), pp.get()]
        units = [(h, k) for hp in range(8) for k in range(nseg) for h in (hp, hp + 8)]
        state = {}

        def qk(u, idx):
            h, k = u
            c0 = base + 512 * k
            w = min(512, L - 512 * k)
            zb, zb_r = pp.get()
            ps = slice(64 * (h % 2), 64 * (h % 2) + 64)
            hp = h // 2
            gset = sorted(set([(c0) // 512, (c0 + w - 1) // 512]))
            R.add("pe", lambda e: e.matmul(zb[:, 0:w], lhsT=qTp[h % 2][:, 128 * hp:128 * hp + 128], rhs=kT[:, hp, c0:c0 + w],
                                           start=True, stop=(k != 0)),
                  reads=[r["qT"]] + [r_kT[g] for g in gset], writes=[zb_r])
            if k == 0:
                R.add("pe", lambda e: e.matmul(zb[:, 0:128], lhsT=identb, rhs=maskb, start=False, stop=True),
                      reads=[r["cstb"]], writes=[zb_r])
            a = idx % 2
            R.add("act", lambda e: e.activation(out=abuf[a][:, 0:w], in_=zb[:, 0:w], func=AF.Sigmoid, scale=-0.125),
                  reads=[zb_r], writes=[r_ab[a]])
            pp.free(zb, zb_r)
            pb = idx % NPIPE
            if k > 0:
                pprev = (idx - 2) % NPIPE
                R.add("pool", lambda e: e.tensor_copy(out=Pb[pb][:, 0:1], in_=Pb[pprev][:, 512:513]),
                      reads=[r_P[pprev]], writes=[r_P[pb]])
            else:
                R.add("pool", lambda e: e.memset(Pb[pb][:, 0:1], 1.0), writes=[r_P[pb]])
            R.add("dve", lambda e: e.tensor_tensor_scan(out=Pb[pb][:, 1:1 + w], data0=abuf[a][:, 0:w], data1=abuf[a][:, 0:w],
                                                        initial=Pb[pb][:, 0:1], op0=ALU.mult, op1=ALU.min),
                  reads=[r_ab[a], r_P[pb]], writes=[r_P[pb]])
            state[idx] = (h, k, c0, w, a)

        def tr(idx):
            h, k, c0, w, a = state[idx]
            tb, tb_r = pp.get()
            for jb in range(w // 128):
                R.add("pe", lambda e, jb=jb: e.matmul(tb[:, 128 * jb:128 * jb + 128], lhsT=Pb[idx % NPIPE][:, 1 + 128 * jb:129 + 128 * jb], rhs=identb,
                                                      start=True, stop=False),
                      reads=[r_P[idx % NPIPE], r["cstb"]], writes=[tb_r])
                R.add("pe", lambda e, jb=jb: e.matmul(tb[:, 128 * jb:128 * jb + 128], lhsT=Pb[idx % NPIPE][:, 128 * jb:128 + 128 * jb], rhs=negidentb,
                                                      start=False, stop=True),
                      reads=[r_P[idx % NPIPE], r["cstb"]], writes=[tb_r])
            a2 = idx % 2
            R.add("act", lambda e: e.activation(out=wT[a2][:, 0:w], in_=tb[:, 0:w], func=AF.Copy),
                  reads=[tb_r], writes=[r_wT[a2]])
            pp.free(tb, tb_r)

        def pv(idx):
            h, k, c0, w, a = state[idx]
            ob, ob_r = obanks[h // 8]
            a2 = idx % 2
            nb_ = w // 128
            for jb in range(nb_):
                blk = (c0 + 128 * jb) // 128
                first = (k == 0 and jb == 0)
                last = (k == nseg - 1 and jb == nb_ - 1)
                R.add("pe", lambda e, jb=jb, blk=blk, first=first, last=last: e.matmul(
                    ob[:, 64 * (h % 8):64 * (h % 8) + 64], lhsT=wT[a2][:, 128 * jb:128 * jb + 128], rhs=v_sb[:, blk, 64 * h:64 * h + 64],
                    start=first, stop=last),
                    reads=[r_wT[a2], r_v[blk // 4]], writes=[ob_r])

        n = len(units)
        for s in range(n + 4):
            if s < n:
                qk(units[s], s)
            if 0 <= s - 3 < n:
                tr(s - 3)
            if 0 <= s - 4 < n:
                pv(s - 4)
        return obanks

    def epilogue(i, obanks):
        for half in range(2):
            ob, ob_r = obanks[half]
            sl = slice(512 * half, 512 * half + 512)
            R.add("dve", lambda e, ob=ob, sl=sl: e.scalar_tensor_tensor(out=og[:, sl], in0=ob[:, :], scalar=-1.0, in1=sg[:, sl],
                                                                        op0=ALU.mult, op1=ALU.mult),
                  reads=[ob_r, r["sg"]], writes=[r["h1T"]])
            pp.free(ob, ob_r)
        for half in range(2):
            tb, tb_r = pp.get()
            for q in range(4):
                fc = 4 * half + q
                R.add("pe", lambda e, tb=tb, q=q, fc=fc: e.matmul(tb[:, 128 * q:128 * q + 128], lhsT=og[:, 128 * fc:128 * fc + 128],
                                                                    rhs=identb, start=True, stop=True),
                      reads=[r["h1T"], r["cstb"]], writes=[tb_r])
            R.add("act", lambda e, tb=tb, half=half: e.activation(out=ogT[:, 512 * half:512 * half + 512], in_=tb[:, :], func=AF.Copy),
                  reads=[tb_r], writes=[r["wn0"], r["wn1"]])
            pp.free(tb, tb_r)

    def epilogue2(i):
        for half in range(2):
            yb, yb_r = pp.get()
            sl = slice(512 * half, 512 * half + 512)
            for fc in range(KC):
                R.add("pe", lambda e, yb=yb, fc=fc, sl=sl: e.matmul(yb[:, :], lhsT=ogT[:, 128 * fc:128 * fc + 128], rhs=wout_sb[:, fc, sl],
                                                                     start=(fc == 0), stop=(fc == KC - 1)),
                      reads=[r["wn0"], r["wn1"], r_wout[fc]], writes=[yb_r])
            R.add("dve", lambda e, yb=yb, sl=sl: e.tensor_tensor(out=x_sb[:, i, sl], in0=yb[:, :], in1=x_sb[:, i, sl], op=ALU.add),
                  reads=[yb_r, r_x[i]], writes=[r_x[i]])
            pp.free(yb, yb_r)
            R.add("act", lambda e, half=half, sl=sl: e.activation(out=wT[half][:, :], in_=x_sb[:, i, sl], func=AF.Square,
                                                                  accum_out=ssx[:, 32 * half + i:32 * half + i + 1]),
                  reads=[r_x[i]], writes=[r_wT[half], r["ssx"]])

    prologue(0)
    prologue_b(0)
    for i in range(NB):
        ob = attention(i)
        epilogue(i, ob)
        if i + 1 < NB:
            prologue(i + 1)
        epilogue2(i)
        if i + 1 < NB:
            prologue_b(i + 1)
    R.add("dve", lambda e: e.tensor_tensor(out=ssx[:, 0:16], in0=ssx[:, 0:16], in1=ssx[:, 32:48], op=ALU.add),
          reads=[r["ssx"]], writes=[r["ssx"]])

    R.add("act", lambda e: e.activation(out=ssx[:, 16:32], in_=ssx[:, 0:16], func=AF.Ln, scale=1.0 / D, bias=EPS),
          reads=[r["ssx"]], writes=[r["ssx"]])
    R.add("act", lambda e: e.activation(out=ssx[:, 48:64], in_=ssx[:, 16:32], func=AF.Exp, scale=-0.5),
          reads=[r["ssx"]], writes=[r["ssx"]])
    fg = sg
    R.add("sp", lambda e: e.dma_start(out=fg, in_=fg_d), writes=[r["sg"]], dma=True)
    for n in range(NB):
        R.add("act", lambda e, n=n: e.activation(out=x_sb[:, n, :], in_=x_sb[:, n, :], func=AF.Identity, scale=ssx[:, 48 + n:49 + n]),
              reads=[r_x[n], r["ssx"]], writes=[r_x[n]])
        eng = "dve" if n % 2 == 0 else "pool"
        R.add(eng, lambda e, n=n: e.tensor_tensor(out=x_sb[:, n, :], in0=x_sb[:, n, :], in1=fg, op=ALU.mult),
              reads=[r_x[n], r["sg"]], writes=[r_x[n]])
        R.add("sp", lambda e, n=n: e.dma_start(out=outv[:, n, :], in_=x_sb[:, n, :]), reads=[r_x[n]], dma=True, final=True)
    R.emit_phase(last=True)
    esC.close()
    esBC.close()
    es.close()
    return R


def host_inputs(inp, b):
    import ml_dtypes
    f = np.float32
    fm = lambda v: np.ascontiguousarray(np.asarray(v, f).reshape(-1, 128).T)
    vec = np.zeros((128, 100), f)
    vec[:, 0:8] = fm(inp["c"][b])
    vec[:, 8:16] = fm(inp["norm_gain"][0])
    vec[:, 16:40] = fm(inp["b_ada"][0])
    vec[:, 40:44] = fm(inp["gla_b_gk"][0])
    vec[:, 44:52] = fm(inp["kv_gain"])
    vec[:, 52:68] = fm(inp["kv_b_ada"])
    vec[:, 68:76] = fm(inp["norm_gain"][1])
    vec[:, 76:100] = fm(inp["b_ada"][1])
    cst = np.zeros((128, 896), f)
    cst[:, 0:128] = np.eye(128, dtype=f)
    mt = (np.arange(128)[:, None] <= np.arange(128)[None, :]).astype(f)
    cst[:, 128:640] = np.tile(mt, (1, 4))
    cst[:, 640:768] = 1.0
    cst[:, 768:896] = np.eye(128, dtype=f)[::-1]
    cb = np.zeros((128, 384), f)
    cb[:, 0:128] = np.eye(128, dtype=f)
    rr = np.arange(128)[:, None]
    cc = np.arange(128)[None, :]
    cb[:, 128:256] = np.where(cc <= 127 - rr, -30000.0, 0.0)
    cb[:, 256:384] = -np.eye(128, dtype=f)
    return {
        "x": np.ascontiguousarray(inp["x"][b], dtype=f),
        "vec": vec,
        "ogain_rep": np.ascontiguousarray(np.broadcast_to(np.asarray(inp["gla_o_gain"][0], f)[None, :], (128, 256))),
        "consts": cst,
        "constsb": cb.astype(ml_dtypes.bfloat16),
        "fg_rep": np.ascontiguousarray(np.broadcast_to(np.asarray(inp["final_gain"], f)[None, :], (128, D))),
        "w_ada0": np.ascontiguousarray(inp["w_ada"][0], dtype=f),
        "w_ada1": np.ascontiguousarray(inp["w_ada"][1], dtype=f),
        "kv_w_ada": np.ascontiguousarray(inp["kv_w_ada"], dtype=f),
        "gla_w_in": np.ascontiguousarray(inp["gla_w_in"][0], dtype=f),
        "w_gk2": np.ascontiguousarray(inp["gla_w_gk2"][0], dtype=f),
        "gla_w_out": np.ascontiguousarray(inp["gla_w_out"][0], dtype=f),
        "w_kv": np.ascontiguousarray(inp["w_kv"], dtype=f),
        "sb_w_in": np.ascontiguousarray(inp["sb_w_in"][0], dtype=f),
        "sb_w_out": np.ascontiguousarray(inp["sb_w_out"][0], dtype=f),
    }


NCORES = 8


def kernel(**inputs):
    inp = {k: np.asarray(v) for k, v in inputs.items()}
    nc = bass.Bass("TRN2", target_bir_lowering=False)
    build_fused(nc)
    in_maps = [host_inputs(inp, b) for b in range(NCORES)]
    res = run_bass_kernel_spmd(nc, in_maps, core_ids=list(range(NCORES)))
    return np.stack([r["out"] for r in res.results]).astype(np.float32)
```
